# Optimizing a Trainium2 kernel written in Bass

```python
import math
import jax, jax.numpy as jnp
from jax import lax
import numpy as np

D_MODEL = 1024
BATCH = 8
SEQ = 2048
DEPTH = 1

MIX_WIDTH = D_MODEL
CONV_WIDTH = D_MODEL // 2
ATTN_WIDTH = MIX_WIDTH - CONV_WIDTH
N_DIFF_HEADS = 4
DIFF_HEAD_DIM = ATTN_WIDTH // (2 * N_DIFF_HEADS)
DIFF_V_DIM = 2 * DIFF_HEAD_DIM
CONV_KERNEL = 31
CONV_PAD = (CONV_KERNEL - 1) // 2
ROPE_THETA = 10000.0
Q_BLOCK = 128
EPS = 1e-6
LN_EPS = 1e-5
IN_COLS = 3 * CONV_WIDTH + 4 * ATTN_WIDTH

kernel_name = "hybrid_conformer_conv_diff_attn_parallel"


def lambda_init_fn(layer_idx):
    return 0.8 - 0.6 * math.exp(-0.3 * layer_idx)


def rmsnorm(x, g, eps=EPS):
    xf = x.astype(jnp.float32)
    y = xf * lax.rsqrt(jnp.mean(xf * xf, axis=-1, keepdims=True) + eps)
    return (y * g.astype(jnp.float32)).astype(x.dtype)


def layernorm(x, g, b, eps=LN_EPS):
    xf = x.astype(jnp.float32)
    mu = jnp.mean(xf, axis=-1, keepdims=True)
    var = jnp.mean(jnp.square(xf - mu), axis=-1, keepdims=True)
    y = (xf - mu) * lax.rsqrt(var + eps)
    return (y * g.astype(jnp.float32) + b.astype(jnp.float32)).astype(x.dtype)


def rope(t, seq_len):
    dh = t.shape[-1]
    half = dh // 2
    pos = jnp.arange(seq_len, dtype=jnp.float32)
    inv_freq = 1.0 / (ROPE_THETA ** (jnp.arange(half, dtype=jnp.float32) * 2.0 / dh))
    ang = pos[:, None] * inv_freq[None, :]
    ang = jnp.concatenate([ang, ang], axis=-1)
    cos = jnp.cos(ang)[None, :, None, :].astype(t.dtype)
    sin = jnp.sin(ang)[None, :, None, :].astype(t.dtype)
    t1, t2 = t[..., :half], t[..., half:]
    rot = jnp.concatenate([-t2, t1], axis=-1)
    return t * cos + rot * sin


def conformer_conv_branch(a_val, a_glu, conv_w, conv_b, ln_g, ln_b, w_pw, b_pw):
    u = a_val * jax.nn.sigmoid(a_glu)
    u = lax.conv_general_dilated(
        u, conv_w[:, None, :], window_strides=(1,),
        padding=[(CONV_PAD, CONV_PAD)],
        dimension_numbers=("NWC", "WIO", "NWC"),
        feature_group_count=CONV_WIDTH) + conv_b
    u = layernorm(u, ln_g, ln_b)
    u = jax.nn.silu(u)
    return jnp.einsum("bsc,ce->bse", u, w_pw) + b_pw


def diff_attention_branch(q, k, v, lq1, lk1, lq2, lk2, head_g, lam_init):
    b, s, _ = q.shape
    q = rope(q.reshape(b, s, 2 * N_DIFF_HEADS, DIFF_HEAD_DIM), s)
    k = rope(k.reshape(b, s, 2 * N_DIFF_HEADS, DIFF_HEAD_DIM), s)
    q = q.reshape(b, s, N_DIFF_HEADS, 2, DIFF_HEAD_DIM).transpose(3, 0, 2, 1, 4)
    k = k.reshape(b, s, N_DIFF_HEADS, 2, DIFF_HEAD_DIM).transpose(3, 0, 2, 1, 4)
    v = v.reshape(b, s, N_DIFF_HEADS, DIFF_V_DIM).transpose(0, 2, 1, 3)
    k1, k2 = k[0], k[1]
    lam = (jnp.exp(jnp.sum(lq1.astype(jnp.float32) * lk1.astype(jnp.float32)))
           - jnp.exp(jnp.sum(lq2.astype(jnp.float32) * lk2.astype(jnp.float32)))
           + lam_init)
    scale = DIFF_HEAD_DIM ** -0.5
    nb = s // Q_BLOCK
    qb = q.reshape(2, b, N_DIFF_HEADS, nb, Q_BLOCK, DIFF_HEAD_DIM).transpose(3, 0, 1, 2, 4, 5)

    def block(qblk):
        s1 = jnp.einsum("bhqd,bhkd->bhqk", qblk[0], k1).astype(jnp.float32) * scale
        s2 = jnp.einsum("bhqd,bhkd->bhqk", qblk[1], k2).astype(jnp.float32) * scale
        w = jax.nn.softmax(s1, axis=-1) - lam * jax.nn.softmax(s2, axis=-1)
        return jnp.einsum("bhqk,bhke->bhqe", w.astype(v.dtype), v)

    o = lax.map(block, qb)
    o = o.transpose(1, 3, 0, 2, 4).reshape(b, s, N_DIFF_HEADS, DIFF_V_DIM)
    o = rmsnorm(o, head_g) * jnp.asarray(1.0 - lam_init, dtype=o.dtype)
    return o.reshape(b, s, ATTN_WIDTH)


def setup_inputs(seed: int = 0) -> dict:
    key = jax.random.key(seed)
    ks = jax.random.split(key, 18)
    f32 = jnp.float32
    nrm = lambda k, shape, sc: jax.random.normal(k, shape, f32) * sc
    return {
        "x": nrm(ks[0], (BATCH, SEQ, D_MODEL), 1.0),
        "norm_g": 1.0 + nrm(ks[1], (DEPTH, D_MODEL), 0.02),
        "w_in": nrm(ks[2], (DEPTH, D_MODEL, IN_COLS), D_MODEL ** -0.5),
        "conv_w": nrm(ks[3], (DEPTH, CONV_KERNEL, CONV_WIDTH), CONV_KERNEL ** -0.5),
        "conv_b": nrm(ks[4], (DEPTH, CONV_WIDTH), 0.02),
        "conv_ln_g": 1.0 + nrm(ks[5], (DEPTH, CONV_WIDTH), 0.02),
        "conv_ln_b": nrm(ks[6], (DEPTH, CONV_WIDTH), 0.02),
        "w_pw": nrm(ks[7], (DEPTH, CONV_WIDTH, CONV_WIDTH), CONV_WIDTH ** -0.5),
        "b_pw": nrm(ks[8], (DEPTH, CONV_WIDTH), 0.02),
        "lambda_q1": nrm(ks[9], (DEPTH, DIFF_HEAD_DIM), 0.1),
        "lambda_k1": nrm(ks[10], (DEPTH, DIFF_HEAD_DIM), 0.1),
        "lambda_q2": nrm(ks[11], (DEPTH, DIFF_HEAD_DIM), 0.1),
        "lambda_k2": nrm(ks[12], (DEPTH, DIFF_HEAD_DIM), 0.1),
        "head_norm_g": 1.0 + nrm(ks[13], (DEPTH, DIFF_V_DIM), 0.02),
        "w_out": nrm(ks[14], (DEPTH, MIX_WIDTH, D_MODEL), MIX_WIDTH ** -0.5),
        "final_norm_g": 1.0 + nrm(ks[15], (D_MODEL,), 0.02),
    }


def reference(x, norm_g, w_in, conv_w, conv_b, conv_ln_g, conv_ln_b, w_pw, b_pw,
              lambda_q1, lambda_k1, lambda_q2, lambda_k2, head_norm_g, w_out,
              final_norm_g):
    C, A = CONV_WIDTH, ATTN_WIDTH
    for l in range(DEPTH):
        h = rmsnorm(x, norm_g[l])
        p = jnp.einsum("bsd,de->bse", h, w_in[l])
        a_val = p[..., 0:C]
        a_glu = p[..., C:2 * C]
        a_gate = p[..., 2 * C:3 * C]
        o = 3 * C
        q = p[..., o:o + A]
        k = p[..., o + A:o + 2 * A]
        v = p[..., o + 2 * A:o + 3 * A]
        b_gate = p[..., o + 3 * A:o + 4 * A]
        y_a = conformer_conv_branch(a_val, a_glu, conv_w[l], conv_b[l], conv_ln_g[l],
                                    conv_ln_b[l], w_pw[l], b_pw[l]) * jax.nn.silu(a_gate)
        y_b = diff_attention_branch(q, k, v, lambda_q1[l], lambda_k1[l], lambda_q2[l],
                                    lambda_k2[l], head_norm_g[l], lambda_init_fn(l)) * jax.nn.silu(b_gate)
        y = jnp.concatenate([y_a, y_b], axis=-1)
        x = x + jnp.einsum("bse,ed->bsd", y, w_out[l])
    return rmsnorm(x, final_norm_g)
```

```python
import numpy as np
import ml_dtypes
import concourse.bass as bass
import concourse.mybir as mybir
from concourse.bass_utils import run_bass_kernel_spmd

F32 = mybir.dt.float32
BF16 = mybir.dt.bfloat16
ALU = mybir.AluOpType
AF = mybir.ActivationFunctionType
AX = mybir.AxisListType

S = 2048
D = 1024
C = 512
NT = S // 128
NCORES = 8
KW = 31
PADL = 16
UW = PADL + S + 16
EPS = 1e-6
LN_EPS = 1e-5
LAM_INIT = 0.2
VW = 130


class Eng:
    def __init__(self, nc, eng, name):
        self.nc, self.e, self.name = nc, eng, name
        self.sem = nc.alloc_semaphore("sem_" + name)
        self.n = 0
        self.seen = {}

    def wait(self, *toks):
        for t in toks:
            if t is None:
                continue
            if isinstance(t, list):
                self.wait(*t)
                continue
            src, c = t
            if self.seen.get(src, 0) >= c:
                continue
            self.e.wait_ge(src.sem, c)
            self.seen[src] = c

    def done(self, ins):
        self.n += 1
        ins.then_inc(self.sem, 1)
        return (self, self.n)


class DmaSem:
    def __init__(self, nc, name):
        self.sem = nc.alloc_semaphore("dsem_" + name)
        self.n = 0
        DmaSem.ALL.append(self)

    def done(self, ins):
        self.n += 16
        ins.then_inc(self.sem, 16)
        return (self, self.n)


def build(debug=False, stop_after=None):
    nc = bass.Bass("TRN2", target_bir_lowering=False)
    DmaSem.ALL = []

    def din(name, shape, dt=F32):
        return nc.dram_tensor(name, shape, dt, kind="ExternalInput").ap()

    x = din("x", [S, D])
    norm_g = din("norm_g", [8, 128])
    w_in = din("w_in", [D, 3584])
    conv_w = din("conv_w", [KW * 4, 128])
    conv_b = din("conv_b", [4, 128])
    ln_g = din("ln_g", [4, 128])
    ln_b = din("ln_b", [4, 128])
    w_pw = din("w_pw", [C, C])
    b_pw = din("b_pw", [4, 128])
    lq1 = din("lq1", [1, 64])
    lk1 = din("lk1", [1, 64])
    lq2 = din("lq2", [1, 64])
    lk2 = din("lk2", [1, 64])
    head_g = din("head_g", [1, 128])
    w_out = din("w_out", [D, D])
    fin_g = din("fin_g", [1, D])
    c_identb = din("c_identb", [128, 128], BF16)
    c_identf = din("c_identf", [128, 128])
    c_rot = din("c_rot", [128, 128], BF16)
    c_ones = din("c_ones", [128, 128], BF16)
    c_cos = din("c_cos", [128, S])
    c_sin = din("c_sin", [128, S])
    out = nc.dram_tensor("out", [S, D], F32, kind="ExternalOutput").ap()
    dbg = {}

    off = [16512]

    def sb(name, shape, dt, at=None):
        nbytes = int(np.prod(shape[1:])) * (2 if dt == BF16 else 4)
        if at is None:
            at = off[0]
            off[0] += (nbytes + 63) // 64 * 64
        return nc.alloc_sbuf_tensor_at(name, shape, dt, offset=at), at

    identb, _ = sb("identb", [128, 128], BF16)
    identf, _ = sb("identf", [128, 128], F32)
    rotm, _ = sb("rotm", [128, 128], BF16)
    onesm, _ = sb("onesm", [128, 128], BF16)
    rows_c, _ = sb("rows_c", [8, 128], F32)
    colv, _ = sb("colv", [128, 160], F32)
    lam4, _ = sb("lam4", [128, 4, 64], F32)
    small, _ = sb("small", [128, 256], F32)
    hgB, _ = sb("hgB", [128, 4, 128], F32)
    cT0, _ = sb("cT0", [128, S], F32)
    cwh, _ = sb("cwh", [128, 4, 32], F32)
    wpw, _ = sb("wpw", [128, 4, C], BF16)
    hT, hT_at = sb("hT", [128, 8, S], BF16)
    xs, xs_at = sb("xs", [128, 4, D], F32)
    uT, uT_at = sb("uT", [128, 4, UW], BF16)
    sga, _ = sb("sga", [128, 4, S], BF16)
    Gt, G_at = sb("Gt", [128, NT, 512], BF16)
    QT, QT_at = sb("QT", [128, 4, S], BF16)
    KT, KT_at = sb("KT", [128, 4, S], BF16)
    Vg, V_at = sb("Vg", [128, NT, 4, VW], BF16)
    wsl, W_at = sb("wsl", [128, 3, 8, 512], BF16)
    rt1, _ = sb("rt1", [128, 2, 512], F32)
    rt2, _ = sb("rt2", [128, 2, 512], F32)
    junk, T_at = sb("junk", [128, D], BF16)
    xnb, _ = sb("xnb", [128, 2, D], BF16)
    thb, _ = sb("thb", [128, 2, 512], F32)
    qbb, _ = sb("qbb", [128, 2, 512], BF16)
    assert off[0] <= 229376, off[0]
    yT = hT
    yb, _ = sb("yb", [128, NT, 512], BF16, at=hT_at)
    cosT, _ = sb("cosT", [128, S], F32, at=xs_at)
    sinT, _ = sb("sinT", [128, S], F32, at=xs_at + 8192)
    Eb, _ = sb("Eb", [128, 3, 2, 512], BF16, at=xs_at)
    at_t, _ = sb("at_t", [128, 2, 4, 128], F32, at=xs_at + 6144)
    at_o, _ = sb("at_o", [128, 2, 4, 128], F32, at=xs_at + 6144 + 4096)
    cTl = [cT0, sb("cT1", [128, S], F32, at=T_at)[0], sb("cT2", [128, S], F32, at=W_at)[0],
           sb("cT3", [128, S], F32, at=W_at + 8192)[0]]
    rows_a, _ = sb("rows_a", [128, 128], F32, at=QT_at)
    rows_b, _ = sb("rows_b", [32, 128], F32, at=QT_at + 512)
    hg1, _ = sb("hg1", [128, 128], F32, at=QT_at + 1024)
    fgB, _ = sb("fgB", [128, D], F32, at=uT_at)
    cb, _ = sb("cb", [128, 4, S], BF16, at=W_at + 16384)
    csql = [sb("csq%d" % i, [128, S], BF16, at=uT_at + i * UW * 2)[0] for i in range(4)]
    sT, _ = sb("sT", [128, 4, S], BF16, at=V_at)
    wout, _ = sb("wout", [128, 8, D], BF16, at=G_at)
    ln_r, _ = sb("ln_r", [128, 4, 512], F32, at=QT_at)
    ln_n, _ = sb("ln_n", [128, 4, 512], F32, at=QT_at + 8192)
    ln_m, _ = sb("ln_m", [128, 2, 512], F32, at=KT_at)
    ln_v, _ = sb("ln_v", [128, 2, 512], F32, at=KT_at + 4096)
    ln_x, _ = sb("ln_x", [128, 4, 512], F32, at=KT_at + 8192)
    ln_x2, _ = sb("ln_x2", [128, 4, 512], F32, at=uT_at + 4096)
    rbuf, _ = sb("rbuf", [128, 3, D], F32, at=W_at + 16384)
    junk5, _ = sb("junk5", [128, D], BF16, at=W_at + 28672)

    PA = nc.alloc_psum_tensor("PA", [128, 4, 512], F32)
    PB = nc.alloc_psum_tensor("PB", [128, 4, 512], F32)
    PBb = PB[:].bitcast(BF16)

    pe = Eng(nc, nc.tensor, "pe")
    act = Eng(nc, nc.scalar, "act")
    dve = Eng(nc, nc.vector, "dve")
    pool = Eng(nc, nc.gpsimd, "pool")
    sp = Eng(nc, nc.sync, "sp")
    ds_const = DmaSem(nc, "const")
    ds_x = [DmaSem(nc, "x%d" % i) for i in range(4)]
    ds_w = [DmaSem(nc, "w%d" % i) for i in range(3)]
    ds_cs = DmaSem(nc, "cs")
    ds_w2 = DmaSem(nc, "wx")
    ds_w3 = DmaSem(nc, "wy")
    ds_o = [DmaSem(nc, "out%d" % i) for i in range(3)]

    def finish():
        for e_ in (pe, act, dve, pool):
            if e_.n:
                nc.sync.wait_ge(e_.sem, e_.n)
        for d_ in DmaSem.ALL:
            if d_.n:
                nc.sync.wait_ge(d_.sem, d_.n)
        return nc

    SS1, RS1 = 0, 16
    LAMC = 40
    ZR = 48
    NZ = 64
    SSA = 72
    LNA = 80
    RSA = 88
    SS5, LN5, RS5 = 96, 112, 128

    ds_c0 = DmaSem(nc, "c0")
    for dst, src in [(identb[:], c_identb), (identf[:], c_identf), (rows_c[0:8, :], norm_g)]:
        tok_c0 = ds_c0.done(nc.scalar.dma_start(out=dst, in_=src))
    xv = x.rearrange("(t p) d -> t p d", p=128)
    x_tok = [None] * NT
    for t in range(4):
        x_tok[t] = ds_x[t].done(nc.sync.dma_start(out=xs[:, t, :], in_=xv[t]))
    tok_c = None
    for dst, src in [(rotm[:], c_rot), (onesm[:], c_ones),
                     (rows_a[0:124, :], conv_w), (rows_b[0:4, :], conv_b), (rows_b[4:8, :], ln_g),
                     (rows_b[8:12, :], ln_b), (rows_b[12:16, :], b_pw), (rows_b[16:17, :], head_g),
                     (lam4[:, 0, :], lq1.partition_broadcast(128)), (lam4[:, 1, :], lk1.partition_broadcast(128)),
                     (lam4[:, 2, :], lq2.partition_broadcast(128)), (lam4[:, 3, :], lk2.partition_broadcast(128)),
                     (hg1[:], head_g.partition_broadcast(128))]:
        tok_c = ds_const.done(nc.sync.dma_start(out=dst, in_=src))

    w_v = w_in.rearrange("(dt p) c -> p dt c", p=128)
    SLABS = [
        [(512, 256), (0, 256)],
        [(768, 256), (256, 256)],
        [(1024, 512)],
        [(1536, 512)],
        [(2048, 512)],
        [(2560, 512)],
        [(3072, 512)],
    ]
    slab_tok = [None] * 7

    def load_slab(si):
        slot = si % 3
        c0 = 0
        tk = None
        for (cs, wd) in SLABS[si]:
            tk = ds_w[slot].done(nc.gpsimd.dma_start(out=wsl[:, slot, :, c0:c0 + wd], in_=w_v[:, :, cs:cs + wd]))
            c0 += wd
        slab_tok[si] = tk

    load_slab(0)
    wpw_tok = ds_w2.done(nc.gpsimd.dma_start(out=wpw[:], in_=w_pw.rearrange("(ct p) e -> p ct e", p=128)))

    pool.done(nc.gpsimd.memset(Vg[:], 1.0))
    tok_vg1 = pool.done(nc.gpsimd.memset(uT[:, :, 0:PADL], 0.0))
    tok_upad = pool.done(nc.gpsimd.memset(uT[:, :, PADL + S:UW], 0.0))

    CB, LG, LB, BP, NG = 124, 128, 132, 136, 140
    pe.wait(tok_c0)
    t_pe = pe.done(nc.tensor.matmul(PA[:, 1, 0:8], lhsT=rows_c[0:8, :], rhs=identf[0:8, 0:8], start=True, stop=True))
    dve.wait(t_pe)
    tok_ng = dve.done(nc.vector.tensor_copy(out=colv[:, NG:NG + 8], in_=PA[:, 1, 0:8]))

    p1 = {"xnb_rd": [None, None], "tp_rd": [None, None], "last": None}

    def emit_p1(t):
        slot = t % 4
        b = t % 2
        act.wait(x_tok[t])
        t_ss = act.done(nc.scalar.activation(out=junk[:], in_=xs[:, slot, :], func=AF.Square,
                                             accum_out=small[:, SS1 + t:SS1 + t + 1]))
        act.wait(t_ss)
        t_ln = act.done(nc.scalar.activation(out=small[:, RS1 + t:RS1 + t + 1], in_=small[:, SS1 + t:SS1 + t + 1],
                                             func=AF.Ln, scale=1.0 / D, bias=EPS))
        act.wait(t_ln)
        t_rs = act.done(nc.scalar.activation(out=small[:, RS1 + t:RS1 + t + 1], in_=small[:, RS1 + t:RS1 + t + 1],
                                             func=AF.Exp, scale=-0.5))
        dve.wait(t_rs, x_tok[t], p1["xnb_rd"][b])
        t_xn = dve.done(nc.vector.tensor_scalar(out=xnb[:, b, :], in0=xs[:, slot, :],
                                                scalar1=small[:, RS1 + t:RS1 + t + 1], scalar2=None, op0=ALU.mult))
        if t + 4 < NT:
            sp.wait(t_xn, t_ss)
            x_tok[t + 4] = ds_x[slot].done(nc.sync.dma_start(out=xs[:, slot, :], in_=xv[t + 4]))
        pe.wait(t_xn, p1["tp_rd"][b])
        for dt in range(8):
            ins = nc.tensor.transpose(PBb[:, 2 + b, dt * 128:(dt + 1) * 128], xnb[:, b, dt * 128:(dt + 1) * 128], identb[:])
        t_tp = pe.done(ins)
        p1["xnb_rd"][b] = t_tp
        return t_tp

    def emit_p1_evac(t, t_tp):
        b = t % 2
        dve.wait(t_tp, tok_ng)
        t_ev = dve.done(nc.vector.tensor_tensor(
            out=hT[:, :, t * 128:(t + 1) * 128],
            in0=PBb[:, 2 + b, :].rearrange("p (dt i) -> p dt i", dt=8),
            in1=colv[:, NG:NG + 8].unsqueeze(2).to_broadcast([128, 8, 128]), op=ALU.mult))
        p1["tp_rd"][b] = t_ev
        p1["last"] = t_ev
        return t_ev

    hT_tok = [None] * NT
    tp_tok = [None] * NT
    for t in range(NT + 1):
        if t < NT:
            tp_tok[t] = emit_p1(t)
        if t >= 1:
            hT_tok[t - 1] = emit_p1_evac(t - 1, tp_tok[t - 1])
        if t == 9:
            pool.wait(x_tok[12])
            load_slab(1)
        if t == 13:
            pool.wait(x_tok[15])
            load_slab(2)
    sp.wait(p1["last"])
    ds_cs.done(nc.sync.dma_start(out=cosT[:], in_=c_cos))
    cs_tok = ds_cs.done(nc.sync.dma_start(out=sinT[:], in_=c_sin))

    pe.wait(tok_c)
    nc.tensor.matmul(PA[:, 0, 0:124], lhsT=rows_a[0:124, :], rhs=identf[0:124, 0:124], start=True, stop=True)
    t_pe = pe.done(nc.tensor.matmul(PA[:, 1, 0:17], lhsT=rows_b[0:17, :], rhs=identf[0:17, 0:17], start=True, stop=True))
    dve.wait(t_pe)
    nc.vector.tensor_copy(out=colv[:, 0:124], in_=PA[:, 0, 0:124])
    nc.vector.tensor_copy(out=colv[:, 124:140], in_=PA[:, 1, 0:16])
    tok_colv = dve.done(nc.vector.tensor_scalar(out=colv[:, 149:150], in0=PA[:, 1, 16:17], scalar1=1.0 - LAM_INIT,
                                                scalar2=None, op0=ALU.mult))
    dve.wait(tok_colv)
    tok_cwh = dve.done(nc.vector.tensor_scalar(
        out=cwh[:, :, 0:KW], in0=colv[:, 0:124].rearrange("p (j ct) -> p ct j", ct=4),
        scalar1=0.5, scalar2=None, op0=ALU.mult))
    dve.wait(tok_c)
    tok_hgB = dve.done(nc.vector.tensor_scalar(
        out=hgB[:], in0=hg1[:].unsqueeze(1).to_broadcast([128, 4, 128]),
        scalar1=1.0 - LAM_INIT, scalar2=None, op0=ALU.mult))
    nc.vector.tensor_tensor(out=lam4[:, 0, :], in0=lam4[:, 0, :], in1=lam4[:, 1, :], op=ALU.mult)
    t1 = dve.done(nc.vector.tensor_tensor(out=lam4[:, 2, :], in0=lam4[:, 2, :], in1=lam4[:, 3, :], op=ALU.mult))
    dve.wait(t1)
    t2 = dve.done(nc.vector.tensor_reduce(out=small[:, LAMC:LAMC + 2],
                                          in_=lam4[:].rearrange("p (a b) d -> p a b d", b=2)[:, :, 0, :],
                                          axis=AX.X, op=ALU.add))
    act.wait(t2)
    t3 = act.done(nc.scalar.activation(out=small[:, LAMC + 2:LAMC + 4], in_=small[:, LAMC:LAMC + 2], func=AF.Exp))
    dve.wait(t3)
    t4 = dve.done(nc.vector.tensor_tensor(out=small[:, LAMC + 4:LAMC + 5], in0=small[:, LAMC + 3:LAMC + 4],
                                          in1=small[:, LAMC + 2:LAMC + 3], op=ALU.subtract))
    dve.wait(t4)
    tok_lam = dve.done(nc.vector.tensor_scalar(out=small[:, LAMC + 5:LAMC + 6], in0=small[:, LAMC + 4:LAMC + 5],
                                               scalar1=-LAM_INIT, scalar2=None, op0=ALU.add))
    neglam = small[:, LAMC + 5:LAMC + 6]
    tok_pro_pe = t_pe
    if stop_after == "prologue":
        return finish()

    st = {"g": 0, "acc_rd": [tok_colv, tok_colv, None, None], "rot_rd": [None, None], "t_rd": [None, None], "qb_rd": [None, None],
          "th_rd": [None, None], "pending_rot": None, "nqk": 0, "nglu": 0, "nbg": 0}
    slab_done = [None] * 7
    tok_uT = [None] * 4
    tok_QK = {}
    tok_V = [None] * NT
    tok_G = [None] * NT
    tok_sga = None

    conv_prev = {}
    conv_gate = {"w": None}

    def conv_step(ct, j):
        src = uT[:, ct, PADL - 15 + j:PADL - 15 + j + S]
        if j == 0:
            gate = {0: [tok_pro_pe],
                    1: [p1["xnb_rd"][0], p1["xnb_rd"][1], p1["last"], st["th_rd"][0], st["th_rd"][1]],
                    2: conv_gate["w"], 3: conv_gate["w"]}[ct]
            dve.wait(tok_uT[ct], tok_upad, tok_cwh, tok_colv, gate)
            ins = nc.vector.tensor_scalar(out=cTl[ct][:], in0=src, scalar1=cwh[:, ct, 0:1],
                                          scalar2=colv[:, CB + ct:CB + ct + 1], op0=ALU.mult, op1=ALU.add)
        else:
            dve.wait(conv_prev[ct])
            ins = nc.vector.scalar_tensor_tensor(out=cTl[ct][:], in0=src, scalar=cwh[:, ct, j:j + 1],
                                                 in1=cTl[ct][:], op0=ALU.mult, op1=ALU.add)
        conv_prev[ct] = dve.done(ins)

    conv_q = {"early": [(0, j) for j in range(KW)] + [(1, j) for j in range(KW)],
              "late": [(2, j) for j in range(KW)] + [(3, j) for j in range(KW)]}
    conv_i = {"early": 0, "late": 0}
    tok_cb = [None] * 4
    tok_cbp = {}

    def emit_conv(which, n):
        for _ in range(n):
            i = conv_i[which]
            if i >= len(conv_q[which]):
                return
            conv_i[which] = i + 1
            ct, j = conv_q[which][i]
            conv_step(ct, j)
            if j == KW - 1:
                for c_ in range(4):
                    cb_queue.append((ct, c_))

    cb_queue = []

    def emit_cb(n):
        for _ in range(n):
            if not cb_queue:
                return
            ct, c_ = cb_queue.pop(0)
            sl = slice(c_ * 512, (c_ + 1) * 512)
            pool.wait(conv_prev[ct], tok_p2_pe_box[0])
            nc.gpsimd.tensor_copy(out=cb[:, ct, sl], in_=cTl[ct][:, sl])
            tok_cb[ct] = tok_cbp[(ct, c_)] = pool.done(nc.gpsimd.tensor_tensor(out=csql[ct][:, sl], in0=cTl[ct][:, sl],
                                                           in1=cTl[ct][:, sl], op=ALU.mult))

    tok_p2_pe_box = [None]

    def fm_group(slot, ct, tc):
        a = st["g"] % 4
        st["g"] += 1
        pe.wait(st["acc_rd"][a], [hT_tok[tc * 4 + i] for i in range(4)])
        for dt in range(8):
            ins = nc.tensor.matmul(PA[:, a, :], lhsT=wsl[:, slot, dt, ct * 128:(ct + 1) * 128],
                                   rhs=hT[:, dt, tc * 512:(tc + 1) * 512], start=(dt == 0), stop=(dt == 7))
        return a, pe.done(ins)

    def tm_group(slot, tt):
        a = st["g"] % 4
        st["g"] += 1
        pe.wait(st["acc_rd"][a], hT_tok[tt])
        for dt in range(8):
            ins = nc.tensor.matmul(PA[:, a, :], lhsT=hT[:, dt, tt * 128:(tt + 1) * 128],
                                   rhs=wsl[:, slot, dt, :], start=(dt == 0), stop=(dt == 7))
        return a, pe.done(ins)

    def flush_rot():
        pr = st["pending_rot"]
        if pr is None:
            return
        st["pending_rot"] = None
        (dst, ct, tc, b, t_qb, t_t1) = pr
        r = b
        pe.wait(t_qb, st["rot_rd"][r])
        t_rot = pe.done(nc.tensor.matmul(PB[:, r, :], lhsT=rotm[:], rhs=qbb[:, b, :], start=True, stop=True))
        st["qb_rd"][b] = t_rot
        dve.wait(t_rot)
        t_t2 = dve.done(nc.vector.tensor_tensor(out=rt2[:, b, :], in0=PB[:, r, :],
                                                in1=sinT[:, tc * 512:(tc + 1) * 512], op=ALU.mult))
        st["rot_rd"][r] = t_t2
        pool.wait(t_t2, t_t1, tok_hgB, tok_pro_pe)
        if dst is QT:
            o_ap = QT[:, ct, :].rearrange("p (i n) -> p n i", n=NT)[:, 4 * tc:4 * tc + 4, :]
            i0 = rt1[:, b, :].rearrange("p (n i) -> p n i", n=4)
            i1 = rt2[:, b, :].rearrange("p (n i) -> p n i", n=4)
        else:
            o_ap = dst[:, ct, tc * 512:(tc + 1) * 512]
            i0, i1 = rt1[:, b, :], rt2[:, b, :]
        t_add = pool.done(nc.gpsimd.tensor_tensor(out=o_ap, in0=i0, in1=i1, op=ALU.add))
        st["t_rd"][b] = t_add
        tok_QK[(id(dst), ct, tc)] = t_add

    EARLY_TAPS = {1: 12, 2: 11, 3: 2, 4: 5, 5: 11, 6: 11}
    for si in range(7):
        slot = si % 3
        pe.wait(slab_tok[si])
        last_pe = None
        ngrp = [0]

        def after_group():
            ngrp[0] += 1
            n_t = EARLY_TAPS.get(si, 0)
            if n_t and (ngrp[0] * n_t) // 16 > ((ngrp[0] - 1) * n_t) // 16:
                emit_conv("early", 1)

        if si < 2:
            for tc in range(4):
                for cl in range(2):
                    ct = si * 2 + cl
                    a, t_mm = fm_group(slot, cl, tc)
                    b = st["nglu"] % 2
                    st["nglu"] += 1
                    act.wait(t_mm, st["th_rd"][b])
                    t_th = act.done(nc.scalar.activation(out=thb[:, b, :], in_=PA[:, a, :], func=AF.Tanh, scale=0.5))
                    st["acc_rd"][a] = t_th
                    a2, t_mm2 = fm_group(slot, 2 + cl, tc)
                    dve.wait(t_mm2, t_th)
                    t_u = dve.done(nc.vector.scalar_tensor_tensor(
                        out=uT[:, ct, PADL + tc * 512:PADL + (tc + 1) * 512], in0=thb[:, b, :], scalar=1.0,
                        in1=PA[:, a2, :], op0=ALU.add, op1=ALU.mult))
                    st["acc_rd"][a2] = t_u
                    st["th_rd"][b] = t_u
                    tok_uT[ct] = t_u
                    last_pe = t_mm2
                if si == 1:
                    for _ in range(4):
                        after_group()
        elif si == 2:
            for tc in range(4):
                for ct in range(4):
                    a, t_mm = fm_group(slot, ct, tc)
                    act.wait(t_mm)
                    t_s = act.done(nc.scalar.activation(out=sga[:, ct, tc * 512:(tc + 1) * 512], in_=PA[:, a, :],
                                                        func=AF.Silu))
                    st["acc_rd"][a] = t_s
                    tok_sga = t_s
                    last_pe = t_mm
                    after_group()
        elif si in (3, 4):
            dst = QT if si == 3 else KT
            for tc in range(4):
                for ct in range(4):
                    a, t_mm = fm_group(slot, ct, tc)
                    flush_rot()
                    b = st["nqk"] % 2
                    st["nqk"] += 1
                    act.wait(t_mm, st["qb_rd"][b])
                    t_qb = act.done(nc.scalar.activation(out=qbb[:, b, :], in_=PA[:, a, :], func=AF.Copy))
                    dve.wait(t_mm, t_qb, cs_tok, st["t_rd"][b])
                    t_t1 = dve.done(nc.vector.tensor_tensor(out=rt1[:, b, :], in0=PA[:, a, :],
                                                            in1=cosT[:, tc * 512:(tc + 1) * 512], op=ALU.mult))
                    st["acc_rd"][a] = [t_qb, t_t1]
                    st["pending_rot"] = (dst, ct, tc, b, t_qb, t_t1)
                    last_pe = t_mm
                    after_group()
        elif si == 5:
            flush_rot()
            for tt in range(NT):
                a, t_mm = tm_group(slot, tt)
                act.wait(t_mm, tok_vg1)
                t_v = act.done(nc.scalar.activation(out=Vg[:, tt, :, 0:128],
                                                    in_=PA[:, a, :].rearrange("p (h e) -> p h e", h=4), func=AF.Copy))
                st["acc_rd"][a] = t_v
                tok_V[tt] = t_v
                last_pe = t_mm
                after_group()
        else:
            for tt in range(NT):
                a, t_mm = tm_group(slot, tt)
                b = st["nbg"] % 4
                st["nbg"] += 1
                if "g_rd" not in st:
                    st["g_rd"] = [st["t_rd"][0], st["t_rd"][1], st["t_rd"][0], st["t_rd"][1]]
                gbuf = (rt1 if b < 2 else rt2)[:, b % 2, :]
                act.wait(t_mm, st["g_rd"][b])
                t_s = act.done(nc.scalar.activation(out=gbuf, in_=PA[:, a, :], func=AF.Silu))
                st["acc_rd"][a] = t_s
                dve.wait(t_s, tok_hgB)
                t_g = dve.done(nc.vector.tensor_tensor(out=Gt[:, tt, :], in0=gbuf,
                                                       in1=hgB[:].rearrange("p h e -> p (h e)"), op=ALU.mult))
                st["g_rd"][b] = t_g
                tok_G[tt] = t_g
                last_pe = t_mm
                after_group()
        slab_done[si] = last_pe
        if si + 3 < 7:
            pool.wait(last_pe)
            load_slab(si + 3)
        if stop_after == "p2s%d" % si:
            return finish()
    tok_p2_pe = slab_done[6]
    tok_p2_pe_box[0] = tok_p2_pe
    conv_gate["w"] = [tok_p2_pe]
    if stop_after == "p2":
        return finish()

    s_rd = [None, None]
    e_rd = [None, None, None]
    o_rd = None
    tmp_rd = [None, None]
    pend_pv = None
    rounds = [(h, qc) for h in range(4) for qc in range(4)]

    def emit_pv(h, r, p, eb, t_e):
        st_ = r % 2
        fw = [tok_V[2 * p], tok_V[2 * p + 1]]
        if p == 0:
            fw.append(o_rd[st_])
        pe.wait(t_e, fw)
        ins = None
        for kp in range(2):
            kt = 2 * p + kp
            for j in range(2):
                for q2 in range(2):
                    ins = nc.tensor.matmul(PB[:, 2 * st_ + q2, j * 256:j * 256 + 129],
                                           lhsT=Eb[:, eb, j, kp * 256 + q2 * 128:kp * 256 + (q2 + 1) * 128],
                                           rhs=Vg[:, kt, h, 0:129], start=(kt == 0 and j == 0), stop=(kt == NT - 1),
                                           skip_group_check=True)
        return pe.done(ins)

    def act_epilogue(rb, t_ss):
        act.wait(t_ss)
        t_l = act.done(nc.scalar.activation(out=small[:, LNA + 4 * rb:LNA + 4 * rb + 4],
                                            in_=small[:, SSA + 4 * rb:SSA + 4 * rb + 4], func=AF.Ln,
                                            scale=1.0 / 128, bias=EPS))
        act.wait(t_l)
        return act.done(nc.scalar.activation(out=small[:, RSA + 4 * rb:RSA + 4 * rb + 4],
                                             in_=small[:, LNA + 4 * rb:LNA + 4 * rb + 4], func=AF.Exp, scale=-0.5))

    def pool_tail(prb, ph, pqc, t_r):
        pool.wait(t_r)
        t_y = pool.done(nc.gpsimd.tensor_tensor(
            out=at_o[:, prb, :, :], in0=at_o[:, prb, :, :],
            in1=small[:, RSA + 4 * prb:RSA + 4 * prb + 4].unsqueeze(2).to_broadcast([128, 4, 128]), op=ALU.mult))
        pool.wait(t_y, [tok_G[pqc * 4 + i] for i in range(4)])
        return pool.done(nc.gpsimd.tensor_tensor(
            out=yb[:, pqc * 4:(pqc + 1) * 4, ph * 128:(ph + 1) * 128], in0=at_o[:, prb, :, :],
            in1=Gt[:, pqc * 4:(pqc + 1) * 4, ph * 128:(ph + 1) * 128], op=ALU.mult))

    ds_tr = DmaSem(nc, "tr")

    def ybT_dma(ph, pqc, t_yb):
        sp.wait(t_yb, tok_p2_pe)
        for i in range(4):
            tt = pqc * 4 + i
            ds_tr.done(nc.sync.dma_start_transpose(out=yT[:, 4 + ph, tt * 128:(tt + 1) * 128],
                                                   in_=yb[:, tt, ph * 128:(ph + 1) * 128]))

    pend_epi = None
    pe.wait(tok_p2_pe, st["acc_rd"], st["rot_rd"], p1["tp_rd"])
    act.wait(st["t_rd"][0], st["t_rd"][1])
    dve.wait(st["t_rd"][0], st["t_rd"][1])
    o_rd = [None, None]
    NP = NT // 2
    iters = [(r, r // 8, r % 8, p) for r in range(32) for p in range(NP)]
    s_tok = {}

    def emit_scores(g):
        (r_, h_, q8_, p_) = iters[g]
        sb_i = g % 2
        pe.wait(s_rd[sb_i], [tok_QK[(id(QT), h_, tcq)] for tcq in range(4)], tok_QK[(id(KT), h_, p_ // 2)])
        ins = None
        for kp in range(2):
            kt_ = 2 * p_ + kp
            for j in range(2):
                ins = nc.tensor.matmul(PA[:, 2 * sb_i + j, kp * 256:(kp + 1) * 256],
                                       lhsT=KT[64 * j:64 * j + 64, h_, kt_ * 128:(kt_ + 1) * 128],
                                       rhs=QT[64 * j:64 * j + 64, h_, q8_ * 256:(q8_ + 1) * 256], start=True, stop=True,
                                       skip_group_check=True)
        s_tok[g] = pe.done(ins)

    def round_drain(r_, t_pv_last_):
        nonlocal pend_epi
        st_ = r_ % 2
        sr = r_ // 2
        rb = sr % 2
        hh = r_ % 2
        Ov = PB[:, 2 * st_:2 * st_ + 2, :].rearrange("p b (j c) -> p b j c", j=2)
        dve.wait(t_pv_last_, tmp_rd[rb] if hh == 0 else None, tok_lam)
        zr = small[:, ZR + 8 * rb + 4 * hh:ZR + 8 * rb + 4 * hh + 4].rearrange("p (b j) -> p b j", j=2)
        t_z = dve.done(nc.vector.reciprocal(out=zr, in_=Ov[:, :, :, 128]))
        dve.wait(t_z)
        nzc = small[:, NZ + 4 * rb + 2 * hh:NZ + 4 * rb + 2 * hh + 2]
        t_nz = dve.done(nc.vector.tensor_tensor(out=nzc, in0=zr[:, :, 1], in1=neglam.to_broadcast([128, 2]),
                                                op=ALU.mult))
        t_o1 = dve.done(nc.vector.tensor_tensor(
            out=at_o[:, rb, 2 * hh:2 * hh + 2, :], in0=Ov[:, :, 0, 0:128],
            in1=zr[:, :, 0].unsqueeze(2).to_broadcast([128, 2, 128]), op=ALU.mult))
        dve.wait(t_nz, t_o1)
        for q2 in range(2):
            t_o = dve.done(nc.vector.scalar_tensor_tensor(
                out=at_o[:, rb, 2 * hh + q2, :], in0=Ov[:, q2, 1, 0:128], scalar=nzc[:, q2:q2 + 1],
                in1=at_o[:, rb, 2 * hh + q2, :], op0=ALU.mult, op1=ALU.add))
        o_rd[st_] = t_o
        if hh == 1:
            pool.wait(t_o, tmp_rd[rb])
            t_sq = pool.done(nc.gpsimd.tensor_tensor(out=at_t[:, rb, :, :], in0=at_o[:, rb, :, :],
                                                     in1=at_o[:, rb, :, :], op=ALU.mult))
            pend_epi = (rb, r_ // 8, (r_ % 8) // 2, t_sq)

    t_pv_last = None
    emit_scores(0)
    for g, (r, h, q8, p) in enumerate(iters):
        eb = g % 3
        if g + 1 < len(iters):
            emit_scores(g + 1)
        if pend_pv is not None:
            (pr_, ph_, pp_, peb, pt_e) = pend_pv
            e_rd[peb] = emit_pv(ph_, pr_, pp_, peb, pt_e)
            if pp_ == NP - 1:
                round_drain(pr_, e_rd[peb])
        act.wait(s_tok[g], e_rd[eb])
        t_e = act.done(nc.scalar.activation(out=Eb[:, eb, :, :], in_=PA[:, 2 * (g % 2):2 * (g % 2) + 2, :],
                                            func=AF.Exp, scale=0.125))
        s_rd[g % 2] = t_e
        pend_pv = (r, h, p, eb, t_e)
        kk = (r % 2) * NP + p
        if kk == 7 and pend_epi is not None and len(pend_epi) == 4:
            (prb, ph, pqc, t_sq) = pend_epi
            dve.wait(t_sq)
            t_ss = dve.done(nc.vector.tensor_reduce(out=small[:, SSA + 4 * prb:SSA + 4 * prb + 4],
                                                    in_=at_t[:, prb, :, :], axis=AX.X, op=ALU.add))
            pend_epi = (prb, ph, pqc, t_sq, t_ss)
        if kk == 11 and pend_epi is not None and len(pend_epi) == 5:
            (prb, ph, pqc, t_sq, t_ss) = pend_epi
            pend_epi = None
            t_r = act_epilogue(prb, t_ss)
            tmp_rd[prb] = pool_tail(prb, ph, pqc, t_r)
            ybT_dma(ph, pqc, tmp_rd[prb])
        if kk in ((1, 4, 7, 11, 14) if (r // 2) % 2 == 0 else (2, 5, 9, 12)):
            if conv_i["early"] < len(conv_q["early"]):
                emit_conv("early", 1)
            else:
                emit_conv("late", 1)
        if kk == 14:
            emit_cb(2 if r >= 24 else 1)
    (pr_, ph_, pp_, peb, pt_e) = pend_pv
    e_rd[peb] = emit_pv(ph_, pr_, pp_, peb, pt_e)
    t_pv_last = e_rd[peb]
    round_drain(pr_, t_pv_last)
    tok_att_pe = t_pv_last
    emit_conv("early", 2 * KW)
    emit_conv("late", 2 * KW)
    emit_cb(16)
    tail_box = {}
    xr_tok = [None] * NT

    def attention_tail():
        (prb, ph, pqc, t_sq) = pend_epi
        dve.wait(t_sq)
        t_ss = dve.done(nc.vector.tensor_reduce(out=small[:, SSA + 4 * prb:SSA + 4 * prb + 4],
                                                in_=at_t[:, prb, :, :], axis=AX.X, op=ALU.add))
        t_r = act_epilogue(prb, t_ss)
        t_yb = pool_tail(prb, ph, pqc, t_r)
        tail_box["yb"] = t_yb
        ybT_dma(ph, pqc, t_yb)
        sp.wait(t_yb, tok_att_pe, t_ss)
        for t in range(4):
            xr_tok[t] = ds_x[t].done(nc.sync.dma_start(out=xs[:, t, :], in_=xv[t]))
        pool.wait(t_yb)
        tail_box["wout"] = ds_w3.done(nc.gpsimd.dma_start(out=wout[:], in_=w_out.rearrange("(ft p) d -> p ft d", p=128)))

    m_free = [None, None]
    x_rd = [None] * 4
    pw_rd = [None, None]
    stats_rd = [None, None]
    t_stats = {}
    tok_ln = {}
    t_sil_all = {}
    tok_ya = {}
    nx = [0]

    def stats_pe(tc):
        pr = tc % 2
        sl = slice(tc * 512, (tc + 1) * 512)
        pe.wait([tok_cbp[(c_, tc)] for c_ in range(4)], stats_rd[pr], tok_att_pe, s_rd)
        for ct in range(4):
            nc.tensor.matmul(PA[:, 2 * pr, :], lhsT=onesm[:], rhs=cb[:, ct, sl], start=(ct == 0), stop=(ct == 3))
        for ct in range(4):
            ins = nc.tensor.matmul(PA[:, 2 * pr + 1, :], lhsT=onesm[:], rhs=csql[ct][:, sl], start=(ct == 0),
                                   stop=(ct == 3))
        t_stats[tc] = pe.done(ins)

    def stats_chain(tc):
        pr = tc % 2
        s_ = tc % 2
        t_st = t_stats[tc]
        act.wait(t_st, m_free[s_], tok_att_pe)
        t_m2 = act.done(nc.scalar.activation(out=ln_v[:, s_, :], in_=PA[:, 2 * pr, :], func=AF.Square))
        dve.wait(t_m2, t_st)
        t_var = dve.done(nc.vector.tensor_tensor(out=ln_v[:, s_, :], in0=PA[:, 2 * pr + 1, :], in1=ln_v[:, s_, :],
                                                 op=ALU.subtract))
        act.wait(t_var)
        t_l = act.done(nc.scalar.activation(out=ln_r[:, tc, :], in_=ln_v[:, s_, :], func=AF.Ln, bias=LN_EPS))
        act.wait(t_l)
        t_r = act.done(nc.scalar.activation(out=ln_r[:, tc, :], in_=ln_r[:, tc, :], func=AF.Exp, scale=-0.5))
        dve.wait(t_r)
        t_n = dve.done(nc.vector.scalar_tensor_tensor(out=ln_n[:, tc, :], in0=PA[:, 2 * pr, :], scalar=-1.0,
                                                      in1=ln_r[:, tc, :], op0=ALU.mult, op1=ALU.mult))
        stats_rd[pr] = [t_m2, t_var, t_n]
        m_free[s_] = [t_l]
        tok_ln[tc] = [t_r, t_n]

    norm_b = {}

    x_rd2 = [None] * 8

    def lnx(tc, ct):
        return (ln_x if tc % 2 == 0 else ln_x2)[:, ct, :]

    def norm_pre(tc):
        sl = slice(tc * 512, (tc + 1) * 512)
        toks = []
        for ct in range(4):
            b = (tc % 2) * 4 + ct
            eng, E = (nc.vector, dve) if ct in (0, 2) else (nc.gpsimd, pool)
            E.wait(tok_ln[tc], x_rd2[b], tok_att_pe, t_stats[3], [conv_prev[c_] for c_ in range(4)])
            t_a = E.done(eng.tensor_tensor(out=lnx(tc, ct), in0=cTl[ct][:, sl], in1=ln_r[:, tc, :], op=ALU.mult))
            E.wait(t_a)
            toks.append(E.done(eng.tensor_tensor(out=lnx(tc, ct), in0=lnx(tc, ct), in1=ln_n[:, tc, :], op=ALU.add)))
        norm_b[tc] = toks

    def norm_act(tc):
        sl = slice(tc * 512, (tc + 1) * 512)
        t_sil = [None] * 4
        for ct in range(4):
            b = (tc % 2) * 4 + ct
            act.wait(norm_b[tc][ct], tok_att_pe)
            t_sil[ct] = act.done(nc.scalar.activation(out=sT[:, ct, sl], in_=lnx(tc, ct), func=AF.Silu,
                                                      scale=colv[:, LG + ct:LG + ct + 1],
                                                      bias=colv[:, LB + ct:LB + ct + 1]))
            x_rd2[b] = t_sil[ct]
        t_sil_all[tc] = t_sil

    def pw_chunk(tc):
        sl = slice(tc * 512, (tc + 1) * 512)
        toks = []
        for et in range(4):
            a = et % 2
            pe.wait(t_sil_all[tc], wpw_tok, pw_rd[a])
            for ct in range(4):
                ins = nc.tensor.matmul(PB[:, 2 + a, :], lhsT=wpw[:, ct, et * 128:(et + 1) * 128], rhs=sT[:, ct, sl],
                                       start=(ct == 0), stop=(ct == 3))
            t_pw = pe.done(ins)
            dve.wait(t_pw, tok_sga, (ds_tr, 64 * 16))
            t_ya = dve.done(nc.vector.scalar_tensor_tensor(
                out=yT[:, et, sl], in0=PB[:, 2 + a, :], scalar=colv[:, BP + et:BP + et + 1], in1=sga[:, et, sl],
                op0=ALU.add, op1=ALU.mult))
            pw_rd[a] = t_ya
            toks.append(t_ya)
        tok_ya[tc] = toks

    tp_rd = [None, None]
    tp_done = {"pe": None, "ev": None}

    def yb_transposes():
        for pi in range(NT // 2):
            b = pi % 2
            pe.wait(tail_box["yb"], tp_rd[b], o_rd)
            ins = None
            for tl in range(2):
                for hh in range(4):
                    tt = pi * 2 + tl
                    ins = nc.tensor.transpose(PBb[:, b, (tl * 4 + hh) * 128:(tl * 4 + hh + 1) * 128],
                                              yb[:, tt, hh * 128:(hh + 1) * 128], identb[:])
            t_tp = pe.done(ins)
            act.wait(t_tp, t_stats[3])
            t_ev = act.done(nc.scalar.activation(
                out=yT[:, 4:8, pi * 256:(pi + 1) * 256].rearrange("p h (t i) -> p t h i", t=2),
                in_=PBb[:, b, :].rearrange("p (t h i) -> p t h i", t=2, h=4), func=AF.Copy,
                scale=colv[:, 149:150]))
            tp_rd[b] = t_ev
            tp_done["pe"] = t_tp
            tp_done["ev"] = t_ev

    ov = out.rearrange("(t p) d -> t p d", p=128)
    acc_rd = [None, None]
    r_rd = [None, None, None]
    ds_fg = DmaSem(nc, "fg")
    fg_box = [None]

    def p5_tile(t):
        slot = t % 4
        a = t % 2
        rb3 = t % 3
        pe.wait(tok_ya[t // 4], (ds_tr, 64 * 16), tail_box["wout"], acc_rd[a], stats_rd)
        for half in range(2):
            for ft in range(8):
                ins = nc.tensor.matmul(PA[:, 2 * a + half, :], lhsT=yT[:, ft, t * 128:(t + 1) * 128],
                                       rhs=wout[:, ft, half * 512:(half + 1) * 512], start=(ft == 0), stop=(ft == 7))
        t_mm = pe.done(ins)
        dve.wait(t_mm, xr_tok[t], r_rd[rb3], tok_ya[3] if 3 in tok_ya else None)
        t_res = dve.done(nc.vector.tensor_tensor(out=rbuf[:, rb3, :].rearrange("p (a c) -> p a c", a=2),
                                                 in0=PA[:, 2 * a:2 * a + 2, :],
                                                 in1=xs[:, slot, :].rearrange("p (a c) -> p a c", a=2), op=ALU.add))
        acc_rd[a] = t_res
        if t + 4 < NT:
            sp.wait(t_res)
            xr_tok[t + 4] = ds_x[slot].done(nc.sync.dma_start(out=xs[:, slot, :], in_=xv[t + 4]))
        act.wait(t_res)
        t_ss = act.done(nc.scalar.activation(out=junk5[:], in_=rbuf[:, rb3, :], func=AF.Square,
                                             accum_out=small[:, SS5 + t:SS5 + t + 1]))
        act.wait(t_ss)
        t_l = act.done(nc.scalar.activation(out=small[:, LN5 + t:LN5 + t + 1], in_=small[:, SS5 + t:SS5 + t + 1],
                                            func=AF.Ln, scale=1.0 / D, bias=EPS))
        act.wait(t_l)
        t_r = act.done(nc.scalar.activation(out=small[:, RS5 + t:RS5 + t + 1], in_=small[:, LN5 + t:LN5 + t + 1],
                                            func=AF.Exp, scale=-0.5))
        p5_pending.append((t, rb3, t_r, t_res))

    p5_pending = []

    def p5_finish():
        (t, rb3, t_r, t_res) = p5_pending.pop(0)
        dve.wait(t_r, t_res, fg_box[0])
        t_o = dve.done(nc.vector.scalar_tensor_tensor(out=rbuf[:, rb3, :], in0=rbuf[:, rb3, :],
                                                      scalar=small[:, RS5 + t:RS5 + t + 1], in1=fgB[:],
                                                      op0=ALU.mult, op1=ALU.mult))
        sp.wait(t_o)
        r_rd[rb3] = ds_o[rb3].done(nc.sync.dma_start(out=ov[t], in_=rbuf[:, rb3, :]))

    stats_pe(0)
    stats_pe(1)
    stats_chain(0)
    stats_pe(2)
    stats_chain(1)
    stats_pe(3)
    stats_chain(2)
    stats_chain(3)
    attention_tail()
    norm_pre(0)
    norm_pre(1)
    sp.wait(t_stats[3], [conv_prev[c_] for c_ in range(4)])
    fg_box[0] = ds_fg.done(nc.sync.dma_start(out=fgB[:], in_=fin_g.partition_broadcast(128)))
    norm_act(0)
    norm_pre(2)
    pw_chunk(0)
    norm_act(1)
    norm_pre(3)
    pw_chunk(1)
    norm_act(2)
    pw_chunk(2)
    norm_act(3)
    p5_tile(0)
    p5_tile(1)
    p5_finish()
    pw_chunk(3)
    for t in range(2, NT):
        p5_tile(t)
        p5_finish()
    p5_finish()
    return finish()


def _consts():
    idb = np.eye(128, dtype=np.float32).astype(ml_dtypes.bfloat16)
    idf = np.eye(128, dtype=np.float32)
    rot = np.zeros((128, 128), dtype=np.float32)
    for p2 in range(128):
        d = p2 % 64
        if d < 32:
            rot[p2 + 32, p2] = -1.0
        else:
            rot[p2 - 32, p2] = 1.0
    ones = np.full((128, 128), 1.0 / 512, dtype=np.float32).astype(ml_dtypes.bfloat16)
    half = 32
    try:
        import jax
        import jax.numpy as jnp
        with jax.default_device(jax.devices("cpu")[0]):
            inv_j = 1.0 / (10000.0 ** (jnp.arange(half, dtype=jnp.float32) * 2.0 / 64))
            pos_j = jnp.arange(S, dtype=jnp.float32)
            ang_j = pos_j[:, None] * inv_j[None, :]
            cos32 = np.asarray(jnp.cos(ang_j), dtype=np.float32).T
            sin32 = np.asarray(jnp.sin(ang_j), dtype=np.float32).T
    except Exception:
        inv_freq = (1.0 / (10000.0 ** (np.arange(half, dtype=np.float64) * 2.0 / 64.0))).astype(np.float32)
        pos = np.arange(S, dtype=np.float32)
        ang = (pos[None, :] * inv_freq[:, None]).astype(np.float32)
        cos32 = np.cos(ang.astype(np.float64)).astype(np.float32)
        sin32 = np.sin(ang.astype(np.float64)).astype(np.float32)
    idx = (np.arange(128) % 64) % 32
    cosT = cos32[idx].astype(np.float32)
    sinT = sin32[idx].astype(np.float32)
    return {"c_identb": idb, "c_identf": idf, "c_rot": rot.astype(ml_dtypes.bfloat16), "c_ones": ones,
            "c_cos": np.ascontiguousarray(cosT), "c_sin": np.ascontiguousarray(sinT)}


_NC_CACHE = {}


def kernel(x, norm_g, w_in, conv_w, conv_b, conv_ln_g, conv_ln_b, w_pw, b_pw, lambda_q1, lambda_k1,
           lambda_q2, lambda_k2, head_norm_g, w_out, final_norm_g, _debug=False):
    f = lambda a: np.ascontiguousarray(np.asarray(a, dtype=np.float32))
    shared = {
        "norm_g": f(norm_g).reshape(8, 128),
        "w_in": f(w_in).reshape(D, 3584),
        "conv_w": f(conv_w).reshape(KW * 4, 128),
        "conv_b": f(conv_b).reshape(4, 128),
        "ln_g": f(conv_ln_g).reshape(4, 128),
        "ln_b": f(conv_ln_b).reshape(4, 128),
        "w_pw": f(w_pw).reshape(C, C),
        "b_pw": f(b_pw).reshape(4, 128),
        "lq1": f(lambda_q1).reshape(1, 64),
        "lk1": f(lambda_k1).reshape(1, 64),
        "lq2": f(lambda_q2).reshape(1, 64),
        "lk2": f(lambda_k2).reshape(1, 64),
        "head_g": f(head_norm_g).reshape(1, 128),
        "w_out": f(w_out).reshape(D, D),
        "fin_g": f(final_norm_g).reshape(1, D),
    }
    shared.update(_consts())
    xf = f(x)
    in_maps = []
    for c in range(NCORES):
        m = dict(shared)
        m["x"] = np.ascontiguousarray(xf[c])
        in_maps.append(m)
    nc = build(debug=_debug)
    res = run_bass_kernel_spmd(nc, in_maps, core_ids=list(range(NCORES)))
    outp = np.stack([np.asarray(res.results[c]["out"]) for c in range(NCORES)], axis=0).astype(np.float32)
    if _debug:
        return outp, res.results
    return outp
```

```python
import numpy as np
import ml_dtypes
import concourse.bass as bass
import concourse.mybir as mybir
from concourse.bass_utils import run_bass_kernel_spmd

F32 = mybir.dt.float32
BF16 = mybir.dt.bfloat16
ALU = mybir.AluOpType
AF = mybir.ActivationFunctionType
AX = mybir.AxisListType

S = 2048
D = 1024
C = 512
NT = S // 128
NCORES = 8
KW = 31
PADL = 16
UW = PADL + S + 16
EPS = 1e-6
LN_EPS = 1e-5
LAM_INIT = 0.2
VW = 130


class Eng:
    def __init__(self, nc, eng, name):
        self.nc, self.e, self.name = nc, eng, name
        self.sem = nc.alloc_semaphore("sem_" + name)
        self.n = 0
        self.seen = {}

    def wait(self, *toks):
        for t in toks:
            if t is None:
                continue
            if isinstance(t, list):
                self.wait(*t)
                continue
            src, c = t
            if self.seen.get(src, 0) >= c:
                continue
            self.e.wait_ge(src.sem, c)
            self.seen[src] = c

    def done(self, ins):
        self.n += 1
        ins.then_inc(self.sem, 1)
        return (self, self.n)


class DmaSem:
    def __init__(self, nc, name):
        self.sem = nc.alloc_semaphore("dsem_" + name)
        self.n = 0
        DmaSem.ALL.append(self)

    def done(self, ins):
        self.n += 16
        ins.then_inc(self.sem, 16)
        return (self, self.n)


def build(debug=False, stop_after=None):
    nc = bass.Bass("TRN2", target_bir_lowering=False)
    DmaSem.ALL = []

    def din(name, shape, dt=F32):
        return nc.dram_tensor(name, shape, dt, kind="ExternalInput").ap()

    x = din("x", [S, D])
    norm_g = din("norm_g", [8, 128])
    w_in = din("w_in", [D, 3584])
    conv_w = din("conv_w", [KW * 4, 128])
    conv_b = din("conv_b", [4, 128])
    ln_g = din("ln_g", [4, 128])
    ln_b = din("ln_b", [4, 128])
    w_pw = din("w_pw", [C, C])
    b_pw = din("b_pw", [4, 128])
    lq1 = din("lq1", [1, 64])
    lk1 = din("lk1", [1, 64])
    lq2 = din("lq2", [1, 64])
    lk2 = din("lk2", [1, 64])
    head_g = din("head_g", [1, 128])
    w_out = din("w_out", [D, D])
    fin_g = din("fin_g", [1, D])
    c_identb = din("c_identb", [128, 128], BF16)
    c_identf = din("c_identf", [128, 128])
    c_rot = din("c_rot", [128, 128], BF16)
    c_ones = din("c_ones", [128, 128], BF16)
    c_cos = din("c_cos", [128, S])
    c_sin = din("c_sin", [128, S])
    out = nc.dram_tensor("out", [S, D], F32, kind="ExternalOutput").ap()
    dbg = {}

    off = [16512]

    def sb(name, shape, dt, at=None):
        nbytes = int(np.prod(shape[1:])) * (2 if dt == BF16 else 4)
        if at is None:
            at = off[0]
            off[0] += (nbytes + 63) // 64 * 64
        return nc.alloc_sbuf_tensor_at(name, shape, dt, offset=at), at

    identb, _ = sb("identb", [128, 128], BF16)
    identf, _ = sb("identf", [128, 128], F32)
    rotm, _ = sb("rotm", [128, 128], BF16)
    onesm, _ = sb("onesm", [128, 128], BF16)
    rows_c, _ = sb("rows_c", [8, 128], F32)
    colv, _ = sb("colv", [128, 160], F32)
    lam4, _ = sb("lam4", [128, 4, 64], F32)
    small, _ = sb("small", [128, 256], F32)
    hgB, _ = sb("hgB", [128, 4, 128], F32)
    cT0, _ = sb("cT0", [128, S], F32)
    cwh, _ = sb("cwh", [128, 4, 32], F32)
    wpw, _ = sb("wpw", [128, 4, C], BF16)
    hT, hT_at = sb("hT", [128, 8, S], BF16)
    xs, xs_at = sb("xs", [128, 4, D], F32)
    uT, uT_at = sb("uT", [128, 4, UW], BF16)
    sga, _ = sb("sga", [128, 4, S], BF16)
    Gt, G_at = sb("Gt", [128, NT, 512], BF16)
    QT, QT_at = sb("QT", [128, 4, S], BF16)
    KT, KT_at = sb("KT", [128, 4, S], BF16)
    Vg, V_at = sb("Vg", [128, NT, 4, VW], BF16)
    wsl, W_at = sb("wsl", [128, 3, 8, 512], BF16)
    rt1, _ = sb("rt1", [128, 2, 512], F32)
    rt2, _ = sb("rt2", [128, 2, 512], F32)
    junk, T_at = sb("junk", [128, D], BF16)
    xnb, _ = sb("xnb", [128, 2, D], BF16)
    thb, _ = sb("thb", [128, 2, 512], F32)
    qbb, _ = sb("qbb", [128, 2, 512], BF16)
    assert off[0] <= 229376, off[0]
    yT = hT
    yb, _ = sb("yb", [128, NT, 512], BF16, at=hT_at)
    cosT, _ = sb("cosT", [128, S], F32, at=xs_at)
    sinT, _ = sb("sinT", [128, S], F32, at=xs_at + 8192)
    Eb, _ = sb("Eb", [128, 3, 2, 512], BF16, at=xs_at)
    at_t, _ = sb("at_t", [128, 2, 4, 128], F32, at=xs_at + 6144)
    at_o, _ = sb("at_o", [128, 2, 4, 128], F32, at=xs_at + 6144 + 4096)
    cTl = [cT0, sb("cT1", [128, S], F32, at=T_at)[0], sb("cT2", [128, S], F32, at=W_at)[0],
           sb("cT3", [128, S], F32, at=W_at + 8192)[0]]
    rows_a, _ = sb("rows_a", [128, 128], F32, at=QT_at)
    rows_b, _ = sb("rows_b", [32, 128], F32, at=QT_at + 512)
    hg1, _ = sb("hg1", [128, 128], F32, at=QT_at + 1024)
    fgB, _ = sb("fgB", [128, D], F32, at=uT_at)
    cb, _ = sb("cb", [128, 4, S], BF16, at=W_at + 16384)
    csql = [sb("csq%d" % i, [128, S], BF16, at=uT_at + i * UW * 2)[0] for i in range(4)]
    sT, _ = sb("sT", [128, 4, S], BF16, at=V_at)
    wout, _ = sb("wout", [128, 8, D], BF16, at=G_at)
    ln_r, _ = sb("ln_r", [128, 4, 512], F32, at=QT_at)
    ln_n, _ = sb("ln_n", [128, 4, 512], F32, at=QT_at + 8192)
    ln_m, _ = sb("ln_m", [128, 2, 512], F32, at=KT_at)
    ln_v, _ = sb("ln_v", [128, 2, 512], F32, at=KT_at + 4096)
    ln_x, _ = sb("ln_x", [128, 4, 512], F32, at=KT_at + 8192)
    ln_x2, _ = sb("ln_x2", [128, 4, 512], F32, at=uT_at + 4096)
    rbuf, _ = sb("rbuf", [128, 3, D], F32, at=W_at + 16384)
    junk5, _ = sb("junk5", [128, D], BF16, at=W_at + 28672)

    PA = nc.alloc_psum_tensor("PA", [128, 4, 512], F32)
    PB = nc.alloc_psum_tensor("PB", [128, 4, 512], F32)
    PBb = PB[:].bitcast(BF16)

    pe = Eng(nc, nc.tensor, "pe")
    act = Eng(nc, nc.scalar, "act")
    dve = Eng(nc, nc.vector, "dve")
    pool = Eng(nc, nc.gpsimd, "pool")
    sp = Eng(nc, nc.sync, "sp")
    ds_const = DmaSem(nc, "const")
    ds_x = [DmaSem(nc, "x%d" % i) for i in range(4)]
    ds_w = [DmaSem(nc, "w%d" % i) for i in range(3)]
    ds_cs = DmaSem(nc, "cs")
    ds_w2 = DmaSem(nc, "wx")
    ds_w3 = DmaSem(nc, "wy")
    ds_o = [DmaSem(nc, "out%d" % i) for i in range(3)]

    def finish():
        for e_ in (pe, act, dve, pool):
            if e_.n:
                nc.sync.wait_ge(e_.sem, e_.n)
        for d_ in DmaSem.ALL:
            if d_.n:
                nc.sync.wait_ge(d_.sem, d_.n)
        return nc

    SS1, RS1 = 0, 16
    LAMC = 40
    ZR = 48
    NZ = 64
    SSA = 72
    LNA = 80
    RSA = 88
    SS5, LN5, RS5 = 96, 112, 128

    ds_c0 = DmaSem(nc, "c0")
    for dst, src in [(identb[:], c_identb), (identf[:], c_identf), (rows_c[0:8, :], norm_g)]:
        tok_c0 = ds_c0.done(nc.scalar.dma_start(out=dst, in_=src))
    xv = x.rearrange("(t p) d -> t p d", p=128)
    x_tok = [None] * NT
    for t in range(4):
        x_tok[t] = ds_x[t].done(nc.sync.dma_start(out=xs[:, t, :], in_=xv[t]))
    tok_c = None
    for dst, src in [(rotm[:], c_rot), (onesm[:], c_ones),
                     (rows_a[0:124, :], conv_w), (rows_b[0:4, :], conv_b), (rows_b[4:8, :], ln_g),
                     (rows_b[8:12, :], ln_b), (rows_b[12:16, :], b_pw), (rows_b[16:17, :], head_g),
                     (lam4[:, 0, :], lq1.partition_broadcast(128)), (lam4[:, 1, :], lk1.partition_broadcast(128)),
                     (lam4[:, 2, :], lq2.partition_broadcast(128)), (lam4[:, 3, :], lk2.partition_broadcast(128)),
                     (hg1[:], head_g.partition_broadcast(128))]:
        tok_c = ds_const.done(nc.sync.dma_start(out=dst, in_=src))

    w_v = w_in.rearrange("(dt p) c -> p dt c", p=128)
    SLABS = [
        [(512, 256), (0, 256)],
        [(768, 256), (256, 256)],
        [(1024, 512)],
        [(1536, 512)],
        [(2048, 512)],
        [(2560, 512)],
        [(3072, 512)],
    ]
    slab_tok = [None] * 7

    def load_slab(si):
        slot = si % 3
        c0 = 0
        tk = None
        for (cs, wd) in SLABS[si]:
            tk = ds_w[slot].done(nc.gpsimd.dma_start(out=wsl[:, slot, :, c0:c0 + wd], in_=w_v[:, :, cs:cs + wd]))
            c0 += wd
        slab_tok[si] = tk

    load_slab(0)
    wpw_tok = ds_w2.done(nc.gpsimd.dma_start(out=wpw[:], in_=w_pw.rearrange("(ct p) e -> p ct e", p=128)))

    pool.done(nc.gpsimd.memset(Vg[:], 1.0))
    tok_vg1 = pool.done(nc.gpsimd.memset(uT[:, :, 0:PADL], 0.0))
    tok_upad = pool.done(nc.gpsimd.memset(uT[:, :, PADL + S:UW], 0.0))

    CB, LG, LB, BP, NG = 124, 128, 132, 136, 140
    pe.wait(tok_c0)
    t_pe = pe.done(nc.tensor.matmul(PA[:, 1, 0:8], lhsT=rows_c[0:8, :], rhs=identf[0:8, 0:8], start=True, stop=True))
    dve.wait(t_pe)
    tok_ng = dve.done(nc.vector.tensor_copy(out=colv[:, NG:NG + 8], in_=PA[:, 1, 0:8]))

    p1 = {"xnb_rd": [None, None], "tp_rd": [None, None], "last": None}

    def emit_p1(t):
        slot = t % 4
        b = t % 2
        act.wait(x_tok[t])
        t_ss = act.done(nc.scalar.activation(out=junk[:], in_=xs[:, slot, :], func=AF.Square,
                                             accum_out=small[:, SS1 + t:SS1 + t + 1]))
        act.wait(t_ss)
        t_ln = act.done(nc.scalar.activation(out=small[:, RS1 + t:RS1 + t + 1], in_=small[:, SS1 + t:SS1 + t + 1],
                                             func=AF.Ln, scale=1.0 / D, bias=EPS))
        act.wait(t_ln)
        t_rs = act.done(nc.scalar.activation(out=small[:, RS1 + t:RS1 + t + 1], in_=small[:, RS1 + t:RS1 + t + 1],
                                             func=AF.Exp, scale=-0.5))
        dve.wait(t_rs, x_tok[t], p1["xnb_rd"][b])
        t_xn = dve.done(nc.vector.tensor_scalar(out=xnb[:, b, :], in0=xs[:, slot, :],
                                                scalar1=small[:, RS1 + t:RS1 + t + 1], scalar2=None, op0=ALU.mult))
        if t + 4 < NT:
            sp.wait(t_xn, t_ss)
            x_tok[t + 4] = ds_x[slot].done(nc.sync.dma_start(out=xs[:, slot, :], in_=xv[t + 4]))
        pe.wait(t_xn, p1["tp_rd"][b])
        for dt in range(8):
            ins = nc.tensor.transpose(PBb[:, 2 + b, dt * 128:(dt + 1) * 128], xnb[:, b, dt * 128:(dt + 1) * 128], identb[:])
        t_tp = pe.done(ins)
        p1["xnb_rd"][b] = t_tp
        return t_tp

    def emit_p1_evac(t, t_tp):
        b = t % 2
        dve.wait(t_tp, tok_ng)
        t_ev = dve.done(nc.vector.tensor_tensor(
            out=hT[:, :, t * 128:(t + 1) * 128],
            in0=PBb[:, 2 + b, :].rearrange("p (dt i) -> p dt i", dt=8),
            in1=colv[:, NG:NG + 8].unsqueeze(2).to_broadcast([128, 8, 128]), op=ALU.mult))
        p1["tp_rd"][b] = t_ev
        p1["last"] = t_ev
        return t_ev

    hT_tok = [None] * NT
    tp_tok = [None] * NT
    for t in range(NT + 1):
        if t < NT:
            tp_tok[t] = emit_p1(t)
        if t >= 1:
            hT_tok[t - 1] = emit_p1_evac(t - 1, tp_tok[t - 1])
        if t == 9:
            pool.wait(x_tok[12])
            load_slab(1)
        if t == 13:
            pool.wait(x_tok[15])
            load_slab(2)
    sp.wait(p1["last"])
    ds_cs.done(nc.sync.dma_start(out=cosT[:], in_=c_cos))
    cs_tok = ds_cs.done(nc.sync.dma_start(out=sinT[:], in_=c_sin))

    pe.wait(tok_c)
    nc.tensor.matmul(PA[:, 0, 0:124], lhsT=rows_a[0:124, :], rhs=identf[0:124, 0:124], start=True, stop=True)
    t_pe = pe.done(nc.tensor.matmul(PA[:, 1, 0:17], lhsT=rows_b[0:17, :], rhs=identf[0:17, 0:17], start=True, stop=True))
    dve.wait(t_pe)
    nc.vector.tensor_copy(out=colv[:, 0:124], in_=PA[:, 0, 0:124])
    nc.vector.tensor_copy(out=colv[:, 124:140], in_=PA[:, 1, 0:16])
    tok_colv = dve.done(nc.vector.tensor_scalar(out=colv[:, 149:150], in0=PA[:, 1, 16:17], scalar1=1.0 - LAM_INIT,
                                                scalar2=None, op0=ALU.mult))
    dve.wait(tok_colv)
    tok_cwh = dve.done(nc.vector.tensor_scalar(
        out=cwh[:, :, 0:KW], in0=colv[:, 0:124].rearrange("p (j ct) -> p ct j", ct=4),
        scalar1=0.5, scalar2=None, op0=ALU.mult))
    dve.wait(tok_c)
    tok_hgB = dve.done(nc.vector.tensor_scalar(
        out=hgB[:], in0=hg1[:].unsqueeze(1).to_broadcast([128, 4, 128]),
        scalar1=1.0 - LAM_INIT, scalar2=None, op0=ALU.mult))
    nc.vector.tensor_tensor(out=lam4[:, 0, :], in0=lam4[:, 0, :], in1=lam4[:, 1, :], op=ALU.mult)
    t1 = dve.done(nc.vector.tensor_tensor(out=lam4[:, 2, :], in0=lam4[:, 2, :], in1=lam4[:, 3, :], op=ALU.mult))
    dve.wait(t1)
    t2 = dve.done(nc.vector.tensor_reduce(out=small[:, LAMC:LAMC + 2],
                                          in_=lam4[:].rearrange("p (a b) d -> p a b d", b=2)[:, :, 0, :],
                                          axis=AX.X, op=ALU.add))
    act.wait(t2)
    t3 = act.done(nc.scalar.activation(out=small[:, LAMC + 2:LAMC + 4], in_=small[:, LAMC:LAMC + 2], func=AF.Exp))
    dve.wait(t3)
    t4 = dve.done(nc.vector.tensor_tensor(out=small[:, LAMC + 4:LAMC + 5], in0=small[:, LAMC + 3:LAMC + 4],
                                          in1=small[:, LAMC + 2:LAMC + 3], op=ALU.subtract))
    dve.wait(t4)
    tok_lam = dve.done(nc.vector.tensor_scalar(out=small[:, LAMC + 5:LAMC + 6], in0=small[:, LAMC + 4:LAMC + 5],
                                               scalar1=-LAM_INIT, scalar2=None, op0=ALU.add))
    neglam = small[:, LAMC + 5:LAMC + 6]
    tok_pro_pe = t_pe
    if stop_after == "prologue":
        return finish()

    st = {"g": 0, "acc_rd": [tok_colv, tok_colv, None, None], "rot_rd": [None, None], "t_rd": [None, None], "qb_rd": [None, None],
          "th_rd": [None, None], "pending_rot": None, "nqk": 0, "nglu": 0, "nbg": 0}
    slab_done = [None] * 7
    tok_uT = [None] * 4
    tok_QK = {}
    tok_V = [None] * NT
    tok_G = [None] * NT
    tok_sga = None

    conv_prev = {}
    conv_gate = {"w": None}

    def conv_step(ct, j):
        src = uT[:, ct, PADL - 15 + j:PADL - 15 + j + S]
        if j == 0:
            gate = {0: [tok_pro_pe],
                    1: [p1["xnb_rd"][0], p1["xnb_rd"][1], p1["last"], st["th_rd"][0], st["th_rd"][1]],
                    2: conv_gate["w"], 3: conv_gate["w"]}[ct]
            dve.wait(tok_uT[ct], tok_upad, tok_cwh, tok_colv, gate)
            ins = nc.vector.tensor_scalar(out=cTl[ct][:], in0=src, scalar1=cwh[:, ct, 0:1],
                                          scalar2=colv[:, CB + ct:CB + ct + 1], op0=ALU.mult, op1=ALU.add)
        else:
            dve.wait(conv_prev[ct])
            ins = nc.vector.scalar_tensor_tensor(out=cTl[ct][:], in0=src, scalar=cwh[:, ct, j:j + 1],
                                                 in1=cTl[ct][:], op0=ALU.mult, op1=ALU.add)
        conv_prev[ct] = dve.done(ins)

    conv_q = {"early": [(0, j) for j in range(KW)] + [(1, j) for j in range(KW)],
              "late": [(2, j) for j in range(KW)] + [(3, j) for j in range(KW)]}
    conv_i = {"early": 0, "late": 0}
    tok_cb = [None] * 4
    tok_cbp = {}

    def emit_conv(which, n):
        for _ in range(n):
            i = conv_i[which]
            if i >= len(conv_q[which]):
                return
            conv_i[which] = i + 1
            ct, j = conv_q[which][i]
            conv_step(ct, j)
            if j == KW - 1:
                for c_ in range(4):
                    cb_queue.append((ct, c_))

    cb_queue = []

    def emit_cb(n):
        for _ in range(n):
            if not cb_queue:
                return
            ct, c_ = cb_queue.pop(0)
            sl = slice(c_ * 512, (c_ + 1) * 512)
            pool.wait(conv_prev[ct], tok_p2_pe_box[0])
            nc.gpsimd.tensor_copy(out=cb[:, ct, sl], in_=cTl[ct][:, sl])
            tok_cb[ct] = tok_cbp[(ct, c_)] = pool.done(nc.gpsimd.tensor_tensor(out=csql[ct][:, sl], in0=cTl[ct][:, sl],
                                                           in1=cTl[ct][:, sl], op=ALU.mult))

    tok_p2_pe_box = [None]

    def fm_group(slot, ct, tc):
        a = st["g"] % 4
        st["g"] += 1
        pe.wait(st["acc_rd"][a], [hT_tok[tc * 4 + i] for i in range(4)])
        for dt in range(8):
            ins = nc.tensor.matmul(PA[:, a, :], lhsT=wsl[:, slot, dt, ct * 128:(ct + 1) * 128],
                                   rhs=hT[:, dt, tc * 512:(tc + 1) * 512], start=(dt == 0), stop=(dt == 7))
        return a, pe.done(ins)

    def tm_group(slot, tt):
        a = st["g"] % 4
        st["g"] += 1
        pe.wait(st["acc_rd"][a], hT_tok[tt])
        for dt in range(8):
            ins = nc.tensor.matmul(PA[:, a, :], lhsT=hT[:, dt, tt * 128:(tt + 1) * 128],
                                   rhs=wsl[:, slot, dt, :], start=(dt == 0), stop=(dt == 7))
        return a, pe.done(ins)

    def flush_rot():
        pr = st["pending_rot"]
        if pr is None:
            return
        st["pending_rot"] = None
        (dst, ct, tc, b, t_qb, t_t1) = pr
        r = b
        pe.wait(t_qb, st["rot_rd"][r])
        t_rot = pe.done(nc.tensor.matmul(PB[:, r, :], lhsT=rotm[:], rhs=qbb[:, b, :], start=True, stop=True))
        st["qb_rd"][b] = t_rot
        dve.wait(t_rot)
        t_t2 = dve.done(nc.vector.tensor_tensor(out=rt2[:, b, :], in0=PB[:, r, :],
                                                in1=sinT[:, tc * 512:(tc + 1) * 512], op=ALU.mult))
        st["rot_rd"][r] = t_t2
        pool.wait(t_t2, t_t1, tok_hgB, tok_pro_pe)
        if dst is QT:
            o_ap = QT[:, ct, :].rearrange("p (i n) -> p n i", n=NT)[:, 4 * tc:4 * tc + 4, :]
            i0 = rt1[:, b, :].rearrange("p (n i) -> p n i", n=4)
            i1 = rt2[:, b, :].rearrange("p (n i) -> p n i", n=4)
        else:
            o_ap = dst[:, ct, tc * 512:(tc + 1) * 512]
            i0, i1 = rt1[:, b, :], rt2[:, b, :]
        t_add = pool.done(nc.gpsimd.tensor_tensor(out=o_ap, in0=i0, in1=i1, op=ALU.add))
        st["t_rd"][b] = t_add
        tok_QK[(id(dst), ct, tc)] = t_add

    EARLY_TAPS = {1: 12, 2: 13, 3: 2, 4: 5, 5: 12, 6: 12}
    for si in range(7):
        slot = si % 3
        pe.wait(slab_tok[si])
        last_pe = None
        ngrp = [0]

        def after_group():
            ngrp[0] += 1
            n_t = EARLY_TAPS.get(si, 0)
            if n_t and (ngrp[0] * n_t) // 16 > ((ngrp[0] - 1) * n_t) // 16:
                emit_conv("early", 1)

        if si < 2:
            for tc in range(4):
                for cl in range(2):
                    ct = si * 2 + cl
                    a, t_mm = fm_group(slot, cl, tc)
                    b = st["nglu"] % 2
                    st["nglu"] += 1
                    act.wait(t_mm, st["th_rd"][b])
                    t_th = act.done(nc.scalar.activation(out=thb[:, b, :], in_=PA[:, a, :], func=AF.Tanh, scale=0.5))
                    st["acc_rd"][a] = t_th
                    a2, t_mm2 = fm_group(slot, 2 + cl, tc)
                    dve.wait(t_mm2, t_th)
                    t_u = dve.done(nc.vector.scalar_tensor_tensor(
                        out=uT[:, ct, PADL + tc * 512:PADL + (tc + 1) * 512], in0=thb[:, b, :], scalar=1.0,
                        in1=PA[:, a2, :], op0=ALU.add, op1=ALU.mult))
                    st["acc_rd"][a2] = t_u
                    st["th_rd"][b] = t_u
                    tok_uT[ct] = t_u
                    last_pe = t_mm2
                if si == 1:
                    for _ in range(4):
                        after_group()
        elif si == 2:
            for tc in range(4):
                for ct in range(4):
                    a, t_mm = fm_group(slot, ct, tc)
                    act.wait(t_mm)
                    t_s = act.done(nc.scalar.activation(out=sga[:, ct, tc * 512:(tc + 1) * 512], in_=PA[:, a, :],
                                                        func=AF.Silu))
                    st["acc_rd"][a] = t_s
                    tok_sga = t_s
                    last_pe = t_mm
                    after_group()
        elif si in (3, 4):
            dst = QT if si == 3 else KT
            for tc in range(4):
                for ct in range(4):
                    a, t_mm = fm_group(slot, ct, tc)
                    flush_rot()
                    b = st["nqk"] % 2
                    st["nqk"] += 1
                    act.wait(t_mm, st["qb_rd"][b])
                    t_qb = act.done(nc.scalar.activation(out=qbb[:, b, :], in_=PA[:, a, :], func=AF.Copy))
                    dve.wait(t_mm, t_qb, cs_tok, st["t_rd"][b])
                    t_t1 = dve.done(nc.vector.tensor_tensor(out=rt1[:, b, :], in0=PA[:, a, :],
                                                            in1=cosT[:, tc * 512:(tc + 1) * 512], op=ALU.mult))
                    st["acc_rd"][a] = [t_qb, t_t1]
                    st["pending_rot"] = (dst, ct, tc, b, t_qb, t_t1)
                    last_pe = t_mm
                    after_group()
        elif si == 5:
            flush_rot()
            for tt in range(NT):
                a, t_mm = tm_group(slot, tt)
                act.wait(t_mm, tok_vg1)
                t_v = act.done(nc.scalar.activation(out=Vg[:, tt, :, 0:128],
                                                    in_=PA[:, a, :].rearrange("p (h e) -> p h e", h=4), func=AF.Copy))
                st["acc_rd"][a] = t_v
                tok_V[tt] = t_v
                last_pe = t_mm
                after_group()
        else:
            for tt in range(NT):
                a, t_mm = tm_group(slot, tt)
                b = st["nbg"] % 4
                st["nbg"] += 1
                if "g_rd" not in st:
                    st["g_rd"] = [st["t_rd"][0], st["t_rd"][1], st["t_rd"][0], st["t_rd"][1]]
                gbuf = (rt1 if b < 2 else rt2)[:, b % 2, :]
                act.wait(t_mm, st["g_rd"][b])
                t_s = act.done(nc.scalar.activation(out=gbuf, in_=PA[:, a, :], func=AF.Silu))
                st["acc_rd"][a] = t_s
                dve.wait(t_s, tok_hgB)
                t_g = dve.done(nc.vector.tensor_tensor(out=Gt[:, tt, :], in0=gbuf,
                                                       in1=hgB[:].rearrange("p h e -> p (h e)"), op=ALU.mult))
                st["g_rd"][b] = t_g
                tok_G[tt] = t_g
                last_pe = t_mm
                after_group()
        slab_done[si] = last_pe
        if si + 3 < 7:
            pool.wait(last_pe)
            load_slab(si + 3)
        if stop_after == "p2s%d" % si:
            return finish()
    tok_p2_pe = slab_done[6]
    tok_p2_pe_box[0] = tok_p2_pe
    conv_gate["w"] = [tok_p2_pe]
    if stop_after == "p2":
        return finish()

    s_rd = [None, None]
    e_rd = [None, None, None]
    o_rd = None
    tmp_rd = [None, None]
    pend_pv = None
    rounds = [(h, qc) for h in range(4) for qc in range(4)]

    def emit_pv(h, r, p, eb, t_e):
        st_ = r % 2
        fw = [tok_V[2 * p], tok_V[2 * p + 1]]
        if p == 0:
            fw.append(o_rd[st_])
        pe.wait(t_e, fw)
        ins = None
        for kp in range(2):
            kt = 2 * p + kp
            for j in range(2):
                for q2 in range(2):
                    ins = nc.tensor.matmul(PB[:, 2 * st_ + q2, j * 256:j * 256 + 129],
                                           lhsT=Eb[:, eb, j, kp * 256 + q2 * 128:kp * 256 + (q2 + 1) * 128],
                                           rhs=Vg[:, kt, h, 0:129], start=(kt == 0 and j == 0), stop=(kt == NT - 1),
                                           skip_group_check=True)
        return pe.done(ins)

    def act_epilogue(rb, t_ss):
        act.wait(t_ss)
        t_l = act.done(nc.scalar.activation(out=small[:, LNA + 4 * rb:LNA + 4 * rb + 4],
                                            in_=small[:, SSA + 4 * rb:SSA + 4 * rb + 4], func=AF.Ln,
                                            scale=1.0 / 128, bias=EPS))
        act.wait(t_l)
        return act.done(nc.scalar.activation(out=small[:, RSA + 4 * rb:RSA + 4 * rb + 4],
                                             in_=small[:, LNA + 4 * rb:LNA + 4 * rb + 4], func=AF.Exp, scale=-0.5))

    def pool_tail(prb, ph, pqc, t_r):
        pool.wait(t_r)
        t_y = pool.done(nc.gpsimd.tensor_tensor(
            out=at_o[:, prb, :, :], in0=at_o[:, prb, :, :],
            in1=small[:, RSA + 4 * prb:RSA + 4 * prb + 4].unsqueeze(2).to_broadcast([128, 4, 128]), op=ALU.mult))
        pool.wait(t_y, [tok_G[pqc * 4 + i] for i in range(4)])
        return pool.done(nc.gpsimd.tensor_tensor(
            out=yb[:, pqc * 4:(pqc + 1) * 4, ph * 128:(ph + 1) * 128], in0=at_o[:, prb, :, :],
            in1=Gt[:, pqc * 4:(pqc + 1) * 4, ph * 128:(ph + 1) * 128], op=ALU.mult))

    ds_tr = DmaSem(nc, "tr")

    def ybT_dma(ph, pqc, t_yb):
        sp.wait(t_yb, tok_p2_pe)
        for i in range(4):
            tt = pqc * 4 + i
            ds_tr.done(nc.sync.dma_start_transpose(out=yT[:, 4 + ph, tt * 128:(tt + 1) * 128],
                                                   in_=yb[:, tt, ph * 128:(ph + 1) * 128]))

    pend_epi = None
    pe.wait(tok_p2_pe, st["acc_rd"], st["rot_rd"], p1["tp_rd"])
    act.wait(st["t_rd"][0], st["t_rd"][1])
    dve.wait(st["t_rd"][0], st["t_rd"][1])
    o_rd = [None, None]
    NP = NT // 2
    iters = [(r, r // 8, r % 8, p) for r in range(32) for p in range(NP)]
    s_tok = {}

    def emit_scores(g):
        (r_, h_, q8_, p_) = iters[g]
        sb_i = g % 2
        pe.wait(s_rd[sb_i], [tok_QK[(id(QT), h_, tcq)] for tcq in range(4)], tok_QK[(id(KT), h_, p_ // 2)])
        ins = None
        for kp in range(2):
            kt_ = 2 * p_ + kp
            for j in range(2):
                ins = nc.tensor.matmul(PA[:, 2 * sb_i + j, kp * 256:(kp + 1) * 256],
                                       lhsT=KT[64 * j:64 * j + 64, h_, kt_ * 128:(kt_ + 1) * 128],
                                       rhs=QT[64 * j:64 * j + 64, h_, q8_ * 256:(q8_ + 1) * 256], start=True, stop=True,
                                       skip_group_check=True)
        s_tok[g] = pe.done(ins)

    def round_drain(r_, t_pv_last_):
        nonlocal pend_epi
        st_ = r_ % 2
        sr = r_ // 2
        rb = sr % 2
        hh = r_ % 2
        Ov = PB[:, 2 * st_:2 * st_ + 2, :].rearrange("p b (j c) -> p b j c", j=2)
        dve.wait(t_pv_last_, tmp_rd[rb] if hh == 0 else None, tok_lam)
        zr = small[:, ZR + 8 * rb + 4 * hh:ZR + 8 * rb + 4 * hh + 4].rearrange("p (b j) -> p b j", j=2)
        t_z = dve.done(nc.vector.reciprocal(out=zr, in_=Ov[:, :, :, 128]))
        dve.wait(t_z)
        nzc = small[:, NZ + 4 * rb + 2 * hh:NZ + 4 * rb + 2 * hh + 2]
        t_nz = dve.done(nc.vector.tensor_tensor(out=nzc, in0=zr[:, :, 1], in1=neglam.to_broadcast([128, 2]),
                                                op=ALU.mult))
        t_o1 = dve.done(nc.vector.tensor_tensor(
            out=at_o[:, rb, 2 * hh:2 * hh + 2, :], in0=Ov[:, :, 0, 0:128],
            in1=zr[:, :, 0].unsqueeze(2).to_broadcast([128, 2, 128]), op=ALU.mult))
        dve.wait(t_nz, t_o1)
        for q2 in range(2):
            t_o = dve.done(nc.vector.scalar_tensor_tensor(
                out=at_o[:, rb, 2 * hh + q2, :], in0=Ov[:, q2, 1, 0:128], scalar=nzc[:, q2:q2 + 1],
                in1=at_o[:, rb, 2 * hh + q2, :], op0=ALU.mult, op1=ALU.add))
        o_rd[st_] = t_o
        if hh == 1:
            pool.wait(t_o, tmp_rd[rb])
            t_sq = pool.done(nc.gpsimd.tensor_tensor(out=at_t[:, rb, :, :], in0=at_o[:, rb, :, :],
                                                     in1=at_o[:, rb, :, :], op=ALU.mult))
            pend_epi = (rb, r_ // 8, (r_ % 8) // 2, t_sq)

    t_pv_last = None
    emit_scores(0)
    for g, (r, h, q8, p) in enumerate(iters):
        eb = g % 3
        if g + 1 < len(iters):
            emit_scores(g + 1)
        if pend_pv is not None:
            (pr_, ph_, pp_, peb, pt_e) = pend_pv
            e_rd[peb] = emit_pv(ph_, pr_, pp_, peb, pt_e)
            if pp_ == NP - 1:
                round_drain(pr_, e_rd[peb])
        act.wait(s_tok[g], e_rd[eb])
        t_e = act.done(nc.scalar.activation(out=Eb[:, eb, :, :], in_=PA[:, 2 * (g % 2):2 * (g % 2) + 2, :],
                                            func=AF.Exp, scale=0.125))
        s_rd[g % 2] = t_e
        pend_pv = (r, h, p, eb, t_e)
        kk = (r % 2) * NP + p
        if kk == 7 and pend_epi is not None and len(pend_epi) == 4:
            (prb, ph, pqc, t_sq) = pend_epi
            dve.wait(t_sq)
            t_ss = dve.done(nc.vector.tensor_reduce(out=small[:, SSA + 4 * prb:SSA + 4 * prb + 4],
                                                    in_=at_t[:, prb, :, :], axis=AX.X, op=ALU.add))
            pend_epi = (prb, ph, pqc, t_sq, t_ss)
        if kk == 11 and pend_epi is not None and len(pend_epi) == 5:
            (prb, ph, pqc, t_sq, t_ss) = pend_epi
            pend_epi = None
            t_r = act_epilogue(prb, t_ss)
            tmp_rd[prb] = pool_tail(prb, ph, pqc, t_r)
            ybT_dma(ph, pqc, tmp_rd[prb])
        if kk in ((1, 3, 5, 7, 10) if (r // 2) in (0, 4, 8, 12) else (2, 5, 9, 12)):
            if conv_i["early"] < len(conv_q["early"]):
                emit_conv("early", 1)
            else:
                emit_conv("late", 1)
        if kk == 14:
            emit_cb(2 if r >= 24 else 1)
    (pr_, ph_, pp_, peb, pt_e) = pend_pv
    e_rd[peb] = emit_pv(ph_, pr_, pp_, peb, pt_e)
    t_pv_last = e_rd[peb]
    round_drain(pr_, t_pv_last)
    tok_att_pe = t_pv_last
    emit_conv("early", 2 * KW)
    emit_conv("late", 2 * KW)
    emit_cb(16)
    tail_box = {}
    xr_tok = [None] * NT

    def attention_tail():
        (prb, ph, pqc, t_sq) = pend_epi
        dve.wait(t_sq)
        t_ss = dve.done(nc.vector.tensor_reduce(out=small[:, SSA + 4 * prb:SSA + 4 * prb + 4],
                                                in_=at_t[:, prb, :, :], axis=AX.X, op=ALU.add))
        t_r = act_epilogue(prb, t_ss)
        t_yb = pool_tail(prb, ph, pqc, t_r)
        tail_box["yb"] = t_yb
        ybT_dma(ph, pqc, t_yb)
        sp.wait(t_yb, tok_att_pe, t_ss)
        for t in range(4):
            xr_tok[t] = ds_x[t].done(nc.sync.dma_start(out=xs[:, t, :], in_=xv[t]))
        pool.wait(t_yb)
        tail_box["wout"] = ds_w3.done(nc.gpsimd.dma_start(out=wout[:], in_=w_out.rearrange("(ft p) d -> p ft d", p=128)))

    m_free = [None, None]
    x_rd = [None] * 4
    pw_rd = [None, None]
    stats_rd = [None, None]
    t_stats = {}
    tok_ln = {}
    t_sil_all = {}
    tok_ya = {}
    nx = [0]

    def stats_pe(tc):
        pr = tc % 2
        sl = slice(tc * 512, (tc + 1) * 512)
        pe.wait([tok_cbp[(c_, tc)] for c_ in range(4)], stats_rd[pr], tok_att_pe, s_rd)
        for ct in range(4):
            nc.tensor.matmul(PA[:, 2 * pr, :], lhsT=onesm[:], rhs=cb[:, ct, sl], start=(ct == 0), stop=(ct == 3))
        for ct in range(4):
            ins = nc.tensor.matmul(PA[:, 2 * pr + 1, :], lhsT=onesm[:], rhs=csql[ct][:, sl], start=(ct == 0),
                                   stop=(ct == 3))
        t_stats[tc] = pe.done(ins)

    def stats_chain(tc):
        pr = tc % 2
        s_ = tc % 2
        t_st = t_stats[tc]
        act.wait(t_st, m_free[s_], tok_att_pe)
        t_m2 = act.done(nc.scalar.activation(out=ln_v[:, s_, :], in_=PA[:, 2 * pr, :], func=AF.Square))
        dve.wait(t_m2, t_st)
        t_var = dve.done(nc.vector.tensor_tensor(out=ln_v[:, s_, :], in0=PA[:, 2 * pr + 1, :], in1=ln_v[:, s_, :],
                                                 op=ALU.subtract))
        act.wait(t_var)
        t_l = act.done(nc.scalar.activation(out=ln_r[:, tc, :], in_=ln_v[:, s_, :], func=AF.Ln, bias=LN_EPS))
        act.wait(t_l)
        t_r = act.done(nc.scalar.activation(out=ln_r[:, tc, :], in_=ln_r[:, tc, :], func=AF.Exp, scale=-0.5))
        dve.wait(t_r)
        t_n = dve.done(nc.vector.scalar_tensor_tensor(out=ln_n[:, tc, :], in0=PA[:, 2 * pr, :], scalar=-1.0,
                                                      in1=ln_r[:, tc, :], op0=ALU.mult, op1=ALU.mult))
        stats_rd[pr] = [t_m2, t_var, t_n]
        m_free[s_] = [t_l]
        tok_ln[tc] = [t_r, t_n]

    norm_b = {}

    x_rd2 = [None] * 8

    def lnx(tc, ct):
        return (ln_x if tc % 2 == 0 else ln_x2)[:, ct, :]

    def norm_pre(tc):
        sl = slice(tc * 512, (tc + 1) * 512)
        toks = []
        for ct in range(4):
            b = (tc % 2) * 4 + ct
            eng, E = (nc.vector, dve) if ct in (0, 2) else (nc.gpsimd, pool)
            E.wait(tok_ln[tc], x_rd2[b], tok_att_pe, t_stats[3], [conv_prev[c_] for c_ in range(4)])
            t_a = E.done(eng.tensor_tensor(out=lnx(tc, ct), in0=cTl[ct][:, sl], in1=ln_r[:, tc, :], op=ALU.mult))
            E.wait(t_a)
            toks.append(E.done(eng.tensor_tensor(out=lnx(tc, ct), in0=lnx(tc, ct), in1=ln_n[:, tc, :], op=ALU.add)))
        norm_b[tc] = toks

    def norm_act(tc):
        sl = slice(tc * 512, (tc + 1) * 512)
        t_sil = [None] * 4
        for ct in range(4):
            b = (tc % 2) * 4 + ct
            act.wait(norm_b[tc][ct], tok_att_pe)
            t_sil[ct] = act.done(nc.scalar.activation(out=sT[:, ct, sl], in_=lnx(tc, ct), func=AF.Silu,
                                                      scale=colv[:, LG + ct:LG + ct + 1],
                                                      bias=colv[:, LB + ct:LB + ct + 1]))
            x_rd2[b] = t_sil[ct]
        t_sil_all[tc] = t_sil

    def pw_chunk(tc):
        sl = slice(tc * 512, (tc + 1) * 512)
        toks = []
        for et in range(4):
            a = et % 2
            pe.wait(t_sil_all[tc], wpw_tok, pw_rd[a])
            for ct in range(4):
                ins = nc.tensor.matmul(PB[:, 2 + a, :], lhsT=wpw[:, ct, et * 128:(et + 1) * 128], rhs=sT[:, ct, sl],
                                       start=(ct == 0), stop=(ct == 3))
            t_pw = pe.done(ins)
            dve.wait(t_pw, tok_sga, (ds_tr, 64 * 16))
            t_ya = dve.done(nc.vector.scalar_tensor_tensor(
                out=yT[:, et, sl], in0=PB[:, 2 + a, :], scalar=colv[:, BP + et:BP + et + 1], in1=sga[:, et, sl],
                op0=ALU.add, op1=ALU.mult))
            pw_rd[a] = t_ya
            toks.append(t_ya)
        tok_ya[tc] = toks

    tp_rd = [None, None]
    tp_done = {"pe": None, "ev": None}

    def yb_transposes():
        for pi in range(NT // 2):
            b = pi % 2
            pe.wait(tail_box["yb"], tp_rd[b], o_rd)
            ins = None
            for tl in range(2):
                for hh in range(4):
                    tt = pi * 2 + tl
                    ins = nc.tensor.transpose(PBb[:, b, (tl * 4 + hh) * 128:(tl * 4 + hh + 1) * 128],
                                              yb[:, tt, hh * 128:(hh + 1) * 128], identb[:])
            t_tp = pe.done(ins)
            act.wait(t_tp, t_stats[3])
            t_ev = act.done(nc.scalar.activation(
                out=yT[:, 4:8, pi * 256:(pi + 1) * 256].rearrange("p h (t i) -> p t h i", t=2),
                in_=PBb[:, b, :].rearrange("p (t h i) -> p t h i", t=2, h=4), func=AF.Copy,
                scale=colv[:, 149:150]))
            tp_rd[b] = t_ev
            tp_done["pe"] = t_tp
            tp_done["ev"] = t_ev

    ov = out.rearrange("(t p) d -> t p d", p=128)
    acc_rd = [None, None]
    r_rd = [None, None, None]
    ds_fg = DmaSem(nc, "fg")
    fg_box = [None]

    def p5_tile(t):
        slot = t % 4
        a = t % 2
        rb3 = t % 3
        pe.wait(tok_ya[t // 4], (ds_tr, 64 * 16), tail_box["wout"], acc_rd[a], stats_rd)
        for half in range(2):
            for ft in range(8):
                ins = nc.tensor.matmul(PA[:, 2 * a + half, :], lhsT=yT[:, ft, t * 128:(t + 1) * 128],
                                       rhs=wout[:, ft, half * 512:(half + 1) * 512], start=(ft == 0), stop=(ft == 7))
        t_mm = pe.done(ins)
        dve.wait(t_mm, xr_tok[t], r_rd[rb3], tok_ya[3] if 3 in tok_ya else None)
        t_res = dve.done(nc.vector.tensor_tensor(out=rbuf[:, rb3, :].rearrange("p (a c) -> p a c", a=2),
                                                 in0=PA[:, 2 * a:2 * a + 2, :],
                                                 in1=xs[:, slot, :].rearrange("p (a c) -> p a c", a=2), op=ALU.add))
        acc_rd[a] = t_res
        if t + 4 < NT:
            sp.wait(t_res)
            xr_tok[t + 4] = ds_x[slot].done(nc.sync.dma_start(out=xs[:, slot, :], in_=xv[t + 4]))
        act.wait(t_res)
        t_ss = act.done(nc.scalar.activation(out=junk5[:], in_=rbuf[:, rb3, :], func=AF.Square,
                                             accum_out=small[:, SS5 + t:SS5 + t + 1]))
        act.wait(t_ss)
        t_l = act.done(nc.scalar.activation(out=small[:, LN5 + t:LN5 + t + 1], in_=small[:, SS5 + t:SS5 + t + 1],
                                            func=AF.Ln, scale=1.0 / D, bias=EPS))
        act.wait(t_l)
        t_r = act.done(nc.scalar.activation(out=small[:, RS5 + t:RS5 + t + 1], in_=small[:, LN5 + t:LN5 + t + 1],
                                            func=AF.Exp, scale=-0.5))
        p5_pending.append((t, rb3, t_r, t_res))

    p5_pending = []

    def p5_finish():
        (t, rb3, t_r, t_res) = p5_pending.pop(0)
        dve.wait(t_r, t_res, fg_box[0])
        t_o = dve.done(nc.vector.scalar_tensor_tensor(out=rbuf[:, rb3, :], in0=rbuf[:, rb3, :],
                                                      scalar=small[:, RS5 + t:RS5 + t + 1], in1=fgB[:],
                                                      op0=ALU.mult, op1=ALU.mult))
        sp.wait(t_o)
        r_rd[rb3] = ds_o[rb3].done(nc.sync.dma_start(out=ov[t], in_=rbuf[:, rb3, :]))

    stats_pe(0)
    stats_pe(1)
    stats_chain(0)
    stats_pe(2)
    stats_chain(1)
    stats_pe(3)
    stats_chain(2)
    stats_chain(3)
    attention_tail()
    norm_pre(0)
    norm_pre(1)
    sp.wait(t_stats[3], [conv_prev[c_] for c_ in range(4)])
    fg_box[0] = ds_fg.done(nc.sync.dma_start(out=fgB[:], in_=fin_g.partition_broadcast(128)))
    norm_act(0)
    norm_pre(2)
    pw_chunk(0)
    norm_act(1)
    norm_pre(3)
    pw_chunk(1)
    norm_act(2)
    pw_chunk(2)
    norm_act(3)
    p5_tile(0)
    p5_tile(1)
    p5_finish()
    pw_chunk(3)
    for t in range(2, NT):
        p5_tile(t)
        p5_finish()
    p5_finish()
    return finish()


def _consts():
    idb = np.eye(128, dtype=np.float32).astype(ml_dtypes.bfloat16)
    idf = np.eye(128, dtype=np.float32)
    rot = np.zeros((128, 128), dtype=np.float32)
    for p2 in range(128):
        d = p2 % 64
        if d < 32:
            rot[p2 + 32, p2] = -1.0
        else:
            rot[p2 - 32, p2] = 1.0
    ones = np.full((128, 128), 1.0 / 512, dtype=np.float32).astype(ml_dtypes.bfloat16)
    half = 32
    try:
        import jax
        import jax.numpy as jnp
        with jax.default_device(jax.devices("cpu")[0]):
            inv_j = 1.0 / (10000.0 ** (jnp.arange(half, dtype=jnp.float32) * 2.0 / 64))
            pos_j = jnp.arange(S, dtype=jnp.float32)
            ang_j = pos_j[:, None] * inv_j[None, :]
            cos32 = np.asarray(jnp.cos(ang_j), dtype=np.float32).T
            sin32 = np.asarray(jnp.sin(ang_j), dtype=np.float32).T
    except Exception:
        inv_freq = (1.0 / (10000.0 ** (np.arange(half, dtype=np.float64) * 2.0 / 64.0))).astype(np.float32)
        pos = np.arange(S, dtype=np.float32)
        ang = (pos[None, :] * inv_freq[:, None]).astype(np.float32)
        cos32 = np.cos(ang.astype(np.float64)).astype(np.float32)
        sin32 = np.sin(ang.astype(np.float64)).astype(np.float32)
    idx = (np.arange(128) % 64) % 32
    cosT = cos32[idx].astype(np.float32)
    sinT = sin32[idx].astype(np.float32)
    return {"c_identb": idb, "c_identf": idf, "c_rot": rot.astype(ml_dtypes.bfloat16), "c_ones": ones,
            "c_cos": np.ascontiguousarray(cosT), "c_sin": np.ascontiguousarray(sinT)}


_NC_CACHE = {}


def kernel(x, norm_g, w_in, conv_w, conv_b, conv_ln_g, conv_ln_b, w_pw, b_pw, lambda_q1, lambda_k1,
           lambda_q2, lambda_k2, head_norm_g, w_out, final_norm_g, _debug=False):
    f = lambda a: np.ascontiguousarray(np.asarray(a, dtype=np.float32))
    shared = {
        "norm_g": f(norm_g).reshape(8, 128),
        "w_in": f(w_in).reshape(D, 3584),
        "conv_w": f(conv_w).reshape(KW * 4, 128),
        "conv_b": f(conv_b).reshape(4, 128),
        "ln_g": f(conv_ln_g).reshape(4, 128),
        "ln_b": f(conv_ln_b).reshape(4, 128),
        "w_pw": f(w_pw).reshape(C, C),
        "b_pw": f(b_pw).reshape(4, 128),
        "lq1": f(lambda_q1).reshape(1, 64),
        "lk1": f(lambda_k1).reshape(1, 64),
        "lq2": f(lambda_q2).reshape(1, 64),
        "lk2": f(lambda_k2).reshape(1, 64),
        "head_g": f(head_norm_g).reshape(1, 128),
        "w_out": f(w_out).reshape(D, D),
        "fin_g": f(final_norm_g).reshape(1, D),
    }
    shared.update(_consts())
    xf = f(x)
    in_maps = []
    for c in range(NCORES):
        m = dict(shared)
        m["x"] = np.ascontiguousarray(xf[c])
        in_maps.append(m)
    nc = build(debug=_debug)
    res = run_bass_kernel_spmd(nc, in_maps, core_ids=list(range(NCORES)))
    outp = np.stack([np.asarray(res.results[c]["out"]) for c in range(NCORES)], axis=0).astype(np.float32)
    if _debug:
        return outp, res.results
    return outp
```

```python
import numpy as np
import ml_dtypes
import concourse.bass as bass
import concourse.mybir as mybir
from concourse.bass_utils import run_bass_kernel_spmd

F32 = mybir.dt.float32
BF16 = mybir.dt.bfloat16
ALU = mybir.AluOpType
AF = mybir.ActivationFunctionType
AX = mybir.AxisListType

S = 2048
D = 1024
C = 512
NT = S // 128
NCORES = 8
KW = 31
PADL = 16
UW = PADL + S + 16
EPS = 1e-6
LN_EPS = 1e-5
LAM_INIT = 0.2
VW = 130


class Eng:
    def __init__(self, nc, eng, name):
        self.nc, self.e, self.name = nc, eng, name
        self.sem = nc.alloc_semaphore("sem_" + name)
        self.n = 0
        self.seen = {}

    def wait(self, *toks):
        for t in toks:
            if t is None:
                continue
            if isinstance(t, list):
                self.wait(*t)
                continue
            src, c = t
            if self.seen.get(src, 0) >= c:
                continue
            self.e.wait_ge(src.sem, c)
            self.seen[src] = c

    def done(self, ins):
        self.n += 1
        ins.then_inc(self.sem, 1)
        return (self, self.n)


class DmaSem:
    def __init__(self, nc, name):
        self.sem = nc.alloc_semaphore("dsem_" + name)
        self.n = 0
        DmaSem.ALL.append(self)

    def done(self, ins):
        self.n += 16
        ins.then_inc(self.sem, 16)
        return (self, self.n)


def build(debug=False, stop_after=None):
    nc = bass.Bass("TRN2", target_bir_lowering=False)
    DmaSem.ALL = []

    def din(name, shape, dt=F32):
        return nc.dram_tensor(name, shape, dt, kind="ExternalInput").ap()

    x = din("x", [S, D])
    norm_g = din("norm_g", [8, 128])
    w_in = din("w_in", [D, 3584])
    conv_w = din("conv_w", [KW * 4, 128])
    conv_b = din("conv_b", [4, 128])
    ln_g = din("ln_g", [4, 128])
    ln_b = din("ln_b", [4, 128])
    w_pw = din("w_pw", [C, C])
    b_pw = din("b_pw", [4, 128])
    lq1 = din("lq1", [1, 64])
    lk1 = din("lk1", [1, 64])
    lq2 = din("lq2", [1, 64])
    lk2 = din("lk2", [1, 64])
    head_g = din("head_g", [1, 128])
    w_out = din("w_out", [D, D])
    fin_g = din("fin_g", [1, D])
    c_identb = din("c_identb", [128, 128], BF16)
    c_identf = din("c_identf", [128, 128])
    c_rot = din("c_rot", [128, 128], BF16)
    c_ones = din("c_ones", [128, 128], BF16)
    c_cos = din("c_cos", [128, S])
    c_sin = din("c_sin", [128, S])
    out = nc.dram_tensor("out", [S, D], F32, kind="ExternalOutput").ap()
    dbg = {}

    off = [16512]

    def sb(name, shape, dt, at=None):
        nbytes = int(np.prod(shape[1:])) * (2 if dt == BF16 else 4)
        if at is None:
            at = off[0]
            off[0] += (nbytes + 63) // 64 * 64
        return nc.alloc_sbuf_tensor_at(name, shape, dt, offset=at), at

    identb, _ = sb("identb", [128, 128], BF16)
    identf, _ = sb("identf", [128, 128], F32)
    rotm, _ = sb("rotm", [128, 128], BF16)
    onesm, _ = sb("onesm", [128, 128], BF16)
    rows_c, _ = sb("rows_c", [8, 128], F32)
    colv, _ = sb("colv", [128, 160], F32)
    lam4, _ = sb("lam4", [128, 4, 64], F32)
    small, _ = sb("small", [128, 256], F32)
    hgB, _ = sb("hgB", [128, 4, 128], F32)
    cT0, _ = sb("cT0", [128, S], F32)
    cwh, _ = sb("cwh", [128, 4, 32], F32)
    wpw, _ = sb("wpw", [128, 4, C], BF16)
    hT, hT_at = sb("hT", [128, 8, S], BF16)
    xs, xs_at = sb("xs", [128, 4, D], F32)
    uT, uT_at = sb("uT", [128, 4, UW], BF16)
    sga, _ = sb("sga", [128, 4, S], BF16)
    Gt, G_at = sb("Gt", [128, NT, 512], BF16)
    QT, QT_at = sb("QT", [128, 4, S], BF16)
    KT, KT_at = sb("KT", [128, 4, S], BF16)
    Vg, V_at = sb("Vg", [128, NT, 4, VW], BF16)
    wsl, W_at = sb("wsl", [128, 3, 8, 512], BF16)
    rt1, _ = sb("rt1", [128, 2, 512], F32)
    rt2, _ = sb("rt2", [128, 2, 512], F32)
    junk, T_at = sb("junk", [128, D], BF16)
    xnb, _ = sb("xnb", [128, 2, D], BF16)
    thb, _ = sb("thb", [128, 2, 512], F32)
    qbb, _ = sb("qbb", [128, 2, 512], BF16)
    assert off[0] <= 229376, off[0]
    yT = hT
    yb, _ = sb("yb", [128, NT, 512], BF16, at=hT_at)
    cosT, _ = sb("cosT", [128, S], F32, at=xs_at)
    sinT, _ = sb("sinT", [128, S], F32, at=xs_at + 8192)
    Eb, _ = sb("Eb", [128, 4, 2, 512], BF16, at=xs_at)
    at_t, _ = sb("at_t", [128, 2, 4, 128], F32, at=xs_at + 8192)
    at_o, _ = sb("at_o", [128, 2, 4, 128], F32, at=xs_at + 8192 + 4096)
    cTl = [cT0, sb("cT1", [128, S], F32, at=T_at)[0], sb("cT2", [128, S], F32, at=W_at)[0],
           sb("cT3", [128, S], F32, at=W_at + 8192)[0]]
    rows_a, _ = sb("rows_a", [128, 128], F32, at=QT_at)
    rows_b, _ = sb("rows_b", [32, 128], F32, at=QT_at + 512)
    hg1, _ = sb("hg1", [128, 128], F32, at=QT_at + 1024)
    fgB, _ = sb("fgB", [128, D], F32, at=uT_at)
    cb, _ = sb("cb", [128, 4, S], BF16, at=W_at + 16384)
    csql = [sb("csq%d" % i, [128, S], BF16, at=uT_at + i * UW * 2)[0] for i in range(4)]
    sT, _ = sb("sT", [128, 4, S], BF16, at=V_at)
    wout, _ = sb("wout", [128, 8, D], BF16, at=G_at)
    ln_r, _ = sb("ln_r", [128, 4, 512], F32, at=QT_at)
    ln_n, _ = sb("ln_n", [128, 4, 512], F32, at=QT_at + 8192)
    ln_m, _ = sb("ln_m", [128, 2, 512], F32, at=KT_at)
    ln_v, _ = sb("ln_v", [128, 2, 512], F32, at=KT_at + 4096)
    ln_x, _ = sb("ln_x", [128, 4, 512], F32, at=KT_at + 8192)
    ln_x2, _ = sb("ln_x2", [128, 4, 512], F32, at=uT_at + 4096)
    rbuf, _ = sb("rbuf", [128, 3, D], F32, at=W_at + 16384)
    junk5, _ = sb("junk5", [128, D], BF16, at=W_at + 28672)

    PA = nc.alloc_psum_tensor("PA", [128, 4, 512], F32)
    PB = nc.alloc_psum_tensor("PB", [128, 4, 512], F32)
    PBb = PB[:].bitcast(BF16)

    pe = Eng(nc, nc.tensor, "pe")
    act = Eng(nc, nc.scalar, "act")
    dve = Eng(nc, nc.vector, "dve")
    pool = Eng(nc, nc.gpsimd, "pool")
    sp = Eng(nc, nc.sync, "sp")
    ds_const = DmaSem(nc, "const")
    ds_x = [DmaSem(nc, "x%d" % i) for i in range(4)]
    ds_w = [DmaSem(nc, "w%d" % i) for i in range(3)]
    ds_cs = DmaSem(nc, "cs")
    ds_w2 = DmaSem(nc, "wx")
    ds_w3 = DmaSem(nc, "wy")
    ds_o = [DmaSem(nc, "out%d" % i) for i in range(3)]

    def finish():
        for e_ in (pe, act, dve, pool):
            if e_.n:
                nc.sync.wait_ge(e_.sem, e_.n)
        for d_ in DmaSem.ALL:
            if d_.n:
                nc.sync.wait_ge(d_.sem, d_.n)
        return nc

    SS1, RS1 = 0, 16
    LAMC = 40
    ZR = 48
    NZ = 64
    SSA = 72
    LNA = 80
    RSA = 88
    SS5, LN5, RS5 = 96, 112, 128

    ds_c0 = DmaSem(nc, "c0")
    for dst, src in [(identb[:], c_identb), (identf[:], c_identf), (rows_c[0:8, :], norm_g)]:
        tok_c0 = ds_c0.done(nc.scalar.dma_start(out=dst, in_=src))
    xv = x.rearrange("(t p) d -> t p d", p=128)
    x_tok = [None] * NT
    for t in range(4):
        x_tok[t] = ds_x[t].done(nc.sync.dma_start(out=xs[:, t, :], in_=xv[t]))
    tok_c = None
    for dst, src in [(rotm[:], c_rot), (onesm[:], c_ones),
                     (rows_a[0:124, :], conv_w), (rows_b[0:4, :], conv_b), (rows_b[4:8, :], ln_g),
                     (rows_b[8:12, :], ln_b), (rows_b[12:16, :], b_pw), (rows_b[16:17, :], head_g),
                     (lam4[:, 0, :], lq1.partition_broadcast(128)), (lam4[:, 1, :], lk1.partition_broadcast(128)),
                     (lam4[:, 2, :], lq2.partition_broadcast(128)), (lam4[:, 3, :], lk2.partition_broadcast(128)),
                     (hg1[:], head_g.partition_broadcast(128))]:
        tok_c = ds_const.done(nc.sync.dma_start(out=dst, in_=src))

    w_v = w_in.rearrange("(dt p) c -> p dt c", p=128)
    SLABS = [
        [(512, 256), (0, 256)],
        [(768, 256), (256, 256)],
        [(1024, 512)],
        [(1536, 512)],
        [(2048, 512)],
        [(2560, 512)],
        [(3072, 512)],
    ]
    slab_tok = [None] * 7

    def load_slab(si):
        slot = si % 3
        c0 = 0
        tk = None
        for (cs, wd) in SLABS[si]:
            tk = ds_w[slot].done(nc.gpsimd.dma_start(out=wsl[:, slot, :, c0:c0 + wd], in_=w_v[:, :, cs:cs + wd]))
            c0 += wd
        slab_tok[si] = tk

    load_slab(0)
    wpw_tok = ds_w2.done(nc.gpsimd.dma_start(out=wpw[:], in_=w_pw.rearrange("(ct p) e -> p ct e", p=128)))

    pool.done(nc.gpsimd.memset(Vg[:], 1.0))
    tok_vg1 = pool.done(nc.gpsimd.memset(uT[:, :, 0:PADL], 0.0))
    tok_upad = pool.done(nc.gpsimd.memset(uT[:, :, PADL + S:UW], 0.0))

    CB, LG, LB, BP, NG = 124, 128, 132, 136, 140
    pe.wait(tok_c0)
    t_pe = pe.done(nc.tensor.matmul(PA[:, 1, 0:8], lhsT=rows_c[0:8, :], rhs=identf[0:8, 0:8], start=True, stop=True))
    dve.wait(t_pe)
    tok_ng = dve.done(nc.vector.tensor_copy(out=colv[:, NG:NG + 8], in_=PA[:, 1, 0:8]))

    p1 = {"xnb_rd": [None, None], "tp_rd": [None, None], "last": None}

    def emit_p1(t):
        slot = t % 4
        b = t % 2
        act.wait(x_tok[t])
        t_ss = act.done(nc.scalar.activation(out=junk[:], in_=xs[:, slot, :], func=AF.Square,
                                             accum_out=small[:, SS1 + t:SS1 + t + 1]))
        act.wait(t_ss)
        t_ln = act.done(nc.scalar.activation(out=small[:, RS1 + t:RS1 + t + 1], in_=small[:, SS1 + t:SS1 + t + 1],
                                             func=AF.Ln, scale=1.0 / D, bias=EPS))
        act.wait(t_ln)
        t_rs = act.done(nc.scalar.activation(out=small[:, RS1 + t:RS1 + t + 1], in_=small[:, RS1 + t:RS1 + t + 1],
                                             func=AF.Exp, scale=-0.5))
        dve.wait(t_rs, x_tok[t], p1["xnb_rd"][b])
        t_xn = dve.done(nc.vector.tensor_scalar(out=xnb[:, b, :], in0=xs[:, slot, :],
                                                scalar1=small[:, RS1 + t:RS1 + t + 1], scalar2=None, op0=ALU.mult))
        if t + 4 < NT:
            sp.wait(t_xn, t_ss)
            x_tok[t + 4] = ds_x[slot].done(nc.sync.dma_start(out=xs[:, slot, :], in_=xv[t + 4]))
        pe.wait(t_xn, p1["tp_rd"][b])
        for dt in range(8):
            ins = nc.tensor.transpose(PBb[:, 2 + b, dt * 128:(dt + 1) * 128], xnb[:, b, dt * 128:(dt + 1) * 128], identb[:])
        t_tp = pe.done(ins)
        p1["xnb_rd"][b] = t_tp
        return t_tp

    def emit_p1_evac(t, t_tp):
        b = t % 2
        dve.wait(t_tp, tok_ng)
        t_ev = dve.done(nc.vector.tensor_tensor(
            out=hT[:, :, t * 128:(t + 1) * 128],
            in0=PBb[:, 2 + b, :].rearrange("p (dt i) -> p dt i", dt=8),
            in1=colv[:, NG:NG + 8].unsqueeze(2).to_broadcast([128, 8, 128]), op=ALU.mult))
        p1["tp_rd"][b] = t_ev
        p1["last"] = t_ev
        return t_ev

    hT_tok = [None] * NT
    tp_tok = [None] * NT
    for t in range(NT + 1):
        if t < NT:
            tp_tok[t] = emit_p1(t)
        if t >= 1:
            hT_tok[t - 1] = emit_p1_evac(t - 1, tp_tok[t - 1])
        if t == 9:
            pool.wait(x_tok[12])
            load_slab(1)
        if t == 13:
            pool.wait(x_tok[15])
            load_slab(2)
    sp.wait(p1["last"])
    ds_cs.done(nc.sync.dma_start(out=cosT[:], in_=c_cos))
    cs_tok = ds_cs.done(nc.sync.dma_start(out=sinT[:], in_=c_sin))

    pe.wait(tok_c)
    nc.tensor.matmul(PA[:, 0, 0:124], lhsT=rows_a[0:124, :], rhs=identf[0:124, 0:124], start=True, stop=True)
    t_pe = pe.done(nc.tensor.matmul(PA[:, 1, 0:17], lhsT=rows_b[0:17, :], rhs=identf[0:17, 0:17], start=True, stop=True))
    dve.wait(t_pe)
    nc.vector.tensor_copy(out=colv[:, 0:124], in_=PA[:, 0, 0:124])
    nc.vector.tensor_copy(out=colv[:, 124:140], in_=PA[:, 1, 0:16])
    tok_colv = dve.done(nc.vector.tensor_scalar(out=colv[:, 149:150], in0=PA[:, 1, 16:17], scalar1=1.0 - LAM_INIT,
                                                scalar2=None, op0=ALU.mult))
    dve.wait(tok_colv)
    tok_cwh = dve.done(nc.vector.tensor_scalar(
        out=cwh[:, :, 0:KW], in0=colv[:, 0:124].rearrange("p (j ct) -> p ct j", ct=4),
        scalar1=0.5, scalar2=None, op0=ALU.mult))
    dve.wait(tok_c)
    tok_hgB = dve.done(nc.vector.tensor_scalar(
        out=hgB[:], in0=hg1[:].unsqueeze(1).to_broadcast([128, 4, 128]),
        scalar1=1.0 - LAM_INIT, scalar2=None, op0=ALU.mult))
    nc.vector.tensor_tensor(out=lam4[:, 0, :], in0=lam4[:, 0, :], in1=lam4[:, 1, :], op=ALU.mult)
    t1 = dve.done(nc.vector.tensor_tensor(out=lam4[:, 2, :], in0=lam4[:, 2, :], in1=lam4[:, 3, :], op=ALU.mult))
    dve.wait(t1)
    t2 = dve.done(nc.vector.tensor_reduce(out=small[:, LAMC:LAMC + 2],
                                          in_=lam4[:].rearrange("p (a b) d -> p a b d", b=2)[:, :, 0, :],
                                          axis=AX.X, op=ALU.add))
    act.wait(t2)
    t3 = act.done(nc.scalar.activation(out=small[:, LAMC + 2:LAMC + 4], in_=small[:, LAMC:LAMC + 2], func=AF.Exp))
    dve.wait(t3)
    t4 = dve.done(nc.vector.tensor_tensor(out=small[:, LAMC + 4:LAMC + 5], in0=small[:, LAMC + 3:LAMC + 4],
                                          in1=small[:, LAMC + 2:LAMC + 3], op=ALU.subtract))
    dve.wait(t4)
    tok_lam = dve.done(nc.vector.tensor_scalar(out=small[:, LAMC + 5:LAMC + 6], in0=small[:, LAMC + 4:LAMC + 5],
                                               scalar1=-LAM_INIT, scalar2=None, op0=ALU.add))
    neglam = small[:, LAMC + 5:LAMC + 6]
    tok_pro_pe = t_pe
    if stop_after == "prologue":
        return finish()

    st = {"g": 0, "acc_rd": [tok_colv, tok_colv, None, None], "rot_rd": [None, None], "t_rd": [None, None], "qb_rd": [None, None],
          "th_rd": [None, None], "pending_rot": None, "nqk": 0, "nglu": 0, "nbg": 0}
    slab_done = [None] * 7
    tok_uT = [None] * 4
    tok_QK = {}
    tok_V = [None] * NT
    tok_G = [None] * NT
    tok_sga = None

    conv_prev = {}
    conv_gate = {"w": None}

    def conv_step(ct, j):
        src = uT[:, ct, PADL - 15 + j:PADL - 15 + j + S]
        if j == 0:
            gate = {0: [tok_pro_pe],
                    1: [p1["xnb_rd"][0], p1["xnb_rd"][1], p1["last"], st["th_rd"][0], st["th_rd"][1]],
                    2: conv_gate["w"], 3: conv_gate["w"]}[ct]
            dve.wait(tok_uT[ct], tok_upad, tok_cwh, tok_colv, gate)
            ins = nc.vector.tensor_scalar(out=cTl[ct][:], in0=src, scalar1=cwh[:, ct, 0:1],
                                          scalar2=colv[:, CB + ct:CB + ct + 1], op0=ALU.mult, op1=ALU.add)
        else:
            dve.wait(conv_prev[ct])
            ins = nc.vector.scalar_tensor_tensor(out=cTl[ct][:], in0=src, scalar=cwh[:, ct, j:j + 1],
                                                 in1=cTl[ct][:], op0=ALU.mult, op1=ALU.add)
        conv_prev[ct] = dve.done(ins)

    conv_q = {"early": [(0, j) for j in range(KW)] + [(1, j) for j in range(KW)],
              "late": [(2, j) for j in range(KW)] + [(3, j) for j in range(KW)]}
    conv_i = {"early": 0, "late": 0}
    tok_cb = [None] * 4
    tok_cbp = {}

    def emit_conv(which, n):
        for _ in range(n):
            i = conv_i[which]
            if i >= len(conv_q[which]):
                return
            conv_i[which] = i + 1
            ct, j = conv_q[which][i]
            conv_step(ct, j)
            if j == KW - 1:
                for c_ in range(4):
                    cb_queue.append((ct, c_))

    cb_queue = []

    def emit_cb(n):
        for _ in range(n):
            if not cb_queue:
                return
            ct, c_ = cb_queue.pop(0)
            sl = slice(c_ * 512, (c_ + 1) * 512)
            pool.wait(conv_prev[ct], tok_p2_pe_box[0])
            nc.gpsimd.tensor_copy(out=cb[:, ct, sl], in_=cTl[ct][:, sl])
            tok_cb[ct] = tok_cbp[(ct, c_)] = pool.done(nc.gpsimd.tensor_tensor(out=csql[ct][:, sl], in0=cTl[ct][:, sl],
                                                           in1=cTl[ct][:, sl], op=ALU.mult))

    tok_p2_pe_box = [None]

    def fm_group(slot, ct, tc):
        a = st["g"] % 4
        st["g"] += 1
        pe.wait(st["acc_rd"][a], [hT_tok[tc * 4 + i] for i in range(4)])
        for dt in range(8):
            ins = nc.tensor.matmul(PA[:, a, :], lhsT=wsl[:, slot, dt, ct * 128:(ct + 1) * 128],
                                   rhs=hT[:, dt, tc * 512:(tc + 1) * 512], start=(dt == 0), stop=(dt == 7))
        return a, pe.done(ins)

    def tm_group(slot, tt):
        a = st["g"] % 4
        st["g"] += 1
        pe.wait(st["acc_rd"][a], hT_tok[tt])
        for dt in range(8):
            ins = nc.tensor.matmul(PA[:, a, :], lhsT=hT[:, dt, tt * 128:(tt + 1) * 128],
                                   rhs=wsl[:, slot, dt, :], start=(dt == 0), stop=(dt == 7))
        return a, pe.done(ins)

    def flush_rot():
        pr = st["pending_rot"]
        if pr is None:
            return
        st["pending_rot"] = None
        (dst, ct, tc, b, t_qb, t_t1) = pr
        r = b
        pe.wait(t_qb, st["rot_rd"][r])
        t_rot = pe.done(nc.tensor.matmul(PB[:, r, :], lhsT=rotm[:], rhs=qbb[:, b, :], start=True, stop=True))
        st["qb_rd"][b] = t_rot
        dve.wait(t_rot)
        t_t2 = dve.done(nc.vector.tensor_tensor(out=rt2[:, b, :], in0=PB[:, r, :],
                                                in1=sinT[:, tc * 512:(tc + 1) * 512], op=ALU.mult))
        st["rot_rd"][r] = t_t2
        pool.wait(t_t2, t_t1, tok_hgB, tok_pro_pe)
        if dst is QT:
            o_ap = QT[:, ct, :].rearrange("p (i n) -> p n i", n=NT)[:, 4 * tc:4 * tc + 4, :]
            i0 = rt1[:, b, :].rearrange("p (n i) -> p n i", n=4)
            i1 = rt2[:, b, :].rearrange("p (n i) -> p n i", n=4)
        else:
            o_ap = dst[:, ct, tc * 512:(tc + 1) * 512]
            i0, i1 = rt1[:, b, :], rt2[:, b, :]
        t_add = pool.done(nc.gpsimd.tensor_tensor(out=o_ap, in0=i0, in1=i1, op=ALU.add))
        st["t_rd"][b] = t_add
        tok_QK[(id(dst), ct, tc)] = t_add

    EARLY_TAPS = {1: 12, 2: 11, 3: 2, 4: 5, 5: 11, 6: 11}
    for si in range(7):
        slot = si % 3
        pe.wait(slab_tok[si])
        last_pe = None
        ngrp = [0]

        def after_group():
            ngrp[0] += 1
            n_t = EARLY_TAPS.get(si, 0)
            if n_t and (ngrp[0] * n_t) // 16 > ((ngrp[0] - 1) * n_t) // 16:
                emit_conv("early", 1)

        if si < 2:
            for tc in range(4):
                for cl in range(2):
                    ct = si * 2 + cl
                    a, t_mm = fm_group(slot, cl, tc)
                    b = st["nglu"] % 2
                    st["nglu"] += 1
                    act.wait(t_mm, st["th_rd"][b])
                    t_th = act.done(nc.scalar.activation(out=thb[:, b, :], in_=PA[:, a, :], func=AF.Tanh, scale=0.5))
                    st["acc_rd"][a] = t_th
                    a2, t_mm2 = fm_group(slot, 2 + cl, tc)
                    dve.wait(t_mm2, t_th)
                    t_u = dve.done(nc.vector.scalar_tensor_tensor(
                        out=uT[:, ct, PADL + tc * 512:PADL + (tc + 1) * 512], in0=thb[:, b, :], scalar=1.0,
                        in1=PA[:, a2, :], op0=ALU.add, op1=ALU.mult))
                    st["acc_rd"][a2] = t_u
                    st["th_rd"][b] = t_u
                    tok_uT[ct] = t_u
                    last_pe = t_mm2
                if si == 1:
                    for _ in range(4):
                        after_group()
        elif si == 2:
            for tc in range(4):
                for ct in range(4):
                    a, t_mm = fm_group(slot, ct, tc)
                    act.wait(t_mm)
                    t_s = act.done(nc.scalar.activation(out=sga[:, ct, tc * 512:(tc + 1) * 512], in_=PA[:, a, :],
                                                        func=AF.Silu))
                    st["acc_rd"][a] = t_s
                    tok_sga = t_s
                    last_pe = t_mm
                    after_group()
        elif si in (3, 4):
            dst = QT if si == 3 else KT
            for tc in range(4):
                for ct in range(4):
                    a, t_mm = fm_group(slot, ct, tc)
                    flush_rot()
                    b = st["nqk"] % 2
                    st["nqk"] += 1
                    act.wait(t_mm, st["qb_rd"][b])
                    t_qb = act.done(nc.scalar.activation(out=qbb[:, b, :], in_=PA[:, a, :], func=AF.Copy))
                    dve.wait(t_mm, t_qb, cs_tok, st["t_rd"][b])
                    t_t1 = dve.done(nc.vector.tensor_tensor(out=rt1[:, b, :], in0=PA[:, a, :],
                                                            in1=cosT[:, tc * 512:(tc + 1) * 512], op=ALU.mult))
                    st["acc_rd"][a] = [t_qb, t_t1]
                    st["pending_rot"] = (dst, ct, tc, b, t_qb, t_t1)
                    last_pe = t_mm
                    after_group()
        elif si == 5:
            flush_rot()
            for tt in range(NT):
                a, t_mm = tm_group(slot, tt)
                act.wait(t_mm, tok_vg1)
                t_v = act.done(nc.scalar.activation(out=Vg[:, tt, :, 0:128],
                                                    in_=PA[:, a, :].rearrange("p (h e) -> p h e", h=4), func=AF.Copy))
                st["acc_rd"][a] = t_v
                tok_V[tt] = t_v
                last_pe = t_mm
                after_group()
        else:
            for tt in range(NT):
                a, t_mm = tm_group(slot, tt)
                b = st["nbg"] % 4
                st["nbg"] += 1
                if "g_rd" not in st:
                    st["g_rd"] = [st["t_rd"][0], st["t_rd"][1], st["t_rd"][0], st["t_rd"][1]]
                gbuf = (rt1 if b < 2 else rt2)[:, b % 2, :]
                act.wait(t_mm, st["g_rd"][b])
                t_s = act.done(nc.scalar.activation(out=gbuf, in_=PA[:, a, :], func=AF.Silu))
                st["acc_rd"][a] = t_s
                dve.wait(t_s, tok_hgB)
                t_g = dve.done(nc.vector.tensor_tensor(out=Gt[:, tt, :], in0=gbuf,
                                                       in1=hgB[:].rearrange("p h e -> p (h e)"), op=ALU.mult))
                st["g_rd"][b] = t_g
                tok_G[tt] = t_g
                last_pe = t_mm
                after_group()
        slab_done[si] = last_pe
        if si + 3 < 7:
            pool.wait(last_pe)
            load_slab(si + 3)
        if stop_after == "p2s%d" % si:
            return finish()
    tok_p2_pe = slab_done[6]
    tok_p2_pe_box[0] = tok_p2_pe
    conv_gate["w"] = [tok_p2_pe]
    if stop_after == "p2":
        return finish()

    s_rd = [None, None]
    e_rd = [None, None, None, None]
    o_rd = None
    tmp_rd = [None, None]
    pend_pv = None
    rounds = [(h, qc) for h in range(4) for qc in range(4)]

    def emit_pv(h, r, p, eb, t_e):
        st_ = r % 2
        fw = [tok_V[2 * p], tok_V[2 * p + 1]]
        if p == 0:
            fw.append(o_rd[st_])
        pe.wait(t_e, fw)
        ins = None
        for kp in range(2):
            kt = 2 * p + kp
            for j in range(2):
                for q2 in range(2):
                    ins = nc.tensor.matmul(PB[:, 2 * st_ + q2, j * 256:j * 256 + 129],
                                           lhsT=Eb[:, eb, j, kp * 256 + q2 * 128:kp * 256 + (q2 + 1) * 128],
                                           rhs=Vg[:, kt, h, 0:129], start=(kt == 0 and j == 0), stop=(kt == NT - 1),
                                           skip_group_check=True)
        return pe.done(ins)

    def act_epilogue(rb, t_ss):
        act.wait(t_ss)
        t_l = act.done(nc.scalar.activation(out=small[:, LNA + 4 * rb:LNA + 4 * rb + 4],
                                            in_=small[:, SSA + 4 * rb:SSA + 4 * rb + 4], func=AF.Ln,
                                            scale=1.0 / 128, bias=EPS))
        act.wait(t_l)
        return act.done(nc.scalar.activation(out=small[:, RSA + 4 * rb:RSA + 4 * rb + 4],
                                             in_=small[:, LNA + 4 * rb:LNA + 4 * rb + 4], func=AF.Exp, scale=-0.5))

    def pool_tail(prb, ph, pqc, t_r):
        pool.wait(t_r)
        t_y = pool.done(nc.gpsimd.tensor_tensor(
            out=at_o[:, prb, :, :], in0=at_o[:, prb, :, :],
            in1=small[:, RSA + 4 * prb:RSA + 4 * prb + 4].unsqueeze(2).to_broadcast([128, 4, 128]), op=ALU.mult))
        pool.wait(t_y, [tok_G[pqc * 4 + i] for i in range(4)])
        return pool.done(nc.gpsimd.tensor_tensor(
            out=yb[:, pqc * 4:(pqc + 1) * 4, ph * 128:(ph + 1) * 128], in0=at_o[:, prb, :, :],
            in1=Gt[:, pqc * 4:(pqc + 1) * 4, ph * 128:(ph + 1) * 128], op=ALU.mult))

    ds_tr = DmaSem(nc, "tr")

    def ybT_dma(ph, pqc, t_yb):
        sp.wait(t_yb, tok_p2_pe)
        for i in range(4):
            tt = pqc * 4 + i
            ds_tr.done(nc.sync.dma_start_transpose(out=yT[:, 4 + ph, tt * 128:(tt + 1) * 128],
                                                   in_=yb[:, tt, ph * 128:(ph + 1) * 128]))

    pend_epi = None
    pe.wait(tok_p2_pe, st["acc_rd"], st["rot_rd"], p1["tp_rd"])
    act.wait(st["t_rd"][0], st["t_rd"][1])
    dve.wait(st["t_rd"][0], st["t_rd"][1])
    o_rd = [None, None]
    NP = NT // 2
    iters = [(r, r // 8, r % 8, p) for r in range(32) for p in range(NP)]
    s_tok = {}

    def emit_scores(g):
        (r_, h_, q8_, p_) = iters[g]
        sb_i = g % 2
        pe.wait(s_rd[sb_i], [tok_QK[(id(QT), h_, tcq)] for tcq in range(4)], tok_QK[(id(KT), h_, p_ // 2)])
        ins = None
        for kp in range(2):
            kt_ = 2 * p_ + kp
            for j in range(2):
                ins = nc.tensor.matmul(PA[:, 2 * sb_i + j, kp * 256:(kp + 1) * 256],
                                       lhsT=KT[64 * j:64 * j + 64, h_, kt_ * 128:(kt_ + 1) * 128],
                                       rhs=QT[64 * j:64 * j + 64, h_, q8_ * 256:(q8_ + 1) * 256], start=True, stop=True,
                                       skip_group_check=True)
        s_tok[g] = pe.done(ins)

    def round_drain(r_, t_pv_last_):
        nonlocal pend_epi
        st_ = r_ % 2
        sr = r_ // 2
        rb = sr % 2
        hh = r_ % 2
        Ov = PB[:, 2 * st_:2 * st_ + 2, :].rearrange("p b (j c) -> p b j c", j=2)
        dve.wait(t_pv_last_, tmp_rd[rb] if hh == 0 else None, tok_lam)
        zr = small[:, ZR + 8 * rb + 4 * hh:ZR + 8 * rb + 4 * hh + 4].rearrange("p (b j) -> p b j", j=2)
        t_z = dve.done(nc.vector.reciprocal(out=zr, in_=Ov[:, :, :, 128]))
        dve.wait(t_z)
        nzc = small[:, NZ + 4 * rb + 2 * hh:NZ + 4 * rb + 2 * hh + 2]
        t_nz = dve.done(nc.vector.tensor_tensor(out=nzc, in0=zr[:, :, 1], in1=neglam.to_broadcast([128, 2]),
                                                op=ALU.mult))
        t_o1 = dve.done(nc.vector.tensor_tensor(
            out=at_o[:, rb, 2 * hh:2 * hh + 2, :], in0=Ov[:, :, 0, 0:128],
            in1=zr[:, :, 0].unsqueeze(2).to_broadcast([128, 2, 128]), op=ALU.mult))
        dve.wait(t_nz, t_o1)
        for q2 in range(2):
            t_o = dve.done(nc.vector.scalar_tensor_tensor(
                out=at_o[:, rb, 2 * hh + q2, :], in0=Ov[:, q2, 1, 0:128], scalar=nzc[:, q2:q2 + 1],
                in1=at_o[:, rb, 2 * hh + q2, :], op0=ALU.mult, op1=ALU.add))
        o_rd[st_] = t_o
        if hh == 1:
            pool.wait(t_o, tmp_rd[rb])
            t_sq = pool.done(nc.gpsimd.tensor_tensor(out=at_t[:, rb, :, :], in0=at_o[:, rb, :, :],
                                                     in1=at_o[:, rb, :, :], op=ALU.mult))
            pend_epi = (rb, r_ // 8, (r_ % 8) // 2, t_sq)

    t_pv_last = None
    emit_scores(0)
    for g, (r, h, q8, p) in enumerate(iters):
        eb = g % 4
        if g + 1 < len(iters):
            emit_scores(g + 1)
        if pend_pv is not None:
            (pr_, ph_, pp_, peb, pt_e) = pend_pv
            e_rd[peb] = emit_pv(ph_, pr_, pp_, peb, pt_e)
            if pp_ == NP - 1:
                round_drain(pr_, e_rd[peb])
        act.wait(s_tok[g], e_rd[eb])
        t_e = act.done(nc.scalar.activation(out=Eb[:, eb, :, :], in_=PA[:, 2 * (g % 2):2 * (g % 2) + 2, :],
                                            func=AF.Exp, scale=0.125))
        s_rd[g % 2] = t_e
        pend_pv = (r, h, p, eb, t_e)
        kk = (r % 2) * NP + p
        if kk == 7 and pend_epi is not None and len(pend_epi) == 4:
            (prb, ph, pqc, t_sq) = pend_epi
            dve.wait(t_sq)
            t_ss = dve.done(nc.vector.tensor_reduce(out=small[:, SSA + 4 * prb:SSA + 4 * prb + 4],
                                                    in_=at_t[:, prb, :, :], axis=AX.X, op=ALU.add))
            pend_epi = (prb, ph, pqc, t_sq, t_ss)
        if kk == 11 and pend_epi is not None and len(pend_epi) == 5:
            (prb, ph, pqc, t_sq, t_ss) = pend_epi
            pend_epi = None
            t_r = act_epilogue(prb, t_ss)
            tmp_rd[prb] = pool_tail(prb, ph, pqc, t_r)
            ybT_dma(ph, pqc, tmp_rd[prb])
        if kk in ((1, 3, 5, 7, 10) if (r // 2) % 2 == 0 else (2, 5, 9, 12)):
            if conv_i["early"] < len(conv_q["early"]):
                emit_conv("early", 1)
            else:
                emit_conv("late", 1)
        if kk == 14:
            emit_cb(2 if r >= 24 else 1)
    (pr_, ph_, pp_, peb, pt_e) = pend_pv
    e_rd[peb] = emit_pv(ph_, pr_, pp_, peb, pt_e)
    t_pv_last = e_rd[peb]
    round_drain(pr_, t_pv_last)
    tok_att_pe = t_pv_last
    emit_conv("early", 2 * KW)
    emit_conv("late", 2 * KW)
    emit_cb(16)
    tail_box = {}
    xr_tok = [None] * NT

    def attention_tail():
        (prb, ph, pqc, t_sq) = pend_epi
        dve.wait(t_sq)
        t_ss = dve.done(nc.vector.tensor_reduce(out=small[:, SSA + 4 * prb:SSA + 4 * prb + 4],
                                                in_=at_t[:, prb, :, :], axis=AX.X, op=ALU.add))
        t_r = act_epilogue(prb, t_ss)
        t_yb = pool_tail(prb, ph, pqc, t_r)
        tail_box["yb"] = t_yb
        ybT_dma(ph, pqc, t_yb)
        sp.wait(t_yb, tok_att_pe, t_ss)
        for t in range(4):
            xr_tok[t] = ds_x[t].done(nc.sync.dma_start(out=xs[:, t, :], in_=xv[t]))
        pool.wait(t_yb)
        tail_box["wout"] = ds_w3.done(nc.gpsimd.dma_start(out=wout[:], in_=w_out.rearrange("(ft p) d -> p ft d", p=128)))

    m_free = [None, None]
    x_rd = [None] * 4
    pw_rd = [None, None]
    stats_rd = [None, None]
    t_stats = {}
    tok_ln = {}
    t_sil_all = {}
    tok_ya = {}
    nx = [0]

    def stats_pe(tc):
        pr = tc % 2
        sl = slice(tc * 512, (tc + 1) * 512)
        pe.wait([tok_cbp[(c_, tc)] for c_ in range(4)], stats_rd[pr], tok_att_pe, s_rd)
        for ct in range(4):
            nc.tensor.matmul(PA[:, 2 * pr, :], lhsT=onesm[:], rhs=cb[:, ct, sl], start=(ct == 0), stop=(ct == 3))
        for ct in range(4):
            ins = nc.tensor.matmul(PA[:, 2 * pr + 1, :], lhsT=onesm[:], rhs=csql[ct][:, sl], start=(ct == 0),
                                   stop=(ct == 3))
        t_stats[tc] = pe.done(ins)

    def stats_chain(tc):
        pr = tc % 2
        s_ = tc % 2
        t_st = t_stats[tc]
        act.wait(t_st, m_free[s_], tok_att_pe)
        t_m2 = act.done(nc.scalar.activation(out=ln_v[:, s_, :], in_=PA[:, 2 * pr, :], func=AF.Square))
        dve.wait(t_m2, t_st)
        t_var = dve.done(nc.vector.tensor_tensor(out=ln_v[:, s_, :], in0=PA[:, 2 * pr + 1, :], in1=ln_v[:, s_, :],
                                                 op=ALU.subtract))
        act.wait(t_var)
        t_l = act.done(nc.scalar.activation(out=ln_r[:, tc, :], in_=ln_v[:, s_, :], func=AF.Ln, bias=LN_EPS))
        act.wait(t_l)
        t_r = act.done(nc.scalar.activation(out=ln_r[:, tc, :], in_=ln_r[:, tc, :], func=AF.Exp, scale=-0.5))
        dve.wait(t_r)
        t_n = dve.done(nc.vector.scalar_tensor_tensor(out=ln_n[:, tc, :], in0=PA[:, 2 * pr, :], scalar=-1.0,
                                                      in1=ln_r[:, tc, :], op0=ALU.mult, op1=ALU.mult))
        stats_rd[pr] = [t_m2, t_var, t_n]
        m_free[s_] = [t_l]
        tok_ln[tc] = [t_r, t_n]

    norm_b = {}

    x_rd2 = [None] * 8

    def lnx(tc, ct):
        return (ln_x if tc % 2 == 0 else ln_x2)[:, ct, :]

    def norm_pre(tc):
        sl = slice(tc * 512, (tc + 1) * 512)
        toks = []
        for ct in range(4):
            b = (tc % 2) * 4 + ct
            eng, E = (nc.vector, dve) if ct in (0, 2) else (nc.gpsimd, pool)
            E.wait(tok_ln[tc], x_rd2[b], tok_att_pe, t_stats[3], [conv_prev[c_] for c_ in range(4)])
            t_a = E.done(eng.tensor_tensor(out=lnx(tc, ct), in0=cTl[ct][:, sl], in1=ln_r[:, tc, :], op=ALU.mult))
            E.wait(t_a)
            toks.append(E.done(eng.tensor_tensor(out=lnx(tc, ct), in0=lnx(tc, ct), in1=ln_n[:, tc, :], op=ALU.add)))
        norm_b[tc] = toks

    def norm_act(tc):
        sl = slice(tc * 512, (tc + 1) * 512)
        t_sil = [None] * 4
        for ct in range(4):
            b = (tc % 2) * 4 + ct
            act.wait(norm_b[tc][ct], tok_att_pe)
            t_sil[ct] = act.done(nc.scalar.activation(out=sT[:, ct, sl], in_=lnx(tc, ct), func=AF.Silu,
                                                      scale=colv[:, LG + ct:LG + ct + 1],
                                                      bias=colv[:, LB + ct:LB + ct + 1]))
            x_rd2[b] = t_sil[ct]
        t_sil_all[tc] = t_sil

    def pw_chunk(tc):
        sl = slice(tc * 512, (tc + 1) * 512)
        toks = []
        for et in range(4):
            a = et % 2
            pe.wait(t_sil_all[tc], wpw_tok, pw_rd[a])
            for ct in range(4):
                ins = nc.tensor.matmul(PB[:, 2 + a, :], lhsT=wpw[:, ct, et * 128:(et + 1) * 128], rhs=sT[:, ct, sl],
                                       start=(ct == 0), stop=(ct == 3))
            t_pw = pe.done(ins)
            dve.wait(t_pw, tok_sga, (ds_tr, 64 * 16))
            t_ya = dve.done(nc.vector.scalar_tensor_tensor(
                out=yT[:, et, sl], in0=PB[:, 2 + a, :], scalar=colv[:, BP + et:BP + et + 1], in1=sga[:, et, sl],
                op0=ALU.add, op1=ALU.mult))
            pw_rd[a] = t_ya
            toks.append(t_ya)
        tok_ya[tc] = toks

    tp_rd = [None, None]
    tp_done = {"pe": None, "ev": None}

    def yb_transposes():
        for pi in range(NT // 2):
            b = pi % 2
            pe.wait(tail_box["yb"], tp_rd[b], o_rd)
            ins = None
            for tl in range(2):
                for hh in range(4):
                    tt = pi * 2 + tl
                    ins = nc.tensor.transpose(PBb[:, b, (tl * 4 + hh) * 128:(tl * 4 + hh + 1) * 128],
                                              yb[:, tt, hh * 128:(hh + 1) * 128], identb[:])
            t_tp = pe.done(ins)
            act.wait(t_tp, t_stats[3])
            t_ev = act.done(nc.scalar.activation(
                out=yT[:, 4:8, pi * 256:(pi + 1) * 256].rearrange("p h (t i) -> p t h i", t=2),
                in_=PBb[:, b, :].rearrange("p (t h i) -> p t h i", t=2, h=4), func=AF.Copy,
                scale=colv[:, 149:150]))
            tp_rd[b] = t_ev
            tp_done["pe"] = t_tp
            tp_done["ev"] = t_ev

    ov = out.rearrange("(t p) d -> t p d", p=128)
    acc_rd = [None, None]
    r_rd = [None, None, None]
    ds_fg = DmaSem(nc, "fg")
    fg_box = [None]

    def p5_tile(t):
        slot = t % 4
        a = t % 2
        rb3 = t % 3
        pe.wait(tok_ya[t // 4], (ds_tr, 64 * 16), tail_box["wout"], acc_rd[a], stats_rd)
        for half in range(2):
            for ft in range(8):
                ins = nc.tensor.matmul(PA[:, 2 * a + half, :], lhsT=yT[:, ft, t * 128:(t + 1) * 128],
                                       rhs=wout[:, ft, half * 512:(half + 1) * 512], start=(ft == 0), stop=(ft == 7))
        t_mm = pe.done(ins)
        dve.wait(t_mm, xr_tok[t], r_rd[rb3], tok_ya[3] if 3 in tok_ya else None)
        t_res = dve.done(nc.vector.tensor_tensor(out=rbuf[:, rb3, :].rearrange("p (a c) -> p a c", a=2),
                                                 in0=PA[:, 2 * a:2 * a + 2, :],
                                                 in1=xs[:, slot, :].rearrange("p (a c) -> p a c", a=2), op=ALU.add))
        acc_rd[a] = t_res
        if t + 4 < NT:
            sp.wait(t_res)
            xr_tok[t + 4] = ds_x[slot].done(nc.sync.dma_start(out=xs[:, slot, :], in_=xv[t + 4]))
        act.wait(t_res)
        t_ss = act.done(nc.scalar.activation(out=junk5[:], in_=rbuf[:, rb3, :], func=AF.Square,
                                             accum_out=small[:, SS5 + t:SS5 + t + 1]))
        act.wait(t_ss)
        t_l = act.done(nc.scalar.activation(out=small[:, LN5 + t:LN5 + t + 1], in_=small[:, SS5 + t:SS5 + t + 1],
                                            func=AF.Ln, scale=1.0 / D, bias=EPS))
        act.wait(t_l)
        t_r = act.done(nc.scalar.activation(out=small[:, RS5 + t:RS5 + t + 1], in_=small[:, LN5 + t:LN5 + t + 1],
                                            func=AF.Exp, scale=-0.5))
        p5_pending.append((t, rb3, t_r, t_res))

    p5_pending = []

    def p5_finish():
        (t, rb3, t_r, t_res) = p5_pending.pop(0)
        dve.wait(t_r, t_res, fg_box[0])
        t_o = dve.done(nc.vector.scalar_tensor_tensor(out=rbuf[:, rb3, :], in0=rbuf[:, rb3, :],
                                                      scalar=small[:, RS5 + t:RS5 + t + 1], in1=fgB[:],
                                                      op0=ALU.mult, op1=ALU.mult))
        sp.wait(t_o)
        r_rd[rb3] = ds_o[rb3].done(nc.sync.dma_start(out=ov[t], in_=rbuf[:, rb3, :]))

    stats_pe(0)
    stats_pe(1)
    stats_chain(0)
    stats_pe(2)
    stats_chain(1)
    stats_pe(3)
    stats_chain(2)
    stats_chain(3)
    attention_tail()
    norm_pre(0)
    norm_pre(1)
    sp.wait(t_stats[3], [conv_prev[c_] for c_ in range(4)])
    fg_box[0] = ds_fg.done(nc.sync.dma_start(out=fgB[:], in_=fin_g.partition_broadcast(128)))
    norm_act(0)
    norm_pre(2)
    pw_chunk(0)
    norm_act(1)
    norm_pre(3)
    pw_chunk(1)
    norm_act(2)
    pw_chunk(2)
    norm_act(3)
    p5_tile(0)
    p5_tile(1)
    p5_finish()
    pw_chunk(3)
    for t in range(2, NT):
        p5_tile(t)
        p5_finish()
    p5_finish()
    return finish()


def _consts():
    idb = np.eye(128, dtype=np.float32).astype(ml_dtypes.bfloat16)
    idf = np.eye(128, dtype=np.float32)
    rot = np.zeros((128, 128), dtype=np.float32)
    for p2 in range(128):
        d = p2 % 64
        if d < 32:
            rot[p2 + 32, p2] = -1.0
        else:
            rot[p2 - 32, p2] = 1.0
    ones = np.full((128, 128), 1.0 / 512, dtype=np.float32).astype(ml_dtypes.bfloat16)
    half = 32
    try:
        import jax
        import jax.numpy as jnp
        with jax.default_device(jax.devices("cpu")[0]):
            inv_j = 1.0 / (10000.0 ** (jnp.arange(half, dtype=jnp.float32) * 2.0 / 64))
            pos_j = jnp.arange(S, dtype=jnp.float32)
            ang_j = pos_j[:, None] * inv_j[None, :]
            cos32 = np.asarray(jnp.cos(ang_j), dtype=np.float32).T
            sin32 = np.asarray(jnp.sin(ang_j), dtype=np.float32).T
    except Exception:
        inv_freq = (1.0 / (10000.0 ** (np.arange(half, dtype=np.float64) * 2.0 / 64.0))).astype(np.float32)
        pos = np.arange(S, dtype=np.float32)
        ang = (pos[None, :] * inv_freq[:, None]).astype(np.float32)
        cos32 = np.cos(ang.astype(np.float64)).astype(np.float32)
        sin32 = np.sin(ang.astype(np.float64)).astype(np.float32)
    idx = (np.arange(128) % 64) % 32
    cosT = cos32[idx].astype(np.float32)
    sinT = sin32[idx].astype(np.float32)
    return {"c_identb": idb, "c_identf": idf, "c_rot": rot.astype(ml_dtypes.bfloat16), "c_ones": ones,
            "c_cos": np.ascontiguousarray(cosT), "c_sin": np.ascontiguousarray(sinT)}


_NC_CACHE = {}


def kernel(x, norm_g, w_in, conv_w, conv_b, conv_ln_g, conv_ln_b, w_pw, b_pw, lambda_q1, lambda_k1,
           lambda_q2, lambda_k2, head_norm_g, w_out, final_norm_g, _debug=False):
    f = lambda a: np.ascontiguousarray(np.asarray(a, dtype=np.float32))
    shared = {
        "norm_g": f(norm_g).reshape(8, 128),
        "w_in": f(w_in).reshape(D, 3584),
        "conv_w": f(conv_w).reshape(KW * 4, 128),
        "conv_b": f(conv_b).reshape(4, 128),
        "ln_g": f(conv_ln_g).reshape(4, 128),
        "ln_b": f(conv_ln_b).reshape(4, 128),
        "w_pw": f(w_pw).reshape(C, C),
        "b_pw": f(b_pw).reshape(4, 128),
        "lq1": f(lambda_q1).reshape(1, 64),
        "lk1": f(lambda_k1).reshape(1, 64),
        "lq2": f(lambda_q2).reshape(1, 64),
        "lk2": f(lambda_k2).reshape(1, 64),
        "head_g": f(head_norm_g).reshape(1, 128),
        "w_out": f(w_out).reshape(D, D),
        "fin_g": f(final_norm_g).reshape(1, D),
    }
    shared.update(_consts())
    xf = f(x)
    in_maps = []
    for c in range(NCORES):
        m = dict(shared)
        m["x"] = np.ascontiguousarray(xf[c])
        in_maps.append(m)
    nc = build(debug=_debug)
    res = run_bass_kernel_spmd(nc, in_maps, core_ids=list(range(NCORES)))
    outp = np.stack([np.asarray(res.results[c]["out"]) for c in range(NCORES)], axis=0).astype(np.float32)
    if _debug:
        return outp, res.results
    return outp
```

```python
import numpy as np
import ml_dtypes
import concourse.bass as bass
import concourse.mybir as mybir
from concourse.bass_utils import run_bass_kernel_spmd

F32 = mybir.dt.float32
BF16 = mybir.dt.bfloat16
ALU = mybir.AluOpType
AF = mybir.ActivationFunctionType
AX = mybir.AxisListType

S = 2048
D = 1024
C = 512
NT = S // 128
NCORES = 8
KW = 31
PADL = 16
UW = PADL + S + 16
EPS = 1e-6
LN_EPS = 1e-5
LAM_INIT = 0.2
VW = 130


class Eng:
    def __init__(self, nc, eng, name):
        self.nc, self.e, self.name = nc, eng, name
        self.sem = nc.alloc_semaphore("sem_" + name)
        self.n = 0
        self.seen = {}

    def wait(self, *toks):
        for t in toks:
            if t is None:
                continue
            if isinstance(t, list):
                self.wait(*t)
                continue
            src, c = t
            if self.seen.get(src, 0) >= c:
                continue
            self.e.wait_ge(src.sem, c)
            self.seen[src] = c

    def done(self, ins):
        self.n += 1
        ins.then_inc(self.sem, 1)
        return (self, self.n)


class DmaSem:
    def __init__(self, nc, name):
        self.sem = nc.alloc_semaphore("dsem_" + name)
        self.n = 0
        DmaSem.ALL.append(self)

    def done(self, ins):
        self.n += 16
        ins.then_inc(self.sem, 16)
        return (self, self.n)


def build(debug=False, stop_after=None):
    nc = bass.Bass("TRN2", target_bir_lowering=False)
    DmaSem.ALL = []

    def din(name, shape, dt=F32):
        return nc.dram_tensor(name, shape, dt, kind="ExternalInput").ap()

    x = din("x", [S, D])
    norm_g = din("norm_g", [8, 128])
    w_in = din("w_in", [D, 3584])
    conv_w = din("conv_w", [KW * 4, 128])
    conv_b = din("conv_b", [4, 128])
    ln_g = din("ln_g", [4, 128])
    ln_b = din("ln_b", [4, 128])
    w_pw = din("w_pw", [C, C])
    b_pw = din("b_pw", [4, 128])
    lq1 = din("lq1", [1, 64])
    lk1 = din("lk1", [1, 64])
    lq2 = din("lq2", [1, 64])
    lk2 = din("lk2", [1, 64])
    head_g = din("head_g", [1, 128])
    w_out = din("w_out", [D, D])
    fin_g = din("fin_g", [1, D])
    c_identb = din("c_identb", [128, 128], BF16)
    c_identf = din("c_identf", [128, 128])
    c_rot = din("c_rot", [128, 128], BF16)
    c_ones = din("c_ones", [128, 128], BF16)
    c_cos = din("c_cos", [128, S])
    c_sin = din("c_sin", [128, S])
    out = nc.dram_tensor("out", [S, D], F32, kind="ExternalOutput").ap()
    dbg = {}

    off = [16512]

    def sb(name, shape, dt, at=None):
        nbytes = int(np.prod(shape[1:])) * (2 if dt == BF16 else 4)
        if at is None:
            at = off[0]
            off[0] += (nbytes + 63) // 64 * 64
        return nc.alloc_sbuf_tensor_at(name, shape, dt, offset=at), at

    identb, _ = sb("identb", [128, 128], BF16)
    identf, _ = sb("identf", [128, 128], F32)
    rotm, _ = sb("rotm", [128, 128], BF16)
    onesm, _ = sb("onesm", [128, 128], BF16)
    rows_c, _ = sb("rows_c", [8, 128], F32)
    colv, _ = sb("colv", [128, 160], F32)
    lam4, _ = sb("lam4", [128, 4, 64], F32)
    small, _ = sb("small", [128, 256], F32)
    hgB, _ = sb("hgB", [128, 4, 128], F32)
    cT0, _ = sb("cT0", [128, S], F32)
    cwh, _ = sb("cwh", [128, 4, 32], F32)
    wpw, _ = sb("wpw", [128, 4, C], BF16)
    hT, hT_at = sb("hT", [128, 8, S], BF16)
    xs, xs_at = sb("xs", [128, 4, D], F32)
    uT, uT_at = sb("uT", [128, 4, UW], BF16)
    sga, _ = sb("sga", [128, 4, S], BF16)
    Gt, G_at = sb("Gt", [128, NT, 512], BF16)
    QT, QT_at = sb("QT", [128, 4, S], BF16)
    KT, KT_at = sb("KT", [128, 4, S], BF16)
    Vg, V_at = sb("Vg", [128, NT, 4, VW], BF16)
    wsl, W_at = sb("wsl", [128, 3, 8, 512], BF16)
    rt1, _ = sb("rt1", [128, 2, 512], F32)
    rt2, _ = sb("rt2", [128, 2, 512], F32)
    junk, T_at = sb("junk", [128, D], BF16)
    xnb, _ = sb("xnb", [128, 2, D], BF16)
    thb, _ = sb("thb", [128, 2, 512], F32)
    qbb, _ = sb("qbb", [128, 2, 512], BF16)
    assert off[0] <= 229376, off[0]
    yT = hT
    yb, _ = sb("yb", [128, NT, 512], BF16, at=hT_at)
    cosT, _ = sb("cosT", [128, S], F32, at=xs_at)
    sinT, _ = sb("sinT", [128, S], F32, at=xs_at + 8192)
    Eb, _ = sb("Eb", [128, 3, 2, 512], BF16, at=xs_at)
    at_t, _ = sb("at_t", [128, 2, 4, 128], F32, at=xs_at + 6144)
    at_o, _ = sb("at_o", [128, 2, 4, 128], F32, at=xs_at + 6144 + 4096)
    cTl = [cT0, sb("cT1", [128, S], F32, at=T_at)[0], sb("cT2", [128, S], F32, at=W_at)[0],
           sb("cT3", [128, S], F32, at=W_at + 8192)[0]]
    rows_a, _ = sb("rows_a", [128, 128], F32, at=QT_at)
    rows_b, _ = sb("rows_b", [32, 128], F32, at=QT_at + 512)
    hg1, _ = sb("hg1", [128, 128], F32, at=QT_at + 1024)
    fgB, _ = sb("fgB", [128, D], F32, at=uT_at)
    cb, _ = sb("cb", [128, 4, S], BF16, at=W_at + 16384)
    csql = [sb("csq%d" % i, [128, S], BF16, at=uT_at + i * UW * 2)[0] for i in range(4)]
    sT, _ = sb("sT", [128, 4, S], BF16, at=V_at)
    wout, _ = sb("wout", [128, 8, D], BF16, at=G_at)
    ln_r, _ = sb("ln_r", [128, 4, 512], F32, at=QT_at)
    ln_n, _ = sb("ln_n", [128, 4, 512], F32, at=QT_at + 8192)
    ln_m, _ = sb("ln_m", [128, 2, 512], F32, at=KT_at)
    ln_v, _ = sb("ln_v", [128, 2, 512], F32, at=KT_at + 4096)
    ln_x, _ = sb("ln_x", [128, 4, 512], F32, at=KT_at + 8192)
    ln_x2, _ = sb("ln_x2", [128, 4, 512], F32, at=uT_at + 4096)
    rbuf, _ = sb("rbuf", [128, 3, D], F32, at=W_at + 16384)
    junk5, _ = sb("junk5", [128, D], BF16, at=W_at + 28672)

    PA = nc.alloc_psum_tensor("PA", [128, 4, 512], F32)
    PB = nc.alloc_psum_tensor("PB", [128, 4, 512], F32)
    PBb = PB[:].bitcast(BF16)

    pe = Eng(nc, nc.tensor, "pe")
    act = Eng(nc, nc.scalar, "act")
    dve = Eng(nc, nc.vector, "dve")
    pool = Eng(nc, nc.gpsimd, "pool")
    sp = Eng(nc, nc.sync, "sp")
    ds_const = DmaSem(nc, "const")
    ds_x = [DmaSem(nc, "x%d" % i) for i in range(4)]
    ds_w = [DmaSem(nc, "w%d" % i) for i in range(3)]
    ds_cs = DmaSem(nc, "cs")
    ds_w2 = DmaSem(nc, "wx")
    ds_w3 = DmaSem(nc, "wy")
    ds_o = [DmaSem(nc, "out%d" % i) for i in range(3)]

    def finish():
        for e_ in (pe, act, dve, pool):
            if e_.n:
                nc.sync.wait_ge(e_.sem, e_.n)
        for d_ in DmaSem.ALL:
            if d_.n:
                nc.sync.wait_ge(d_.sem, d_.n)
        return nc

    SS1, RS1 = 0, 16
    LAMC = 40
    ZR = 48
    NZ = 64
    SSA = 72
    LNA = 80
    RSA = 88
    SS5, LN5, RS5 = 96, 112, 128

    ds_c0 = DmaSem(nc, "c0")
    for dst, src in [(identb[:], c_identb), (identf[:], c_identf), (rows_c[0:8, :], norm_g)]:
        tok_c0 = ds_c0.done(nc.scalar.dma_start(out=dst, in_=src))
    xv = x.rearrange("(t p) d -> t p d", p=128)
    x_tok = [None] * NT
    for t in range(4):
        x_tok[t] = ds_x[t].done(nc.sync.dma_start(out=xs[:, t, :], in_=xv[t]))
    tok_c = None
    for dst, src in [(rotm[:], c_rot), (onesm[:], c_ones),
                     (rows_a[0:124, :], conv_w), (rows_b[0:4, :], conv_b), (rows_b[4:8, :], ln_g),
                     (rows_b[8:12, :], ln_b), (rows_b[12:16, :], b_pw), (rows_b[16:17, :], head_g),
                     (lam4[:, 0, :], lq1.partition_broadcast(128)), (lam4[:, 1, :], lk1.partition_broadcast(128)),
                     (lam4[:, 2, :], lq2.partition_broadcast(128)), (lam4[:, 3, :], lk2.partition_broadcast(128)),
                     (hg1[:], head_g.partition_broadcast(128))]:
        tok_c = ds_const.done(nc.sync.dma_start(out=dst, in_=src))

    w_v = w_in.rearrange("(dt p) c -> p dt c", p=128)
    SLABS = [
        [(512, 256), (0, 256)],
        [(768, 256), (256, 256)],
        [(1024, 512)],
        [(1536, 512)],
        [(2048, 512)],
        [(2560, 512)],
        [(3072, 512)],
    ]
    slab_tok = [None] * 7

    def load_slab(si):
        slot = si % 3
        c0 = 0
        tk = None
        for (cs, wd) in SLABS[si]:
            tk = ds_w[slot].done(nc.gpsimd.dma_start(out=wsl[:, slot, :, c0:c0 + wd], in_=w_v[:, :, cs:cs + wd]))
            c0 += wd
        slab_tok[si] = tk

    load_slab(0)
    wpw_tok = ds_w2.done(nc.gpsimd.dma_start(out=wpw[:], in_=w_pw.rearrange("(ct p) e -> p ct e", p=128)))

    pool.done(nc.gpsimd.memset(Vg[:], 1.0))
    tok_vg1 = pool.done(nc.gpsimd.memset(uT[:, :, 0:PADL], 0.0))
    tok_upad = pool.done(nc.gpsimd.memset(uT[:, :, PADL + S:UW], 0.0))

    CB, LG, LB, BP, NG = 124, 128, 132, 136, 140
    pe.wait(tok_c0)
    t_pe = pe.done(nc.tensor.matmul(PA[:, 1, 0:8], lhsT=rows_c[0:8, :], rhs=identf[0:8, 0:8], start=True, stop=True))
    dve.wait(t_pe)
    tok_ng = dve.done(nc.vector.tensor_copy(out=colv[:, NG:NG + 8], in_=PA[:, 1, 0:8]))

    p1 = {"xnb_rd": [None, None], "tp_rd": [None, None], "last": None}

    def emit_p1(t):
        slot = t % 4
        b = t % 2
        act.wait(x_tok[t])
        t_ss = act.done(nc.scalar.activation(out=junk[:], in_=xs[:, slot, :], func=AF.Square,
                                             accum_out=small[:, SS1 + t:SS1 + t + 1]))
        act.wait(t_ss)
        t_ln = act.done(nc.scalar.activation(out=small[:, RS1 + t:RS1 + t + 1], in_=small[:, SS1 + t:SS1 + t + 1],
                                             func=AF.Ln, scale=1.0 / D, bias=EPS))
        act.wait(t_ln)
        t_rs = act.done(nc.scalar.activation(out=small[:, RS1 + t:RS1 + t + 1], in_=small[:, RS1 + t:RS1 + t + 1],
                                             func=AF.Exp, scale=-0.5))
        dve.wait(t_rs, x_tok[t], p1["xnb_rd"][b])
        t_xn = dve.done(nc.vector.tensor_scalar(out=xnb[:, b, :], in0=xs[:, slot, :],
                                                scalar1=small[:, RS1 + t:RS1 + t + 1], scalar2=None, op0=ALU.mult))
        if t + 4 < NT:
            sp.wait(t_xn, t_ss)
            x_tok[t + 4] = ds_x[slot].done(nc.sync.dma_start(out=xs[:, slot, :], in_=xv[t + 4]))
        pe.wait(t_xn, p1["tp_rd"][b])
        for dt in range(8):
            ins = nc.tensor.transpose(PBb[:, 2 + b, dt * 128:(dt + 1) * 128], xnb[:, b, dt * 128:(dt + 1) * 128], identb[:])
        t_tp = pe.done(ins)
        p1["xnb_rd"][b] = t_tp
        return t_tp

    def emit_p1_evac(t, t_tp):
        b = t % 2
        dve.wait(t_tp, tok_ng)
        t_ev = dve.done(nc.vector.tensor_tensor(
            out=hT[:, :, t * 128:(t + 1) * 128],
            in0=PBb[:, 2 + b, :].rearrange("p (dt i) -> p dt i", dt=8),
            in1=colv[:, NG:NG + 8].unsqueeze(2).to_broadcast([128, 8, 128]), op=ALU.mult))
        p1["tp_rd"][b] = t_ev
        p1["last"] = t_ev
        return t_ev

    hT_tok = [None] * NT
    tp_tok = [None] * NT
    for t in range(NT + 1):
        if t < NT:
            tp_tok[t] = emit_p1(t)
        if t >= 1:
            hT_tok[t - 1] = emit_p1_evac(t - 1, tp_tok[t - 1])
        if t == 9:
            pool.wait(x_tok[12])
            load_slab(1)
        if t == 13:
            pool.wait(x_tok[15])
            load_slab(2)
    sp.wait(p1["last"])
    ds_cs.done(nc.sync.dma_start(out=cosT[:], in_=c_cos))
    cs_tok = ds_cs.done(nc.sync.dma_start(out=sinT[:], in_=c_sin))

    pe.wait(tok_c)
    nc.tensor.matmul(PA[:, 0, 0:124], lhsT=rows_a[0:124, :], rhs=identf[0:124, 0:124], start=True, stop=True)
    t_pe = pe.done(nc.tensor.matmul(PA[:, 1, 0:17], lhsT=rows_b[0:17, :], rhs=identf[0:17, 0:17], start=True, stop=True))
    dve.wait(t_pe)
    nc.vector.tensor_copy(out=colv[:, 0:124], in_=PA[:, 0, 0:124])
    nc.vector.tensor_copy(out=colv[:, 124:140], in_=PA[:, 1, 0:16])
    tok_colv = dve.done(nc.vector.tensor_scalar(out=colv[:, 149:150], in0=PA[:, 1, 16:17], scalar1=1.0 - LAM_INIT,
                                                scalar2=None, op0=ALU.mult))
    dve.wait(tok_colv)
    tok_cwh = dve.done(nc.vector.tensor_scalar(
        out=cwh[:, :, 0:KW], in0=colv[:, 0:124].rearrange("p (j ct) -> p ct j", ct=4),
        scalar1=0.5, scalar2=None, op0=ALU.mult))
    dve.wait(tok_c)
    tok_hgB = dve.done(nc.vector.tensor_scalar(
        out=hgB[:], in0=hg1[:].unsqueeze(1).to_broadcast([128, 4, 128]),
        scalar1=1.0 - LAM_INIT, scalar2=None, op0=ALU.mult))
    nc.vector.tensor_tensor(out=lam4[:, 0, :], in0=lam4[:, 0, :], in1=lam4[:, 1, :], op=ALU.mult)
    t1 = dve.done(nc.vector.tensor_tensor(out=lam4[:, 2, :], in0=lam4[:, 2, :], in1=lam4[:, 3, :], op=ALU.mult))
    dve.wait(t1)
    t2 = dve.done(nc.vector.tensor_reduce(out=small[:, LAMC:LAMC + 2],
                                          in_=lam4[:].rearrange("p (a b) d -> p a b d", b=2)[:, :, 0, :],
                                          axis=AX.X, op=ALU.add))
    act.wait(t2)
    t3 = act.done(nc.scalar.activation(out=small[:, LAMC + 2:LAMC + 4], in_=small[:, LAMC:LAMC + 2], func=AF.Exp))
    dve.wait(t3)
    t4 = dve.done(nc.vector.tensor_tensor(out=small[:, LAMC + 4:LAMC + 5], in0=small[:, LAMC + 3:LAMC + 4],
                                          in1=small[:, LAMC + 2:LAMC + 3], op=ALU.subtract))
    dve.wait(t4)
    tok_lam = dve.done(nc.vector.tensor_scalar(out=small[:, LAMC + 5:LAMC + 6], in0=small[:, LAMC + 4:LAMC + 5],
                                               scalar1=-LAM_INIT, scalar2=None, op0=ALU.add))
    neglam = small[:, LAMC + 5:LAMC + 6]
    tok_pro_pe = t_pe
    if stop_after == "prologue":
        return finish()

    st = {"g": 0, "acc_rd": [tok_colv, tok_colv, None, None], "rot_rd": [None, None], "t_rd": [None, None], "qb_rd": [None, None],
          "th_rd": [None, None], "pending_rot": None, "nqk": 0, "nglu": 0, "nbg": 0}
    slab_done = [None] * 7
    tok_uT = [None] * 4
    tok_QK = {}
    tok_V = [None] * NT
    tok_G = [None] * NT
    tok_sga = None

    conv_prev = {}
    conv_gate = {"w": None}

    def conv_step(ct, j):
        src = uT[:, ct, PADL - 15 + j:PADL - 15 + j + S]
        if j == 0:
            gate = {0: [tok_pro_pe],
                    1: [p1["xnb_rd"][0], p1["xnb_rd"][1], p1["last"], st["th_rd"][0], st["th_rd"][1]],
                    2: conv_gate["w"], 3: conv_gate["w"]}[ct]
            dve.wait(tok_uT[ct], tok_upad, tok_cwh, tok_colv, gate)
            ins = nc.vector.tensor_scalar(out=cTl[ct][:], in0=src, scalar1=cwh[:, ct, 0:1],
                                          scalar2=colv[:, CB + ct:CB + ct + 1], op0=ALU.mult, op1=ALU.add)
        else:
            dve.wait(conv_prev[ct])
            ins = nc.vector.scalar_tensor_tensor(out=cTl[ct][:], in0=src, scalar=cwh[:, ct, j:j + 1],
                                                 in1=cTl[ct][:], op0=ALU.mult, op1=ALU.add)
        conv_prev[ct] = dve.done(ins)

    conv_q = {"early": [(0, j) for j in range(KW)] + [(1, j) for j in range(KW)],
              "late": [(2, j) for j in range(KW)] + [(3, j) for j in range(KW)]}
    conv_i = {"early": 0, "late": 0}
    tok_cb = [None] * 4
    tok_cbp = {}

    def emit_conv(which, n):
        for _ in range(n):
            i = conv_i[which]
            if i >= len(conv_q[which]):
                return
            conv_i[which] = i + 1
            ct, j = conv_q[which][i]
            conv_step(ct, j)
            if j == KW - 1:
                for c_ in range(4):
                    cb_queue.append((ct, c_))

    cb_queue = []

    def emit_cb(n):
        for _ in range(n):
            if not cb_queue:
                return
            ct, c_ = cb_queue.pop(0)
            sl = slice(c_ * 512, (c_ + 1) * 512)
            pool.wait(conv_prev[ct], tok_p2_pe_box[0])
            nc.gpsimd.tensor_copy(out=cb[:, ct, sl], in_=cTl[ct][:, sl])
            tok_cb[ct] = tok_cbp[(ct, c_)] = pool.done(nc.gpsimd.tensor_tensor(out=csql[ct][:, sl], in0=cTl[ct][:, sl],
                                                           in1=cTl[ct][:, sl], op=ALU.mult))

    tok_p2_pe_box = [None]

    def fm_group(slot, ct, tc):
        a = st["g"] % 4
        st["g"] += 1
        pe.wait(st["acc_rd"][a], [hT_tok[tc * 4 + i] for i in range(4)])
        for dt in range(8):
            ins = nc.tensor.matmul(PA[:, a, :], lhsT=wsl[:, slot, dt, ct * 128:(ct + 1) * 128],
                                   rhs=hT[:, dt, tc * 512:(tc + 1) * 512], start=(dt == 0), stop=(dt == 7))
        return a, pe.done(ins)

    def tm_group(slot, tt):
        a = st["g"] % 4
        st["g"] += 1
        pe.wait(st["acc_rd"][a], hT_tok[tt])
        for dt in range(8):
            ins = nc.tensor.matmul(PA[:, a, :], lhsT=hT[:, dt, tt * 128:(tt + 1) * 128],
                                   rhs=wsl[:, slot, dt, :], start=(dt == 0), stop=(dt == 7))
        return a, pe.done(ins)

    def flush_rot():
        pr = st["pending_rot"]
        if pr is None:
            return
        st["pending_rot"] = None
        (dst, ct, tc, b, t_qb, t_t1) = pr
        r = b
        pe.wait(t_qb, st["rot_rd"][r])
        t_rot = pe.done(nc.tensor.matmul(PB[:, r, :], lhsT=rotm[:], rhs=qbb[:, b, :], start=True, stop=True))
        st["qb_rd"][b] = t_rot
        dve.wait(t_rot)
        t_t2 = dve.done(nc.vector.tensor_tensor(out=rt2[:, b, :], in0=PB[:, r, :],
                                                in1=sinT[:, tc * 512:(tc + 1) * 512], op=ALU.mult))
        st["rot_rd"][r] = t_t2
        pool.wait(t_t2, t_t1, tok_hgB, tok_pro_pe)
        if dst is QT:
            o_ap = QT[:, ct, :].rearrange("p (i n) -> p n i", n=NT)[:, 4 * tc:4 * tc + 4, :]
            i0 = rt1[:, b, :].rearrange("p (n i) -> p n i", n=4)
            i1 = rt2[:, b, :].rearrange("p (n i) -> p n i", n=4)
        else:
            o_ap = dst[:, ct, tc * 512:(tc + 1) * 512]
            i0, i1 = rt1[:, b, :], rt2[:, b, :]
        t_add = pool.done(nc.gpsimd.tensor_tensor(out=o_ap, in0=i0, in1=i1, op=ALU.add))
        st["t_rd"][b] = t_add
        tok_QK[(id(dst), ct, tc)] = t_add

    EARLY_TAPS = {1: 12, 2: 11, 3: 2, 4: 5, 5: 11, 6: 11}
    for si in range(7):
        slot = si % 3
        pe.wait(slab_tok[si])
        last_pe = None
        ngrp = [0]

        def after_group():
            ngrp[0] += 1
            n_t = EARLY_TAPS.get(si, 0)
            if n_t and (ngrp[0] * n_t) // 16 > ((ngrp[0] - 1) * n_t) // 16:
                emit_conv("early", 1)

        if si < 2:
            for tc in range(4):
                for cl in range(2):
                    ct = si * 2 + cl
                    a, t_mm = fm_group(slot, cl, tc)
                    b = st["nglu"] % 2
                    st["nglu"] += 1
                    act.wait(t_mm, st["th_rd"][b])
                    t_th = act.done(nc.scalar.activation(out=thb[:, b, :], in_=PA[:, a, :], func=AF.Tanh, scale=0.5))
                    st["acc_rd"][a] = t_th
                    a2, t_mm2 = fm_group(slot, 2 + cl, tc)
                    dve.wait(t_mm2, t_th)
                    t_u = dve.done(nc.vector.scalar_tensor_tensor(
                        out=uT[:, ct, PADL + tc * 512:PADL + (tc + 1) * 512], in0=thb[:, b, :], scalar=1.0,
                        in1=PA[:, a2, :], op0=ALU.add, op1=ALU.mult))
                    st["acc_rd"][a2] = t_u
                    st["th_rd"][b] = t_u
                    tok_uT[ct] = t_u
                    last_pe = t_mm2
                if si == 1:
                    for _ in range(4):
                        after_group()
        elif si == 2:
            for tc in range(4):
                for ct in range(4):
                    a, t_mm = fm_group(slot, ct, tc)
                    act.wait(t_mm)
                    t_s = act.done(nc.scalar.activation(out=sga[:, ct, tc * 512:(tc + 1) * 512], in_=PA[:, a, :],
                                                        func=AF.Silu))
                    st["acc_rd"][a] = t_s
                    tok_sga = t_s
                    last_pe = t_mm
                    after_group()
        elif si in (3, 4):
            dst = QT if si == 3 else KT
            for tc in range(4):
                for ct in range(4):
                    a, t_mm = fm_group(slot, ct, tc)
                    flush_rot()
                    b = st["nqk"] % 2
                    st["nqk"] += 1
                    act.wait(t_mm, st["qb_rd"][b])
                    t_qb = act.done(nc.scalar.activation(out=qbb[:, b, :], in_=PA[:, a, :], func=AF.Copy))
                    dve.wait(t_mm, t_qb, cs_tok, st["t_rd"][b])
                    t_t1 = dve.done(nc.vector.tensor_tensor(out=rt1[:, b, :], in0=PA[:, a, :],
                                                            in1=cosT[:, tc * 512:(tc + 1) * 512], op=ALU.mult))
                    st["acc_rd"][a] = [t_qb, t_t1]
                    st["pending_rot"] = (dst, ct, tc, b, t_qb, t_t1)
                    last_pe = t_mm
                    after_group()
        elif si == 5:
            flush_rot()
            for tt in range(NT):
                a, t_mm = tm_group(slot, tt)
                act.wait(t_mm, tok_vg1)
                t_v = act.done(nc.scalar.activation(out=Vg[:, tt, :, 0:128],
                                                    in_=PA[:, a, :].rearrange("p (h e) -> p h e", h=4), func=AF.Copy))
                st["acc_rd"][a] = t_v
                tok_V[tt] = t_v
                last_pe = t_mm
                after_group()
        else:
            for tt in range(NT):
                a, t_mm = tm_group(slot, tt)
                b = st["nbg"] % 4
                st["nbg"] += 1
                if "g_rd" not in st:
                    st["g_rd"] = [st["t_rd"][0], st["t_rd"][1], st["t_rd"][0], st["t_rd"][1]]
                gbuf = (rt1 if b < 2 else rt2)[:, b % 2, :]
                act.wait(t_mm, st["g_rd"][b])
                t_s = act.done(nc.scalar.activation(out=gbuf, in_=PA[:, a, :], func=AF.Silu))
                st["acc_rd"][a] = t_s
                dve.wait(t_s, tok_hgB)
                t_g = dve.done(nc.vector.tensor_tensor(out=Gt[:, tt, :], in0=gbuf,
                                                       in1=hgB[:].rearrange("p h e -> p (h e)"), op=ALU.mult))
                st["g_rd"][b] = t_g
                tok_G[tt] = t_g
                last_pe = t_mm
                after_group()
        slab_done[si] = last_pe
        if si + 3 < 7:
            pool.wait(last_pe)
            load_slab(si + 3)
        if stop_after == "p2s%d" % si:
            return finish()
    tok_p2_pe = slab_done[6]
    tok_p2_pe_box[0] = tok_p2_pe
    conv_gate["w"] = [tok_p2_pe]
    if stop_after == "p2":
        return finish()

    s_rd = [None, None]
    e_rd = [None, None, None]
    o_rd = None
    tmp_rd = [None, None]
    pend_pv = None
    rounds = [(h, qc) for h in range(4) for qc in range(4)]

    def emit_pv(h, r, p, eb, t_e):
        st_ = r % 2
        fw = [tok_V[2 * p], tok_V[2 * p + 1]]
        if p == 0:
            fw.append(o_rd[st_])
        pe.wait(t_e, fw)
        ins = None
        for kp in range(2):
            kt = 2 * p + kp
            for j in range(2):
                for q2 in range(2):
                    ins = nc.tensor.matmul(PB[:, 2 * st_ + q2, j * 256:j * 256 + 129],
                                           lhsT=Eb[:, eb, j, kp * 256 + q2 * 128:kp * 256 + (q2 + 1) * 128],
                                           rhs=Vg[:, kt, h, 0:129], start=(kt == 0 and j == 0), stop=(kt == NT - 1),
                                           skip_group_check=True)
        return pe.done(ins)

    def act_epilogue(rb, t_ss):
        act.wait(t_ss)
        t_l = act.done(nc.scalar.activation(out=small[:, LNA + 4 * rb:LNA + 4 * rb + 4],
                                            in_=small[:, SSA + 4 * rb:SSA + 4 * rb + 4], func=AF.Ln,
                                            scale=1.0 / 128, bias=EPS))
        act.wait(t_l)
        return act.done(nc.scalar.activation(out=small[:, RSA + 4 * rb:RSA + 4 * rb + 4],
                                             in_=small[:, LNA + 4 * rb:LNA + 4 * rb + 4], func=AF.Exp, scale=-0.5))

    def pool_tail(prb, ph, pqc, t_r):
        pool.wait(t_r)
        t_y = pool.done(nc.gpsimd.tensor_tensor(
            out=at_o[:, prb, :, :], in0=at_o[:, prb, :, :],
            in1=small[:, RSA + 4 * prb:RSA + 4 * prb + 4].unsqueeze(2).to_broadcast([128, 4, 128]), op=ALU.mult))
        pool.wait(t_y, [tok_G[pqc * 4 + i] for i in range(4)])
        return pool.done(nc.gpsimd.tensor_tensor(
            out=yb[:, pqc * 4:(pqc + 1) * 4, ph * 128:(ph + 1) * 128], in0=at_o[:, prb, :, :],
            in1=Gt[:, pqc * 4:(pqc + 1) * 4, ph * 128:(ph + 1) * 128], op=ALU.mult))

    ds_tr = DmaSem(nc, "tr")

    def ybT_dma(ph, pqc, t_yb):
        sp.wait(t_yb, tok_p2_pe)
        for i in range(4):
            tt = pqc * 4 + i
            ds_tr.done(nc.sync.dma_start_transpose(out=yT[:, 4 + ph, tt * 128:(tt + 1) * 128],
                                                   in_=yb[:, tt, ph * 128:(ph + 1) * 128]))

    pend_epi = None
    pe.wait(tok_p2_pe, st["acc_rd"], st["rot_rd"], p1["tp_rd"])
    act.wait(st["t_rd"][0], st["t_rd"][1])
    dve.wait(st["t_rd"][0], st["t_rd"][1])
    o_rd = [None, None]
    NP = NT // 2
    iters = [(r, r // 8, r % 8, p) for r in range(32) for p in range(NP)]
    s_tok = {}

    def emit_scores(g):
        (r_, h_, q8_, p_) = iters[g]
        sb_i = g % 2
        pe.wait(s_rd[sb_i], [tok_QK[(id(QT), h_, tcq)] for tcq in range(4)], tok_QK[(id(KT), h_, p_ // 2)])
        ins = None
        for kp in range(2):
            kt_ = 2 * p_ + kp
            for j in range(2):
                ins = nc.tensor.matmul(PA[:, 2 * sb_i + j, kp * 256:(kp + 1) * 256],
                                       lhsT=KT[64 * j:64 * j + 64, h_, kt_ * 128:(kt_ + 1) * 128],
                                       rhs=QT[64 * j:64 * j + 64, h_, q8_ * 256:(q8_ + 1) * 256], start=True, stop=True,
                                       skip_group_check=True)
        s_tok[g] = pe.done(ins)

    def round_drain(r_, t_pv_last_):
        nonlocal pend_epi
        st_ = r_ % 2
        sr = r_ // 2
        rb = sr % 2
        hh = r_ % 2
        Ov = PB[:, 2 * st_:2 * st_ + 2, :].rearrange("p b (j c) -> p b j c", j=2)
        dve.wait(t_pv_last_, tmp_rd[rb] if hh == 0 else None, tok_lam)
        zr = small[:, ZR + 8 * rb + 4 * hh:ZR + 8 * rb + 4 * hh + 4].rearrange("p (b j) -> p b j", j=2)
        t_z = dve.done(nc.vector.reciprocal(out=zr, in_=Ov[:, :, :, 128]))
        dve.wait(t_z)
        nzc = small[:, NZ + 4 * rb + 2 * hh:NZ + 4 * rb + 2 * hh + 2]
        t_nz = dve.done(nc.vector.tensor_tensor(out=nzc, in0=zr[:, :, 1], in1=neglam.to_broadcast([128, 2]),
                                                op=ALU.mult))
        t_o1 = dve.done(nc.vector.tensor_tensor(
            out=at_o[:, rb, 2 * hh:2 * hh + 2, :], in0=Ov[:, :, 0, 0:128],
            in1=zr[:, :, 0].unsqueeze(2).to_broadcast([128, 2, 128]), op=ALU.mult))
        dve.wait(t_nz, t_o1)
        for q2 in range(2):
            t_o = dve.done(nc.vector.scalar_tensor_tensor(
                out=at_o[:, rb, 2 * hh + q2, :], in0=Ov[:, q2, 1, 0:128], scalar=nzc[:, q2:q2 + 1],
                in1=at_o[:, rb, 2 * hh + q2, :], op0=ALU.mult, op1=ALU.add))
        o_rd[st_] = t_o
        if hh == 1:
            pool.wait(t_o, tmp_rd[rb])
            t_sq = pool.done(nc.gpsimd.tensor_tensor(out=at_t[:, rb, :, :], in0=at_o[:, rb, :, :],
                                                     in1=at_o[:, rb, :, :], op=ALU.mult))
            pend_epi = (rb, r_ // 8, (r_ % 8) // 2, t_sq)

    t_pv_last = None
    emit_scores(0)
    for g, (r, h, q8, p) in enumerate(iters):
        eb = g % 3
        if g + 1 < len(iters):
            emit_scores(g + 1)
        if pend_pv is not None:
            (pr_, ph_, pp_, peb, pt_e) = pend_pv
            e_rd[peb] = emit_pv(ph_, pr_, pp_, peb, pt_e)
            if pp_ == NP - 1:
                round_drain(pr_, e_rd[peb])
        act.wait(s_tok[g], e_rd[eb])
        t_e = act.done(nc.scalar.activation(out=Eb[:, eb, :, :], in_=PA[:, 2 * (g % 2):2 * (g % 2) + 2, :],
                                            func=AF.Exp, scale=0.125))
        s_rd[g % 2] = t_e
        pend_pv = (r, h, p, eb, t_e)
        kk = (r % 2) * NP + p
        if kk == 7 and pend_epi is not None and len(pend_epi) == 4:
            (prb, ph, pqc, t_sq) = pend_epi
            dve.wait(t_sq)
            t_ss = dve.done(nc.vector.tensor_reduce(out=small[:, SSA + 4 * prb:SSA + 4 * prb + 4],
                                                    in_=at_t[:, prb, :, :], axis=AX.X, op=ALU.add))
            pend_epi = (prb, ph, pqc, t_sq, t_ss)
        if kk == 11 and pend_epi is not None and len(pend_epi) == 5:
            (prb, ph, pqc, t_sq, t_ss) = pend_epi
            pend_epi = None
            t_r = act_epilogue(prb, t_ss)
            tmp_rd[prb] = pool_tail(prb, ph, pqc, t_r)
            ybT_dma(ph, pqc, tmp_rd[prb])
        if kk in ((1, 3, 5, 7, 10) if (r // 2) % 2 == 0 else (2, 5, 9, 12)):
            if conv_i["early"] < len(conv_q["early"]):
                emit_conv("early", 1)
            else:
                emit_conv("late", 1)
        if kk == 14:
            emit_cb(2 if r >= 24 else 1)
    (pr_, ph_, pp_, peb, pt_e) = pend_pv
    e_rd[peb] = emit_pv(ph_, pr_, pp_, peb, pt_e)
    t_pv_last = e_rd[peb]
    round_drain(pr_, t_pv_last)
    tok_att_pe = t_pv_last
    emit_conv("early", 2 * KW)
    emit_conv("late", 2 * KW)
    emit_cb(16)
    tail_box = {}
    xr_tok = [None] * NT

    def attention_tail():
        (prb, ph, pqc, t_sq) = pend_epi
        dve.wait(t_sq)
        t_ss = dve.done(nc.vector.tensor_reduce(out=small[:, SSA + 4 * prb:SSA + 4 * prb + 4],
                                                in_=at_t[:, prb, :, :], axis=AX.X, op=ALU.add))
        t_r = act_epilogue(prb, t_ss)
        t_yb = pool_tail(prb, ph, pqc, t_r)
        tail_box["yb"] = t_yb
        ybT_dma(ph, pqc, t_yb)
        sp.wait(t_yb, tok_att_pe, t_ss)
        for t in range(4):
            xr_tok[t] = ds_x[t].done(nc.sync.dma_start(out=xs[:, t, :], in_=xv[t]))
        pool.wait(t_yb)
        tail_box["wout"] = ds_w3.done(nc.gpsimd.dma_start(out=wout[:], in_=w_out.rearrange("(ft p) d -> p ft d", p=128)))

    m_free = [None, None]
    x_rd = [None] * 4
    pw_rd = [None, None]
    stats_rd = [None, None]
    t_stats = {}
    tok_ln = {}
    t_sil_all = {}
    tok_ya = {}
    nx = [0]

    def stats_pe(tc):
        pr = tc % 2
        sl = slice(tc * 512, (tc + 1) * 512)
        pe.wait([tok_cbp[(c_, tc)] for c_ in range(4)], stats_rd[pr], tok_att_pe, s_rd)
        for ct in range(4):
            nc.tensor.matmul(PA[:, 2 * pr, :], lhsT=onesm[:], rhs=cb[:, ct, sl], start=(ct == 0), stop=(ct == 3))
        for ct in range(4):
            ins = nc.tensor.matmul(PA[:, 2 * pr + 1, :], lhsT=onesm[:], rhs=csql[ct][:, sl], start=(ct == 0),
                                   stop=(ct == 3))
        t_stats[tc] = pe.done(ins)

    def stats_chain(tc):
        pr = tc % 2
        s_ = tc % 2
        t_st = t_stats[tc]
        act.wait(t_st, m_free[s_], tok_att_pe)
        t_m2 = act.done(nc.scalar.activation(out=ln_v[:, s_, :], in_=PA[:, 2 * pr, :], func=AF.Square))
        dve.wait(t_m2, t_st)
        t_var = dve.done(nc.vector.tensor_tensor(out=ln_v[:, s_, :], in0=PA[:, 2 * pr + 1, :], in1=ln_v[:, s_, :],
                                                 op=ALU.subtract))
        act.wait(t_var)
        t_l = act.done(nc.scalar.activation(out=ln_r[:, tc, :], in_=ln_v[:, s_, :], func=AF.Ln, bias=LN_EPS))
        act.wait(t_l)
        t_r = act.done(nc.scalar.activation(out=ln_r[:, tc, :], in_=ln_r[:, tc, :], func=AF.Exp, scale=-0.5))
        dve.wait(t_r)
        t_n = dve.done(nc.vector.scalar_tensor_tensor(out=ln_n[:, tc, :], in0=PA[:, 2 * pr, :], scalar=-1.0,
                                                      in1=ln_r[:, tc, :], op0=ALU.mult, op1=ALU.mult))
        stats_rd[pr] = [t_m2, t_var, t_n]
        m_free[s_] = [t_l]
        tok_ln[tc] = [t_r, t_n]

    norm_b = {}

    x_rd2 = [None] * 8

    def lnx(tc, ct):
        return (ln_x if tc % 2 == 0 else ln_x2)[:, ct, :]

    def norm_pre(tc):
        sl = slice(tc * 512, (tc + 1) * 512)
        toks = []
        for ct in range(4):
            b = (tc % 2) * 4 + ct
            eng, E = (nc.vector, dve) if ct in (0, 2) else (nc.gpsimd, pool)
            E.wait(tok_ln[tc], x_rd2[b], tok_att_pe, t_stats[3], [conv_prev[c_] for c_ in range(4)])
            t_a = E.done(eng.tensor_tensor(out=lnx(tc, ct), in0=cTl[ct][:, sl], in1=ln_r[:, tc, :], op=ALU.mult))
            E.wait(t_a)
            toks.append(E.done(eng.tensor_tensor(out=lnx(tc, ct), in0=lnx(tc, ct), in1=ln_n[:, tc, :], op=ALU.add)))
        norm_b[tc] = toks

    def norm_act(tc):
        sl = slice(tc * 512, (tc + 1) * 512)
        t_sil = [None] * 4
        for ct in range(4):
            b = (tc % 2) * 4 + ct
            act.wait(norm_b[tc][ct], tok_att_pe)
            t_sil[ct] = act.done(nc.scalar.activation(out=sT[:, ct, sl], in_=lnx(tc, ct), func=AF.Silu,
                                                      scale=colv[:, LG + ct:LG + ct + 1],
                                                      bias=colv[:, LB + ct:LB + ct + 1]))
            x_rd2[b] = t_sil[ct]
        t_sil_all[tc] = t_sil

    def pw_chunk(tc):
        sl = slice(tc * 512, (tc + 1) * 512)
        toks = []
        for et in range(4):
            a = et % 2
            pe.wait(t_sil_all[tc], wpw_tok, pw_rd[a])
            for ct in range(4):
                ins = nc.tensor.matmul(PB[:, 2 + a, :], lhsT=wpw[:, ct, et * 128:(et + 1) * 128], rhs=sT[:, ct, sl],
                                       start=(ct == 0), stop=(ct == 3))
            t_pw = pe.done(ins)
            dve.wait(t_pw, tok_sga, (ds_tr, 64 * 16))
            t_ya = dve.done(nc.vector.scalar_tensor_tensor(
                out=yT[:, et, sl], in0=PB[:, 2 + a, :], scalar=colv[:, BP + et:BP + et + 1], in1=sga[:, et, sl],
                op0=ALU.add, op1=ALU.mult))
            pw_rd[a] = t_ya
            toks.append(t_ya)
        tok_ya[tc] = toks

    tp_rd = [None, None]
    tp_done = {"pe": None, "ev": None}

    def yb_transposes():
        for pi in range(NT // 2):
            b = pi % 2
            pe.wait(tail_box["yb"], tp_rd[b], o_rd)
            ins = None
            for tl in range(2):
                for hh in range(4):
                    tt = pi * 2 + tl
                    ins = nc.tensor.transpose(PBb[:, b, (tl * 4 + hh) * 128:(tl * 4 + hh + 1) * 128],
                                              yb[:, tt, hh * 128:(hh + 1) * 128], identb[:])
            t_tp = pe.done(ins)
            act.wait(t_tp, t_stats[3])
            t_ev = act.done(nc.scalar.activation(
                out=yT[:, 4:8, pi * 256:(pi + 1) * 256].rearrange("p h (t i) -> p t h i", t=2),
                in_=PBb[:, b, :].rearrange("p (t h i) -> p t h i", t=2, h=4), func=AF.Copy,
                scale=colv[:, 149:150]))
            tp_rd[b] = t_ev
            tp_done["pe"] = t_tp
            tp_done["ev"] = t_ev

    ov = out.rearrange("(t p) d -> t p d", p=128)
    acc_rd = [None, None]
    r_rd = [None, None, None]
    ds_fg = DmaSem(nc, "fg")
    fg_box = [None]

    def p5_tile(t):
        slot = t % 4
        a = t % 2
        rb3 = t % 3
        pe.wait(tok_ya[t // 4], (ds_tr, 64 * 16), tail_box["wout"], acc_rd[a], stats_rd)
        for half in range(2):
            for ft in range(8):
                ins = nc.tensor.matmul(PA[:, 2 * a + half, :], lhsT=yT[:, ft, t * 128:(t + 1) * 128],
                                       rhs=wout[:, ft, half * 512:(half + 1) * 512], start=(ft == 0), stop=(ft == 7))
        t_mm = pe.done(ins)
        dve.wait(t_mm, xr_tok[t], r_rd[rb3], tok_ya[3] if 3 in tok_ya else None)
        t_res = dve.done(nc.vector.tensor_tensor(out=rbuf[:, rb3, :].rearrange("p (a c) -> p a c", a=2),
                                                 in0=PA[:, 2 * a:2 * a + 2, :],
                                                 in1=xs[:, slot, :].rearrange("p (a c) -> p a c", a=2), op=ALU.add))
        acc_rd[a] = t_res
        if t + 4 < NT:
            sp.wait(t_res)
            xr_tok[t + 4] = ds_x[slot].done(nc.sync.dma_start(out=xs[:, slot, :], in_=xv[t + 4]))
        act.wait(t_res)
        t_ss = act.done(nc.scalar.activation(out=junk5[:], in_=rbuf[:, rb3, :], func=AF.Square,
                                             accum_out=small[:, SS5 + t:SS5 + t + 1]))
        act.wait(t_ss)
        t_l = act.done(nc.scalar.activation(out=small[:, LN5 + t:LN5 + t + 1], in_=small[:, SS5 + t:SS5 + t + 1],
                                            func=AF.Ln, scale=1.0 / D, bias=EPS))
        act.wait(t_l)
        t_r = act.done(nc.scalar.activation(out=small[:, RS5 + t:RS5 + t + 1], in_=small[:, LN5 + t:LN5 + t + 1],
                                            func=AF.Exp, scale=-0.5))
        p5_pending.append((t, rb3, t_r, t_res))

    p5_pending = []

    def p5_finish():
        (t, rb3, t_r, t_res) = p5_pending.pop(0)
        dve.wait(t_r, t_res, fg_box[0])
        t_o = dve.done(nc.vector.scalar_tensor_tensor(out=rbuf[:, rb3, :], in0=rbuf[:, rb3, :],
                                                      scalar=small[:, RS5 + t:RS5 + t + 1], in1=fgB[:],
                                                      op0=ALU.mult, op1=ALU.mult))
        sp.wait(t_o)
        r_rd[rb3] = ds_o[rb3].done(nc.sync.dma_start(out=ov[t], in_=rbuf[:, rb3, :]))

    stats_pe(0)
    stats_pe(1)
    stats_chain(0)
    stats_pe(2)
    stats_chain(1)
    stats_pe(3)
    stats_chain(2)
    stats_chain(3)
    attention_tail()
    norm_pre(0)
    norm_pre(1)
    sp.wait(t_stats[3], [conv_prev[c_] for c_ in range(4)])
    fg_box[0] = ds_fg.done(nc.sync.dma_start(out=fgB[:], in_=fin_g.partition_broadcast(128)))
    norm_act(0)
    norm_pre(2)
    pw_chunk(0)
    norm_act(1)
    norm_pre(3)
    pw_chunk(1)
    norm_act(2)
    pw_chunk(2)
    norm_act(3)
    pw_chunk(3)
    p5_tile(0)
    for t in range(1, NT):
        p5_tile(t)
        p5_finish()
    p5_finish()
    return finish()


def _consts():
    idb = np.eye(128, dtype=np.float32).astype(ml_dtypes.bfloat16)
    idf = np.eye(128, dtype=np.float32)
    rot = np.zeros((128, 128), dtype=np.float32)
    for p2 in range(128):
        d = p2 % 64
        if d < 32:
            rot[p2 + 32, p2] = -1.0
        else:
            rot[p2 - 32, p2] = 1.0
    ones = np.full((128, 128), 1.0 / 512, dtype=np.float32).astype(ml_dtypes.bfloat16)
    half = 32
    try:
        import jax
        import jax.numpy as jnp
        with jax.default_device(jax.devices("cpu")[0]):
            inv_j = 1.0 / (10000.0 ** (jnp.arange(half, dtype=jnp.float32) * 2.0 / 64))
            pos_j = jnp.arange(S, dtype=jnp.float32)
            ang_j = pos_j[:, None] * inv_j[None, :]
            cos32 = np.asarray(jnp.cos(ang_j), dtype=np.float32).T
            sin32 = np.asarray(jnp.sin(ang_j), dtype=np.float32).T
    except Exception:
        inv_freq = (1.0 / (10000.0 ** (np.arange(half, dtype=np.float64) * 2.0 / 64.0))).astype(np.float32)
        pos = np.arange(S, dtype=np.float32)
        ang = (pos[None, :] * inv_freq[:, None]).astype(np.float32)
        cos32 = np.cos(ang.astype(np.float64)).astype(np.float32)
        sin32 = np.sin(ang.astype(np.float64)).astype(np.float32)
    idx = (np.arange(128) % 64) % 32
    cosT = cos32[idx].astype(np.float32)
    sinT = sin32[idx].astype(np.float32)
    return {"c_identb": idb, "c_identf": idf, "c_rot": rot.astype(ml_dtypes.bfloat16), "c_ones": ones,
            "c_cos": np.ascontiguousarray(cosT), "c_sin": np.ascontiguousarray(sinT)}


_NC_CACHE = {}


def kernel(x, norm_g, w_in, conv_w, conv_b, conv_ln_g, conv_ln_b, w_pw, b_pw, lambda_q1, lambda_k1,
           lambda_q2, lambda_k2, head_norm_g, w_out, final_norm_g, _debug=False):
    f = lambda a: np.ascontiguousarray(np.asarray(a, dtype=np.float32))
    shared = {
        "norm_g": f(norm_g).reshape(8, 128),
        "w_in": f(w_in).reshape(D, 3584),
        "conv_w": f(conv_w).reshape(KW * 4, 128),
        "conv_b": f(conv_b).reshape(4, 128),
        "ln_g": f(conv_ln_g).reshape(4, 128),
        "ln_b": f(conv_ln_b).reshape(4, 128),
        "w_pw": f(w_pw).reshape(C, C),
        "b_pw": f(b_pw).reshape(4, 128),
        "lq1": f(lambda_q1).reshape(1, 64),
        "lk1": f(lambda_k1).reshape(1, 64),
        "lq2": f(lambda_q2).reshape(1, 64),
        "lk2": f(lambda_k2).reshape(1, 64),
        "head_g": f(head_norm_g).reshape(1, 128),
        "w_out": f(w_out).reshape(D, D),
        "fin_g": f(final_norm_g).reshape(1, D),
    }
    shared.update(_consts())
    xf = f(x)
    in_maps = []
    for c in range(NCORES):
        m = dict(shared)
        m["x"] = np.ascontiguousarray(xf[c])
        in_maps.append(m)
    nc = build(debug=_debug)
    res = run_bass_kernel_spmd(nc, in_maps, core_ids=list(range(NCORES)))
    outp = np.stack([np.asarray(res.results[c]["out"]) for c in range(NCORES)], axis=0).astype(np.float32)
    if _debug:
        return outp, res.results
    return outp
```
